# Optimizing a Trainium2 kernel written in Bass

```python
import math
import jax, jax.numpy as jnp
from jax import lax
import numpy as np

D_MODEL = 2048
BATCH = 1
SEQ = 16384
DEPTH = 2

D_FF = 5632
NORM_EPS = 1e-6
NEG = -1e30

A_HEADS = 4
A_DQK = 128
A_DV = 256
A_WIDTH = A_HEADS * A_DV
A_CHUNK = 64

B_HEADS = 8
B_Q_LORA = 384
B_KV_LORA = 256
B_NOPE = 128
B_ROPE = 64
B_DV = 128
B_WIDTH = B_HEADS * B_DV
B_QBLOCK = 128
ROPE_BASE = 10000.0

C_WIDTH = 1024
C_BLOCKS = 8
C_BLOCK_DIM = C_WIDTH // C_BLOCKS
C_CONV = 4
C_POW = 8.0

N_BRANCH = 3
BR_WIDTH = 1024

IN_SPLITS = (A_HEADS * A_DQK, A_HEADS * A_DQK, A_WIDTH, A_HEADS, A_HEADS, A_WIDTH,
             B_Q_LORA, B_KV_LORA, B_ROPE,
             C_WIDTH,
             N_BRANCH * D_MODEL)
N_IN = sum(IN_SPLITS)

kernel_name = "hybrid_mlstm_mla_rglru_gated_macaron"


def rmsnorm(x, g):
    xf = x.astype(jnp.float32)
    y = xf * lax.rsqrt(jnp.mean(xf * xf, axis=-1, keepdims=True) + NORM_EPS)
    return (y * g.astype(jnp.float32)).astype(x.dtype)


def swiglu(x, w_gate, w_up, w_down):
    return (jax.nn.silu(x @ w_gate) * (x @ w_up)) @ w_down


def apply_rope(x, cos, sin):
    xf = x.astype(jnp.float32)
    x1, x2 = jnp.split(xf, 2, axis=-1)
    return jnp.concatenate([x1 * cos - x2 * sin, x1 * sin + x2 * cos], axis=-1).astype(x.dtype)


def _to_chunks(t):
    b, s, h = t.shape[:3]
    t = t.reshape((b, s // A_CHUNK, A_CHUNK, h) + t.shape[3:])
    perm = (1, 0, 3, 2) + tuple(range(4, t.ndim))
    return t.transpose(perm)


def mlstm_chunkwise(q, k, v, i_pre, f_pre):
    b, s, h, _ = q.shape
    qf = q.astype(jnp.float32) * (A_DQK ** -0.5)
    kf = k.astype(jnp.float32)
    vf = v.astype(jnp.float32)
    log_i = i_pre.astype(jnp.float32)
    log_f = jax.nn.log_sigmoid(f_pre.astype(jnp.float32))
    xs = (_to_chunks(qf), _to_chunks(kf), _to_chunks(vf), _to_chunks(log_i), _to_chunks(log_f))
    causal = jnp.tril(jnp.ones((A_CHUNK, A_CHUNK), dtype=bool))

    def step(carry, inp):
        c_prev, n_prev, m_prev = carry
        qc, kc, vc, ic, lfc = inp
        bcum = jnp.cumsum(lfc, axis=-1)
        b_tot = bcum[..., -1]
        d_intra = bcum[..., :, None] - bcum[..., None, :] + ic[..., None, :]
        d_intra = jnp.where(causal, d_intra, NEG)
        d_inter = bcum + m_prev[..., None]
        m_t = jnp.maximum(jnp.max(d_intra, axis=-1), d_inter)
        w_intra = jnp.exp(d_intra - m_t[..., None])
        w_inter = jnp.exp(d_inter - m_t)
        sc = jnp.einsum('bhtd,bhsd->bhts', qc, kc) * w_intra
        num = (jnp.einsum('bhts,bhsv->bhtv', sc, vc)
               + w_inter[..., None] * jnp.einsum('bhtd,bhdv->bhtv', qc, c_prev))
        den = jnp.sum(sc, axis=-1) + w_inter * jnp.einsum('bhtd,bhd->bht', qc, n_prev)
        h_out = num / jnp.maximum(jnp.abs(den), jnp.exp(-m_t))[..., None]
        d_state = b_tot[..., None] - bcum + ic
        m_new = jnp.maximum(b_tot + m_prev, jnp.max(d_state, axis=-1))
        w_state = jnp.exp(d_state - m_new[..., None])
        w_prev = jnp.exp(b_tot + m_prev - m_new)
        c_new = w_prev[..., None, None] * c_prev + jnp.einsum('bhs,bhsd,bhsv->bhdv', w_state, kc, vc)
        n_new = w_prev[..., None] * n_prev + jnp.einsum('bhs,bhsd->bhd', w_state, kc)
        return (c_new, n_new, m_new), h_out

    init = (jnp.zeros((b, h, A_DQK, A_DV), jnp.float32),
            jnp.zeros((b, h, A_DQK), jnp.float32),
            jnp.full((b, h), NEG, jnp.float32))
    _, hs = lax.scan(step, init, xs)
    return hs.transpose(1, 0, 3, 2, 4).reshape(b, s, h, A_DV)


def mla_attention(q, k, v):
    b, s, h, dk = q.shape
    nb = s // B_QBLOCK
    scale = (B_NOPE + B_ROPE) ** -0.5
    qb = q.reshape(b, nb, B_QBLOCK, h, dk).transpose(1, 0, 2, 3, 4)
    kpos = jnp.arange(s)

    def block(args):
        qblk, start = args
        sc = jnp.einsum('bqhd,bkhd->bhqk', qblk, k).astype(jnp.float32) * scale
        qpos = start + jnp.arange(B_QBLOCK)
        sc = jnp.where(kpos[None, :] <= qpos[:, None], sc, NEG)
        p = jax.nn.softmax(sc, axis=-1)
        return jnp.einsum('bhqk,bkhd->bqhd', p.astype(v.dtype), v)

    out = lax.map(block, (qb, jnp.arange(nb) * B_QBLOCK))
    return out.transpose(1, 0, 2, 3, 4).reshape(b, s, h * B_DV)


def causal_depthwise_conv(x, w, bias):
    y = lax.conv_general_dilated(x, w[:, None, :].astype(x.dtype), window_strides=(1,),
                                 padding=[(C_CONV - 1, 0)],
                                 dimension_numbers=('NWC', 'WIO', 'NWC'),
                                 feature_group_count=x.shape[-1])
    return y + bias


def rg_lru(x, w_a, b_a, w_x, b_x, lam):
    b, s, c = x.shape
    xb = x.reshape(b, s, C_BLOCKS, C_BLOCK_DIM)
    r = jax.nn.sigmoid(jnp.einsum('bsnd,nde->bsne', xb, w_a).reshape(b, s, c) + b_a).astype(jnp.float32)
    gi = jax.nn.sigmoid(jnp.einsum('bsnd,nde->bsne', xb, w_x).reshape(b, s, c) + b_x).astype(jnp.float32)
    log_a = -C_POW * r * jax.nn.softplus(-lam.astype(jnp.float32))
    a = jnp.exp(log_a)
    u = jnp.sqrt(-jnp.expm1(2.0 * log_a)) * (gi * x.astype(jnp.float32))

    def combine(left, right):
        a1, b1 = left
        a2, b2 = right
        return a1 * a2, a2 * b1 + b2

    _, h = lax.associative_scan(combine, (a, u), axis=1)
    return h.astype(x.dtype)


def hybrid_layer(x, cos, sin, ffn1_norm, ffn1_w_gate, ffn1_w_up, ffn1_w_down, mix_norm, w_in,
                 mlstm_gate_bias, mlstm_out_norm, mla_q_norm, mla_w_uq, mla_kv_norm, mla_w_ukv,
                 lru_conv_w, lru_conv_b, lru_w_a, lru_b_a, lru_w_x, lru_b_x, lru_lambda,
                 w_branch, w_out, ffn2_norm, ffn2_w_gate, ffn2_w_up, ffn2_w_down):
    b, s, _ = x.shape
    x = x + 0.5 * swiglu(rmsnorm(x, ffn1_norm), ffn1_w_gate, ffn1_w_up, ffn1_w_down)

    u = rmsnorm(x, mix_norm)
    proj = u @ w_in
    offsets = np.cumsum(IN_SPLITS)[:-1].tolist()
    (a_q, a_k, a_v, a_i, a_f, a_o, b_cq, b_ckv, b_kr, c_x, gates) = jnp.split(proj, offsets, axis=-1)

    h_a = mlstm_chunkwise(a_q.reshape(b, s, A_HEADS, A_DQK), a_k.reshape(b, s, A_HEADS, A_DQK),
                          a_v.reshape(b, s, A_HEADS, A_DV),
                          a_i + mlstm_gate_bias[:A_HEADS], a_f + mlstm_gate_bias[A_HEADS:])
    h_a = rmsnorm(h_a, mlstm_out_norm.reshape(A_HEADS, A_DV)).reshape(b, s, A_WIDTH)
    y_a = (jax.nn.sigmoid(a_o.astype(jnp.float32)) * h_a).astype(x.dtype)

    q = (rmsnorm(b_cq, mla_q_norm) @ mla_w_uq).reshape(b, s, B_HEADS, B_NOPE + B_ROPE)
    q_nope, q_rope = jnp.split(q, [B_NOPE], axis=-1)
    q_rope = apply_rope(q_rope, cos[None, :, None, :], sin[None, :, None, :])
    kv = (rmsnorm(b_ckv, mla_kv_norm) @ mla_w_ukv).reshape(b, s, B_HEADS, B_NOPE + B_DV)
    k_nope, v_b = jnp.split(kv, [B_NOPE], axis=-1)
    k_rope = apply_rope(b_kr, cos[None], sin[None])
    k_b = jnp.concatenate([k_nope, jnp.broadcast_to(k_rope[:, :, None, :], (b, s, B_HEADS, B_ROPE))], axis=-1)
    q_b = jnp.concatenate([q_nope, q_rope], axis=-1)
    y_b = mla_attention(q_b, k_b, v_b)

    xc = causal_depthwise_conv(c_x, lru_conv_w, lru_conv_b)
    y_c = rg_lru(xc, lru_w_a, lru_b_a, lru_w_x, lru_b_x, lru_lambda)

    branches = jnp.stack([y_a, y_b, y_c], axis=2)
    proj_b = jnp.einsum('bsjc,jcd->bsjd', branches, w_branch)
    g = jax.nn.sigmoid(gates.reshape(b, s, N_BRANCH, D_MODEL))
    z = jnp.sum(g * proj_b, axis=2)
    x = x + z @ w_out

    x = x + 0.5 * swiglu(rmsnorm(x, ffn2_norm), ffn2_w_gate, ffn2_w_up, ffn2_w_down)
    return x


def setup_inputs(seed: int = 0) -> dict:
    key = jax.random.key(seed)
    ks = jax.random.split(key, 32)
    f32 = jnp.float32
    L = DEPTH

    def nrm(k, shape, fan_in):
        return jax.random.normal(k, shape, f32) * (fan_in ** -0.5)

    def gain(k, shape):
        return 1.0 + 0.01 * jax.random.normal(k, shape, f32)

    f_bias = jnp.linspace(3.0, 6.0, A_HEADS, dtype=f32)[None, :] + 0.01 * jax.random.normal(ks[7], (L, A_HEADS), f32)
    i_bias = 0.1 * jax.random.normal(ks[8], (L, A_HEADS), f32)
    a8 = jax.random.uniform(ks[20], (L, C_WIDTH), f32, 0.9, 0.999)
    a_base = a8 ** (1.0 / C_POW)
    lam = jnp.log(a_base) - jnp.log1p(-a_base)
    return {
        "x": jax.random.normal(ks[0], (BATCH, SEQ, D_MODEL), f32),
        "ffn1_norm": gain(ks[1], (L, D_MODEL)),
        "ffn1_w_gate": nrm(ks[2], (L, D_MODEL, D_FF), D_MODEL),
        "ffn1_w_up": nrm(ks[3], (L, D_MODEL, D_FF), D_MODEL),
        "ffn1_w_down": nrm(ks[4], (L, D_FF, D_MODEL), D_FF),
        "mix_norm": gain(ks[5], (L, D_MODEL)),
        "w_in": nrm(ks[6], (L, D_MODEL, N_IN), D_MODEL),
        "mlstm_gate_bias": jnp.concatenate([i_bias, f_bias], axis=-1),
        "mlstm_out_norm": gain(ks[9], (L, A_WIDTH)),
        "mla_q_norm": gain(ks[10], (L, B_Q_LORA)),
        "mla_w_uq": nrm(ks[11], (L, B_Q_LORA, B_HEADS * (B_NOPE + B_ROPE)), B_Q_LORA),
        "mla_kv_norm": gain(ks[12], (L, B_KV_LORA)),
        "mla_w_ukv": nrm(ks[13], (L, B_KV_LORA, B_HEADS * (B_NOPE + B_DV)), B_KV_LORA),
        "lru_conv_w": nrm(ks[14], (L, C_CONV, C_WIDTH), C_CONV),
        "lru_conv_b": 0.01 * jax.random.normal(ks[15], (L, C_WIDTH), f32),
        "lru_w_a": nrm(ks[16], (L, C_BLOCKS, C_BLOCK_DIM, C_BLOCK_DIM), C_BLOCK_DIM),
        "lru_b_a": 0.01 * jax.random.normal(ks[17], (L, C_WIDTH), f32),
        "lru_w_x": nrm(ks[18], (L, C_BLOCKS, C_BLOCK_DIM, C_BLOCK_DIM), C_BLOCK_DIM),
        "lru_b_x": 0.01 * jax.random.normal(ks[19], (L, C_WIDTH), f32),
        "lru_lambda": lam,
        "w_branch": nrm(ks[21], (L, N_BRANCH, BR_WIDTH, D_MODEL), BR_WIDTH),
        "w_out": nrm(ks[22], (L, D_MODEL, D_MODEL), D_MODEL),
        "ffn2_norm": gain(ks[23], (L, D_MODEL)),
        "ffn2_w_gate": nrm(ks[24], (L, D_MODEL, D_FF), D_MODEL),
        "ffn2_w_up": nrm(ks[25], (L, D_MODEL, D_FF), D_MODEL),
        "ffn2_w_down": nrm(ks[26], (L, D_FF, D_MODEL), D_FF),
        "final_norm": gain(ks[27], (D_MODEL,)),
    }


def reference(x, ffn1_norm, ffn1_w_gate, ffn1_w_up, ffn1_w_down, mix_norm, w_in,
              mlstm_gate_bias, mlstm_out_norm, mla_q_norm, mla_w_uq, mla_kv_norm, mla_w_ukv,
              lru_conv_w, lru_conv_b, lru_w_a, lru_b_a, lru_w_x, lru_b_x, lru_lambda,
              w_branch, w_out, ffn2_norm, ffn2_w_gate, ffn2_w_up, ffn2_w_down, final_norm):
    s = x.shape[1]
    pos = jnp.arange(s, dtype=jnp.float32)
    inv_freq = jnp.power(ROPE_BASE, -jnp.arange(0, B_ROPE, 2, dtype=jnp.float32) / B_ROPE)
    ang = pos[:, None] * inv_freq[None, :]
    cos, sin = jnp.cos(ang), jnp.sin(ang)
    for l in range(DEPTH):
        x = hybrid_layer(x, cos, sin, ffn1_norm[l], ffn1_w_gate[l], ffn1_w_up[l], ffn1_w_down[l],
                         mix_norm[l], w_in[l], mlstm_gate_bias[l], mlstm_out_norm[l],
                         mla_q_norm[l], mla_w_uq[l], mla_kv_norm[l], mla_w_ukv[l],
                         lru_conv_w[l], lru_conv_b[l], lru_w_a[l], lru_b_a[l], lru_w_x[l], lru_b_x[l],
                         lru_lambda[l], w_branch[l], w_out[l],
                         ffn2_norm[l], ffn2_w_gate[l], ffn2_w_up[l], ffn2_w_down[l])
    return rmsnorm(x, final_norm)
```

```python
import contextlib
import numpy as np
import concourse.bass as bass
import concourse.mybir as mybir

F32 = mybir.dt.float32
BF16 = mybir.dt.bfloat16
AF = mybir.ActivationFunctionType
ALU = mybir.AluOpType
AX = mybir.AxisListType


class Buf:
    __slots__ = ("name", "w", "r", "rd", "excl")

    def __init__(self, name="", excl=False):
        self.name = name
        self.excl = excl
        self.w = None
        self.r = {}
        self.rd = []


class Op:
    __slots__ = ("eng", "fn", "deps", "is_dma", "key", "cnt", "marked", "raw")


class Sched:
    ENG = ("pe", "act", "dve", "pool", "sp")

    def __init__(self, nc, stack, same_engine_sync=True):
        self.nc = nc
        self.stack = stack
        self.ops = []
        self.engs = {"pe": nc.tensor, "act": nc.scalar, "dve": nc.vector,
                     "pool": nc.gpsimd, "sp": nc.sync}
        self.same_engine_sync = same_engine_sync
        self.dma_keys = {}
        self.nsb = 0

    def sbuf(self, name, shape, dtype):
        return self.stack.enter_context(self.nc.sbuf_tensor("sb_" + name, list(shape), dtype))

    def psum(self, name, shape, dtype=F32):
        return self.stack.enter_context(self.nc.psum_tensor("pp_" + name, list(shape), dtype))

    def op(self, eng, fn, reads=(), writes=(), dma=False):
        idx = len(self.ops)
        deps = {}
        if any(b.excl for b in reads):
            writes = tuple(writes) + tuple(b for b in reads if b.excl)
            reads = tuple(b for b in reads if not b.excl)
        for b in reads:
            if b.w is not None:
                deps[b.w] = True
        for b in writes:
            if b.w is not None:
                deps[b.w] = True
            for i in b.r.values():
                deps.setdefault(i, False)
            for i in b.rd:
                deps.setdefault(i, False)
        deps.pop(idx, None)
        o = Op()
        o.eng = eng; o.fn = fn; o.deps = deps; o.is_dma = dma; o.key = None
        o.cnt = 0; o.marked = False
        self.ops.append(o)
        for b in reads:
            if dma:
                b.rd.append(idx)
            else:
                b.r[eng] = idx
        for b in writes:
            b.w = idx; b.r = {}; b.rd = []
        return idx

    def dma(self, eng, out, in_, reads=(), writes=(), key=None, slow=False):
        if key is None:
            key = writes[0] if writes else reads[0]
        kb = self.dma_keys.get(id(key))
        if kb is None:
            kb = [Buf("k_" + key.name), None, 0, key]
            self.dma_keys[id(key)] = kb
        idx = self.op(eng, (lambda e: e.dma_start(out=out, in_=in_, allow_slow_non_contiguous=True)) if slow else (lambda e: e.dma_start(out=out, in_=in_)), reads=reads,
                      writes=tuple(writes) + (kb[0],), dma=True)
        self.ops[idx].key = kb
        return idx

    def _needs_wait(self, o, d, raw):
        od = self.ops[d]
        if od.is_dma:
            return True
        if od.eng != o.eng:
            return True
        if o.is_dma:
            return False
        if self.same_engine_sync and o.eng != "pe":
            return True
        return False

    def emit(self):
        nc = self.nc
        ops = self.ops
        for o in ops:
            for d, raw in o.deps.items():
                od = ops[d]
                need = self._needs_wait(o, d, raw)
                if o.is_dma and (not od.is_dma) and od.eng == o.eng:
                    need = True
                if need and not od.is_dma:
                    od.marked = True
        esem = {e: self.stack.enter_context(nc.semaphore("sem_" + e)) for e in self.ENG}
        ecnt = {e: 0 for e in self.ENG}
        for kb in self.dma_keys.values():
            kb[1] = self.stack.enter_context(nc.semaphore("semd%d" % len([1 for k in self.dma_keys.values() if k[1] is not None])))
        for o in ops:
            if o.is_dma:
                o.key[2] += 16
                o.cnt = o.key[2]
            elif o.marked:
                ecnt[o.eng] += 1
                o.cnt = ecnt[o.eng]
        seen = {e: {} for e in self.ENG}
        nwait = 0
        for o in ops:
            e = self.engs[o.eng]
            sn = seen[o.eng]
            waits = {}
            for d, raw in o.deps.items():
                od = ops[d]
                need = self._needs_wait(o, d, raw)
                if o.is_dma and (not od.is_dma) and od.eng == o.eng:
                    need = True
                if not need:
                    continue
                if od.is_dma:
                    sem = od.key[1]
                else:
                    sem = esem[od.eng]
                k = id(sem)
                if sn.get(k, 0) >= od.cnt:
                    continue
                if k not in waits or waits[k][1] < od.cnt:
                    waits[k] = (sem, od.cnt)
            for k, (sem, val) in waits.items():
                e.wait_ge(sem, val)
                sn[k] = val
                nwait += 1
            ins = o.fn(e)
            if o.is_dma:
                ins.then_inc(o.key[1], 16)
            elif o.marked:
                ins.then_inc(esem[o.eng], 1)
        sp = self.engs["sp"]
        for kb in self.dma_keys.values():
            if kb[2] > 0:
                sp.wait_ge(kb[1], kb[2])
        return dict(n_ops=len(ops), n_wait=nwait, counts=ecnt, n_dma_keys=len(self.dma_keys))

import math
import numpy as np

D = 2048
DFF = 5632
KC = D // 128
FC = DFF // 128
EPS = 1e-6


class Res:
    def __init__(self, S, T):
        self.S = S
        self.T = T
        self.TT = T // 512
        nc = S.nc
        self.x = S.sbuf("x", [128, KC, T], F32)
        self.xb = [Buf("x%d" % d) for d in range(KC)]
        self.u = S.sbuf("u", [128, KC, T], BF16)
        self.ub = [Buf("u%d" % d) for d in range(KC)]
        self.sq = [S.sbuf("sq%d" % i, [128, 512], F32) for i in range(2)]
        self.sqb = [Buf("sq%d" % i) for i in range(2)]
        self.rstd = S.sbuf("rstd", [128, T], F32)
        self.rstdb = Buf("rstd")
        self.ps = [S.psum("ps%d" % i, [128, 512]) for i in range(8)]
        self.psb = [Buf("ps%d" % i, excl=True) for i in range(8)]
        self.ones = S.sbuf("ones", [128, 128], F32)
        self.onesb = Buf("ones")
        S.op("pool", lambda e: e.memset(self.ones[:], 1.0), writes=[self.onesb])
        self.wgu = [[S.sbuf("wgu%d_%d" % (i, j), [128, KC, 256], BF16) for j in range(2)] for i in range(2)]
        self.wgub = [[Buf("wgu%d_%d" % (i, j)) for j in range(2)] for i in range(2)]
        self.wd = [S.sbuf("wd%d" % i, [128, 2, D], BF16) for i in range(2)]
        self.wdb = [Buf("wd%d" % i) for i in range(2)]
        self.h = [S.sbuf("h%d" % i, [128, 4, T], BF16) for i in range(2)]
        self.hb = [[Buf("h%d_%d" % (i, j)) for j in range(4)] for i in range(2)]
        self.sil = [S.sbuf("sil%d" % i, [128, 512], F32) for i in range(2)]
        self.silb = [Buf("sil%d" % i) for i in range(2)]
        self.cnt = {}

    def rot(self, name, n):
        v = self.cnt.get(name, 0)
        self.cnt[name] = v + 1
        return v % n


def load_vec_pp(S, name, dram_vec_ap, nchunks, scale=None):
    t = S.sbuf(name + "_sb", [128, nchunks], F32)
    b = Buf(name)
    S.dma("sp", t[:], dram_vec_ap.rearrange("(k p) -> p k", p=128), writes=[b], slow=True)
    if scale is not None:
        S.op("dve", lambda e: e.tensor_scalar(out=t[:], in0=t[:], scalar1=float(scale), scalar2=None, op0=ALU.mult),
             reads=[b], writes=[b])
    return t, b


def emit_rmsnorm_fm(S, R, g_t, g_b, src=None, srcb=None, dst=None, dstb=None, ps_id=6):
    T, TT = R.T, R.TT
    if src is None:
        src = [R.x[:, d, :] for d in range(KC)]; srcb = R.xb
    if dst is None:
        dst = [R.u[:, d, :] for d in range(KC)]; dstb = R.ub
    nchunks = len(src)
    Dn = nchunks * 128
    for tt in range(TT):
        sl = slice(tt * 512, (tt + 1) * 512)
        p = ps_id + (tt % 2)
        for d in range(nchunks):
            i = R.rot("sq", 2)
            S.op("act", lambda e, d=d, i=i, sl=sl: e.activation(out=R.sq[i][:, :], in_=src[d][:, sl], func=AF.Square),
                 reads=[srcb[d]], writes=[R.sqb[i]])
            S.op("pe", lambda e, d=d, i=i, p=p: e.matmul(R.ps[p][:, :], lhsT=R.ones[:, :], rhs=R.sq[i][:, :],
                                                         start=(d == 0), stop=(d == nchunks - 1)),
                 reads=[R.onesb, R.sqb[i]], writes=[R.psb[p]])
        S.op("dve", lambda e, sl=sl, p=p: e.tensor_scalar(out=R.rstd[:, sl], in0=R.ps[p][:, :],
                                                           scalar1=float(1.0 / Dn), scalar2=float(EPS), op0=ALU.mult, op1=ALU.add),
             reads=[R.psb[p]], writes=[R.rstdb])
        S.op("act", lambda e, sl=sl: e.activation(out=R.rstd[:, sl], in_=R.rstd[:, sl], func=AF.Sqrt),
             reads=[R.rstdb], writes=[R.rstdb])
        S.op("dve", lambda e, sl=sl: e.reciprocal(out=R.rstd[:, sl], in_=R.rstd[:, sl]),
             reads=[R.rstdb], writes=[R.rstdb])
    for d in range(nchunks):
        S.op("dve", lambda e, d=d: e.scalar_tensor_tensor(out=dst[d], in0=src[d], scalar=g_t[:, d:d + 1], in1=R.rstd[:, :],
                                                          op0=ALU.mult, op1=ALU.mult),
             reads=[srcb[d], g_b, R.rstdb], writes=[dstb[d]])


def emit_ffn(S, R, wg, wu, wd):
    T, TT = R.T, R.TT
    NG = FC // 4
    wg_v = wg.rearrange("(k p) f -> p k f", p=128)
    wu_v = wu.rearrange("(k p) f -> p k f", p=128)
    wd_v = wd.rearrange("(c p) d -> p c d", p=128)
    for g in range(NG):
        hi_ = R.rot("h", 2)
        for half in range(2):
            wi = R.rot("wgu", 2)
            c0 = g * 512 + half * 256
            S.dma("pool", R.wgu[wi][0][:], wg_v[:, :, c0:c0 + 256], writes=[R.wgub[wi][0]])
            S.dma("pool", R.wgu[wi][1][:], wu_v[:, :, c0:c0 + 256], writes=[R.wgub[wi][1]])
            for c2 in range(2):
                fc = half * 2 + c2
                for tt in range(TT):
                    pg = R.rot("psg", 2)
                    pu = 2 + pg
                    for k in range(KC):
                        S.op("pe", lambda e, k=k, wi=wi, c2=c2, tt=tt, pg=pg: e.matmul(
                            R.ps[pg][:, :], lhsT=R.wgu[wi][0][:, k, c2 * 128:(c2 + 1) * 128], rhs=R.u[:, k, tt * 512:(tt + 1) * 512],
                            start=(k == 0), stop=(k == KC - 1)),
                            reads=[R.wgub[wi][0], R.ub[k]], writes=[R.psb[pg]])
                    for k in range(KC):
                        S.op("pe", lambda e, k=k, wi=wi, c2=c2, tt=tt, pu=pu: e.matmul(
                            R.ps[pu][:, :], lhsT=R.wgu[wi][1][:, k, c2 * 128:(c2 + 1) * 128], rhs=R.u[:, k, tt * 512:(tt + 1) * 512],
                            start=(k == 0), stop=(k == KC - 1)),
                            reads=[R.wgub[wi][1], R.ub[k]], writes=[R.psb[pu]])
                    si = R.rot("sil", 2)
                    S.op("act", lambda e, si=si, pg=pg: e.activation(out=R.sil[si][:, :], in_=R.ps[pg][:, :], func=AF.Silu),
                         reads=[R.psb[pg]], writes=[R.silb[si]])
                    S.op("dve", lambda e, si=si, pu=pu, hi_=hi_, fc=fc, tt=tt: e.tensor_tensor(
                        out=R.h[hi_][:, fc, tt * 512:(tt + 1) * 512], in0=R.ps[pu][:, :], in1=R.sil[si][:, :], op=ALU.mult),
                        reads=[R.psb[pu], R.silb[si]], writes=[R.hb[hi_][fc]])
        for half in range(2):
            r0 = g * 4 + half * 2
            S.dma("pool", R.wd[half][:], wd_v[:, r0:r0 + 2, :], writes=[R.wdb[half]])
        for d in range(KC):
            for tt in range(TT):
                pd = 4 + R.rot("psd", 2)
                for fc in range(4):
                    S.op("pe", lambda e, d=d, tt=tt, pd=pd, fc=fc, hi_=hi_: e.matmul(
                        R.ps[pd][:, :], lhsT=R.wd[fc // 2][:, fc % 2, d * 128:(d + 1) * 128], rhs=R.h[hi_][:, fc, tt * 512:(tt + 1) * 512],
                        start=(fc == 0), stop=(fc == 3)),
                        reads=[R.wdb[fc // 2], R.hb[hi_][fc]], writes=[R.psb[pd]])
                S.op("dve", lambda e, d=d, tt=tt, pd=pd: e.scalar_tensor_tensor(
                    out=R.x[:, d, tt * 512:(tt + 1) * 512], in0=R.ps[pd][:, :], scalar=0.5, in1=R.x[:, d, tt * 512:(tt + 1) * 512],
                    op0=ALU.mult, op1=ALU.add),
                    reads=[R.psb[pd], R.xb[d]], writes=[R.xb[d]])


O_AQ, O_AK, O_AV, O_AI, O_AF, O_AO = 0, 512, 1024, 2048, 2052, 2056
O_BCQ, O_BCKV, O_BKR, O_CX, O_G = 3080, 3464, 3720, 3784, 4808
N_IN = 10952


class ResA:
    def __init__(self, S, R):
        T = R.T
        self.wt = [R.wgu[0][0], R.wgu[0][1], R.wgu[1][0], R.wgu[1][1]]
        self.wtb = [R.wgub[0][0], R.wgub[0][1], R.wgub[1][0], R.wgub[1][1]]
        self.stf = R.sil
        self.stfb = R.silb
        self.stb = [S.sbuf("stb%d" % i, [128, 512], BF16) for i in range(3)]
        self.stbb = [Buf("stb%d" % i) for i in range(3)]
        self.stm = [R.h[0][:, 0:2, :].rearrange("p a (b c) -> p (a b) c", c=256), R.h[0][:, 2:4, :].rearrange("p a (b c) -> p (a b) c", c=256)]
        self.stmb = [[R.hb[0][0], R.hb[0][1]], [R.hb[0][2], R.hb[0][3]]]
        self.gst = S.sbuf("gst", [128, T // 128, 8], F32)
        self.gstb = Buf("gst")
        lat4 = S.sbuf("lat4", [128, T], F32)
        latn4 = S.sbuf("latn4", [128, T], BF16)
        self.lat = [R.wd[0][:, 0, :].bitcast(F32), R.wd[0][:, 1, :].bitcast(F32), R.wd[1][:, 0, :].bitcast(F32), R.wd[1][:, 1, :].bitcast(F32), lat4[:, :]]
        self.latb = [R.wdb[0], R.wdb[0], R.wdb[1], R.wdb[1], Buf("lat4")]
        self.latn = [R.h[1][:, 0, :], R.h[1][:, 1, :], R.h[1][:, 2, :], R.h[1][:, 3, :], latn4[:, :]]
        self.latnb = [R.hb[1][0], R.hb[1][1], R.hb[1][2], R.hb[1][3], Buf("latn4")]
        self.cs = S.sbuf("cs_sb", [32, 2, 512], F32)
        self.csb = Buf("cs")
        rt2 = [S.sbuf("rt%d" % i, [32, 512], F32) for i in range(2)]
        self.rt = [R.sq[0][:32, :], R.sq[1][:32, :], rt2[0][:, :], rt2[1][:, :]]
        self.rtb = [R.sqb[0], R.sqb[1], Buf("rt2"), Buf("rt3")]


def emit_proj_fm(S, R, A, w_in, col0, ncols, evac, msz=128):
    w_v = w_in.rearrange("(k p) f -> p k f", p=128)
    c = 0
    ci = 0
    while c < ncols:
        wcols = min(256, ncols - c)
        wi = R.rot("wt", 4)
        S.dma("pool", A.wt[wi][:, :, :wcols], w_v[:, :, col0 + c:col0 + c + wcols], writes=[A.wtb[wi]])
        cc = 0
        while cc < wcols:
            m = min(msz, wcols - cc)
            for tt in range(R.TT):
                p = R.rot("psA", 4)
                for k in range(KC):
                    S.op("pe", lambda e, k=k, wi=wi, cc=cc, m=m, tt=tt, p=p: e.matmul(
                        R.ps[p][:m, :], lhsT=A.wt[wi][:, k, cc:cc + m], rhs=R.u[:, k, tt * 512:(tt + 1) * 512],
                        start=(k == 0), stop=(k == KC - 1)),
                        reads=[A.wtb[wi], R.ub[k]], writes=[R.psb[p]])
                evac(R.ps[p], R.psb[p], ci, tt, m)
            cc += m
            ci += 1
        c += wcols


def emit_proj_tm(S, R, A, w_in, col0, ncols, out_dram, t0):
    w_v = w_in.rearrange("(k p) f -> p k f", p=128)
    NB = R.T // 128
    for c in range(0, ncols, 256):
        wi = R.rot("wt", 4)
        S.dma("pool", A.wt[wi][:, :, :], w_v[:, :, col0 + c:col0 + c + 256], writes=[A.wtb[wi]])
        si = R.rot("stm", 2)
        for b in range(NB):
            p = R.rot("psA", 4)
            for k in range(KC):
                S.op("pe", lambda e, k=k, wi=wi, b=b, p=p: e.matmul(
                    R.ps[p][:, :256], lhsT=R.u[:, k, b * 128:(b + 1) * 128], rhs=A.wt[wi][:, k, :],
                    start=(k == 0), stop=(k == KC - 1)),
                    reads=[A.wtb[wi], R.ub[k]], writes=[R.psb[p]])
            if b % 2 == 0:
                S.op("act", lambda e, b=b, p=p, si=si: e.copy(out=A.stm[si][:, b, :], in_=R.ps[p][:, :256]),
                     reads=[R.psb[p]], writes=A.stmb[si])
            else:
                S.op("dve", lambda e, b=b, p=p, si=si: e.tensor_copy(out=A.stm[si][:, b, :], in_=R.ps[p][:, :256]),
                     reads=[R.psb[p]], writes=A.stmb[si])
        S.dma("sp", out_dram[t0:t0 + R.T, c:c + 256].rearrange("(b p) c -> p b c", p=128), A.stm[si],
              reads=A.stmb[si], key=A.stmb[si][0])


def emit_phase_a_proj(S, R, A, P, t0):
    T, TT = R.T, R.TT
    w_in = P["w_in"]
    w_v = w_in.rearrange("(k p) f -> p k f", p=128)

    def evac_bf16_out(dram, scale=None):
        def f(ps, psb, ci, tt, m):
            si = R.rot("stb", 3)
            if scale is None:
                S.op("act", lambda e: e.copy(out=A.stb[si][:m, :], in_=ps[:m, :]), reads=[psb], writes=[A.stbb[si]])
            else:
                S.op("act", lambda e: e.mul(out=A.stb[si][:m, :], in_=ps[:m, :], mul=float(scale)), reads=[psb], writes=[A.stbb[si]])
            S.dma("sp", dram[ci * 128:ci * 128 + m, t0 + tt * 512:t0 + (tt + 1) * 512], A.stb[si][:m, :], reads=[A.stbb[si]])
        return f

    emit_proj_fm(S, R, A, w_in, O_AQ, 512, evac_bf16_out(P["qaT"], 128 ** -0.5))
    emit_proj_fm(S, R, A, w_in, O_AK, 512, evac_bf16_out(P["kaT"]))
    emit_proj_tm(S, R, A, w_in, O_AK, 512, P["ka"], t0)
    emit_proj_tm(S, R, A, w_in, O_AV, 1024, P["va"], t0)
    wi = R.rot("wt", 4)
    S.dma("pool", A.wt[wi][:, :, :8], w_v[:, :, O_AI:O_AI + 8], writes=[A.wtb[wi]], slow=True)
    for b in range(T // 128):
        p = R.rot("psA", 4)
        for k in range(KC):
            S.op("pe", lambda e, k=k, wi=wi, b=b, p=p: e.matmul(
                R.ps[p][:, :8], lhsT=R.u[:, k, b * 128:(b + 1) * 128], rhs=A.wt[wi][:, k, :8],
                start=(k == 0), stop=(k == KC - 1)),
                reads=[A.wtb[wi], R.ub[k]], writes=[R.psb[p]])
        S.op("dve", lambda e, b=b, p=p: e.tensor_copy(out=A.gst[:, b, :], in_=R.ps[p][:, :8]), reads=[R.psb[p]], writes=[A.gstb])
    S.dma("sp", P["gif"][t0:t0 + T, :].rearrange("(b p) c -> p b c", p=128), A.gst[:, :, :], reads=[A.gstb], slow=True)

    def evac_cx(ps, psb, ci, tt, m):
        si = R.rot("stf", 2)
        S.op("act", lambda e: e.copy(out=A.stf[si][:m, :], in_=ps[:m, :]), reads=[psb], writes=[A.stfb[si]])
        S.dma("sp", P["cxT"][ci * 128:ci * 128 + m, t0 + tt * 512:t0 + (tt + 1) * 512], A.stf[si][:m, :], reads=[A.stfb[si]])
    emit_proj_fm(S, R, A, w_in, O_CX, 1024, evac_cx)

    def evac_lat(ps, psb, ci, tt, m):
        S.op("act", lambda e: e.copy(out=A.lat[ci][:, tt * 512:(tt + 1) * 512], in_=ps[:, :]), reads=[psb], writes=[A.latb[ci]])
    emit_proj_fm(S, R, A, w_in, O_BCQ, 640, evac_lat)
    emit_rmsnorm_fm(S, R, A.gq, A.gqb, src=A.lat[0:3], srcb=A.latb[0:3], dst=A.latn[0:3], dstb=A.latnb[0:3])
    emit_rmsnorm_fm(S, R, A.gkv, A.gkvb, src=A.lat[3:5], srcb=A.latb[3:5], dst=A.latn[3:5], dstb=A.latnb[3:5])

    def rope(psA, psAb, psB, psBb, tt, dst1, dst2, h):
        S.op("dve", lambda e: e.tensor_tensor(out=A.rt[0][:, :], in0=psA[:32, :], in1=A.cs[:, 0, :], op=ALU.mult), reads=[psAb, A.csb], writes=[A.rtb[0]])
        S.op("dve", lambda e: e.tensor_tensor(out=A.rt[1][:, :], in0=psB[:32, :], in1=A.cs[:, 1, :], op=ALU.mult), reads=[psBb, A.csb], writes=[A.rtb[1]])
        S.op("dve", lambda e: e.tensor_tensor(out=A.rt[2][:, :], in0=psA[:32, :], in1=A.cs[:, 1, :], op=ALU.mult), reads=[psAb, A.csb], writes=[A.rtb[2]])
        S.op("dve", lambda e: e.tensor_tensor(out=A.rt[3][:, :], in0=psB[:32, :], in1=A.cs[:, 0, :], op=ALU.mult), reads=[psBb, A.csb], writes=[A.rtb[3]])
        s1 = R.rot("stb", 3)
        S.op("dve", lambda e: e.tensor_tensor(out=A.stb[s1][:32, :], in0=A.rt[0][:, :], in1=A.rt[1][:, :], op=ALU.subtract), reads=[A.rtb[0], A.rtb[1]], writes=[A.stbb[s1]])
        S.dma("sp", dst1[h * 32:(h + 1) * 32, t0 + tt * 512:t0 + (tt + 1) * 512], A.stb[s1][:32, :], reads=[A.stbb[s1]])
        s2 = R.rot("stb", 3)
        S.op("dve", lambda e: e.tensor_tensor(out=A.stb[s2][:32, :], in0=A.rt[2][:, :], in1=A.rt[3][:, :], op=ALU.add), reads=[A.rtb[2], A.rtb[3]], writes=[A.stbb[s2]])
        S.dma("sp", dst2[h * 32:(h + 1) * 32, t0 + tt * 512:t0 + (tt + 1) * 512], A.stb[s2][:32, :], reads=[A.stbb[s2]])

    wuq_v = P["w_uq"].rearrange("(k p) f -> p k f", p=128)
    wukv_v = P["w_ukv"].rearrange("(k p) f -> p k f", p=128)

    def wt_view(wi, k, c):
        return A.wt[wi].rearrange("p k c -> p (k c)")[:, :k * c].rearrange("p (k c) -> p k c", k=k)

    for tt in range(TT):
        sl = slice(tt * 512, (tt + 1) * 512)
        S.dma("sp", A.cs[:, :, :], P["cs"][:, :, t0 + tt * 512:t0 + (tt + 1) * 512], writes=[A.csb])
        wk = R.rot("wt", 4)
        S.dma("pool", A.wt[wk][:, :, :64], w_v[:, :, O_BKR:O_BKR + 64], writes=[A.wtb[wk]])
        pa = R.rot("psA", 4); pb = R.rot("psA", 4)
        for (pp, c0) in ((pa, 0), (pb, 32)):
            for k in range(KC):
                S.op("pe", lambda e, k=k, pp=pp, c0=c0, sl=sl, wk=wk: e.matmul(
                    R.ps[pp][:32, :], lhsT=A.wt[wk][:, k, c0:c0 + 32], rhs=R.u[:, k, sl],
                    start=(k == 0), stop=(k == KC - 1)), reads=[A.wtb[wk], R.ub[k]], writes=[R.psb[pp]])
        rope(R.ps[pa], R.psb[pa], R.ps[pb], R.psb[pb], tt, P["kbr1T"], P["kbr2T"], 0)
        for hp in range(4):
            wq = R.rot("wt", 4)
            wqv = wt_view(wq, 3, 384)
            S.dma("pool", wqv, wuq_v[:, :, hp * 384:(hp + 1) * 384], writes=[A.wtb[wq]])
            for h2 in range(2):
                h = hp * 2 + h2
                p = R.rot("psA", 4)
                for k in range(3):
                    S.op("pe", lambda e, k=k, p=p, h2=h2, sl=sl, wqv=wqv: e.matmul(R.ps[p][:, :], lhsT=wqv[:, k, h2 * 192:h2 * 192 + 128], rhs=A.latn[k][:, sl],
                                                                     start=(k == 0), stop=(k == 2)), reads=[A.wtb[wq], A.latnb[k]], writes=[R.psb[p]])
                evac_bf16_out(P["qbnT"])(R.ps[p], R.psb[p], h, tt, 128)
                pa = R.rot("psA", 4); pb = R.rot("psA", 4)
                for (pp, c0) in ((pa, 128), (pb, 160)):
                    for k in range(3):
                        S.op("pe", lambda e, k=k, pp=pp, c0=c0, h2=h2, sl=sl, wqv=wqv: e.matmul(R.ps[pp][:32, :], lhsT=wqv[:, k, h2 * 192 + c0:h2 * 192 + c0 + 32], rhs=A.latn[k][:, sl],
                                                                                 start=(k == 0), stop=(k == 2)), reads=[A.wtb[wq], A.latnb[k]], writes=[R.psb[pp]])
                rope(R.ps[pa], R.psb[pa], R.ps[pb], R.psb[pb], tt, P["qbr1T"], P["qbr2T"], h)
    wkv = R.rot("wt", 4)
    wkvv = wt_view(wkv, 2, 2048)
    S.dma("pool", wkvv, wukv_v, writes=[A.wtb[wkv]])
    for h in range(8):
        for tt in range(TT):
            sl = slice(tt * 512, (tt + 1) * 512)
            p = R.rot("psA", 4)
            for k in range(2):
                S.op("pe", lambda e, k=k, p=p, h=h, sl=sl: e.matmul(R.ps[p][:, :], lhsT=wkvv[:, k, h * 256:h * 256 + 128], rhs=A.latn[3 + k][:, sl],
                                                                 start=(k == 0), stop=(k == 1)), reads=[A.wtb[wkv], A.latnb[3 + k]], writes=[R.psb[p]])
            evac_bf16_out(P["kbnT"])(R.ps[p], R.psb[p], h, tt, 128)
    wv = wkvv.rearrange("p k (h two c) -> p k h two c", two=2, c=128)
    for hq in range(4):
        si = R.rot("stm", 2)
        for b in range(T // 128):
            p = R.rot("psA", 4)
            for k in range(2):
                S.op("pe", lambda e, k=k, p=p, b=b, hq=hq: e.matmul(R.ps[p][:, :256], lhsT=A.latn[3 + k][:, b * 128:(b + 1) * 128], rhs=wv[:, k, 2 * hq:2 * hq + 2, 1, :],
                                                                 start=(k == 0), stop=(k == 1)), reads=[A.wtb[wkv], A.latnb[3 + k]], writes=[R.psb[p]])
            S.op("act", lambda e, b=b, p=p, si=si: e.copy(out=A.stm[si][:, b, :], in_=R.ps[p][:, :256]), reads=[R.psb[p]], writes=A.stmb[si])
        S.dma("sp", P["vb"][t0:t0 + T, hq * 256:(hq + 1) * 256].rearrange("(b p) c -> p b c", p=128), A.stm[si], reads=A.stmb[si], key=A.stmb[si][0])


def emit_merge(S, R, P, t0, gm_t, gm_b, gon_t, gon_b):
    T, TT = R.T, R.TT
    w_v = P["w_in"].rearrange("(k p) f -> p k f", p=128)
    wt = [R.wgu[0][0], R.wgu[0][1], R.wgu[1][0], R.wgu[1][1]]
    wtb = [R.wgub[0][0], R.wgub[0][1], R.wgub[1][0], R.wgub[1][1]]
    y = [R.h[0][:, c, :] for c in range(4)] + [R.h[1][:, c, :] for c in range(4)]
    yb = [R.hb[0][c] for c in range(4)] + [R.hb[1][c] for c in range(4)]
    ha = [R.wd[0][:, 0, :].bitcast(F32), R.wd[0][:, 1, :].bitcast(F32)]
    hab = [R.wdb[0], R.wdb[0]]
    hn = [R.wd[1][:, 0, :].bitcast(F32), R.wd[1][:, 1, :].bitcast(F32)]
    hnb = [R.wdb[1], R.wdb[1]]
    emit_rmsnorm_fm(S, R, gm_t, gm_b)
    for h in range(4):
        for c in range(2):
            S.dma("sp", ha[c], P["haT"][h * 256 + c * 128:h * 256 + (c + 1) * 128, t0:t0 + T], writes=[hab[c]], key=hab[c])
        emit_rmsnorm_fm(S, R, gon_t[:, 2 * h:2 * h + 2], gon_b, src=ha, srcb=hab, dst=hn, dstb=hnb)
        wi = R.rot("wt", 4)
        S.dma("pool", wt[wi][:, :, :], w_v[:, :, O_AO + h * 256:O_AO + (h + 1) * 256], writes=[wtb[wi]])
        for c in range(2):
            for tt in range(TT):
                sl = slice(tt * 512, (tt + 1) * 512)
                p = R.rot("psC", 2)
                for k in range(KC):
                    S.op("pe", lambda e, k=k, wi=wi, c=c, sl=sl, p=p: e.matmul(R.ps[p][:, :], lhsT=wt[wi][:, k, c * 128:(c + 1) * 128], rhs=R.u[:, k, sl],
                                                                          start=(k == 0), stop=(k == KC - 1)), reads=[wtb[wi], R.ub[k]], writes=[R.psb[p]])
                si = R.rot("sil", 2)
                S.op("act", lambda e, si=si, p=p: e.activation(out=R.sil[si][:, :], in_=R.ps[p][:, :], func=AF.Sigmoid), reads=[R.psb[p]], writes=[R.silb[si]])
                S.op("dve", lambda e, si=si, h=h, c=c, sl=sl: e.tensor_tensor(out=y[2 * h + c][:, sl], in0=R.sil[si][:, :], in1=hn[c][:, sl], op=ALU.mult),
                     reads=[R.silb[si], hnb[c]], writes=[yb[2 * h + c]])
    for j in range(3):
        if j > 0:
            src = P["ybT"] if j == 1 else P["ycT"]
            for c in range(8):
                S.dma("sp", y[c], src[c * 128:(c + 1) * 128, t0:t0 + T], writes=[yb[c]], key=yb[c])
        wb_v = P["w_branch"][j].rearrange("(k p) f -> p k f", p=128)
        for d2 in range(KC // 2):
            wg_i = R.rot("wt", 4)
            S.dma("pool", wt[wg_i][:, :, :], w_v[:, :, O_G + j * D + d2 * 256:O_G + j * D + (d2 + 1) * 256], writes=[wtb[wg_i]])
            wb_i = R.rot("wt", 4)
            wbv = wt[wb_i].rearrange("p k c -> p (k c)")[:, :8 * 256].rearrange("p (k c) -> p k c", k=8)
            S.dma("pool", wbv, wb_v[:, :, d2 * 256:(d2 + 1) * 256], writes=[wtb[wb_i]])
            for c in range(2):
                d = d2 * 2 + c
                for tt in range(TT):
                    sl = slice(tt * 512, (tt + 1) * 512)
                    pg = R.rot("psC", 2)
                    pp = 2 + R.rot("psC2", 2)
                    for k in range(KC):
                        S.op("pe", lambda e, k=k, wg_i=wg_i, c=c, sl=sl, pg=pg: e.matmul(R.ps[pg][:, :], lhsT=wt[wg_i][:, k, c * 128:(c + 1) * 128], rhs=R.u[:, k, sl],
                                                                                   start=(k == 0), stop=(k == KC - 1)), reads=[wtb[wg_i], R.ub[k]], writes=[R.psb[pg]])
                    for k in range(8):
                        S.op("pe", lambda e, k=k, wbv=wbv, c=c, sl=sl, pp=pp: e.matmul(R.ps[pp][:, :], lhsT=wbv[:, k, c * 128:(c + 1) * 128], rhs=y[k][:, sl],
                                                                                 start=(k == 0), stop=(k == 7)), reads=[wtb[wb_i], yb[k]], writes=[R.psb[pp]])
                    si = R.rot("sil", 2)
                    S.op("act", lambda e, si=si, pg=pg: e.activation(out=R.sil[si][:, :], in_=R.ps[pg][:, :], func=AF.Sigmoid), reads=[R.psb[pg]], writes=[R.silb[si]])
                    if j == 0:
                        S.op("dve", lambda e, si=si, pp=pp, d=d, sl=sl: e.tensor_tensor(out=R.x[:, d, sl], in0=R.ps[pp][:, :], in1=R.sil[si][:, :], op=ALU.mult),
                             reads=[R.psb[pp], R.silb[si]], writes=[R.xb[d]])
                    else:
                        qi = R.rot("sq", 2)
                        S.op("dve", lambda e, si=si, pp=pp, qi=qi: e.tensor_tensor(out=R.sq[qi][:, :], in0=R.ps[pp][:, :], in1=R.sil[si][:, :], op=ALU.mult),
                             reads=[R.psb[pp], R.silb[si]], writes=[R.sqb[qi]])
                        S.op("dve", lambda e, qi=qi, d=d, sl=sl: e.tensor_tensor(out=R.x[:, d, sl], in0=R.x[:, d, sl], in1=R.sq[qi][:, :], op=ALU.add),
                             reads=[R.sqb[qi], R.xb[d]], writes=[R.xb[d]])
    for d in range(KC):
        S.op("act", lambda e, d=d: e.copy(out=R.u[:, d, :], in_=R.x[:, d, :]), reads=[R.xb[d]], writes=[R.ub[d]])
    for d in range(KC):
        S.dma("sp", R.x[:, d, :], P["x1T"][d * 128:(d + 1) * 128, t0:t0 + T], writes=[R.xb[d]])
    wo_v = P["w_out"].rearrange("(k p) f -> p k f", p=128)
    for d2 in range(KC // 2):
        wi = R.rot("wt", 4)
        S.dma("pool", wt[wi][:, :, :], wo_v[:, :, d2 * 256:(d2 + 1) * 256], writes=[wtb[wi]])
        for c in range(2):
            d = d2 * 2 + c
            for tt in range(TT):
                sl = slice(tt * 512, (tt + 1) * 512)
                p = R.rot("psC", 2)
                for k in range(KC):
                    S.op("pe", lambda e, k=k, wi=wi, c=c, sl=sl, p=p: e.matmul(R.ps[p][:, :], lhsT=wt[wi][:, k, c * 128:(c + 1) * 128], rhs=R.u[:, k, sl],
                                                                          start=(k == 0), stop=(k == KC - 1)), reads=[wtb[wi], R.ub[k]], writes=[R.psb[p]])
                S.op("dve", lambda e, p=p, d=d, sl=sl: e.tensor_tensor(out=R.x[:, d, sl], in0=R.ps[p][:, :], in1=R.x[:, d, sl], op=ALU.add),
                     reads=[R.psb[p], R.xb[d]], writes=[R.xb[d]])

import math
import numpy as np


class ResB:
    def __init__(self, S, SEQ):
        self.S = S
        self.SEQ = SEQ
        self.psw = [S.psum("psw%d" % i, [128, 1024]) for i in range(2)]
        self.pswb = [Buf("psw%d" % i, excl=True) for i in range(2)]
        ps4 = [S.psum("psb%d" % i, [128, 512]) for i in range(4, 8)]
        self.ps = [self.psw[0][:, 0:512], self.psw[0][:, 512:1024], self.psw[1][:, 0:512], self.psw[1][:, 512:1024]] + ps4
        self.psb = [self.pswb[0], self.pswb[0], self.pswb[1], self.pswb[1]] + [Buf("psb%d" % i, excl=True) for i in range(4, 8)]
        self.cnt = {}

    def rot(self, name, n):
        v = self.cnt.get(name, 0)
        self.cnt[name] = v + 1
        return v % n


def load_pp(S, name, ap2d, shape, eng="sp"):
    t = S.sbuf(name + "_sb", shape, F32)
    b = Buf(name)
    S.dma(eng, t[:], ap2d, writes=[b], slow=True)
    return t, b


def gen_lru(S, B, P, TP=512):
    SEQ = B.SEQ
    nc = S.nc
    prm, prmb = load_pp(S, "lru_prm", P["lru_prm"], [128, 8])
    wa = S.sbuf("lru_wa_sb", [128, 128], BF16); wab = Buf("lru_wa")
    wx = S.sbuf("lru_wx_sb", [128, 128], BF16); wxb = Buf("lru_wx")
    S.dma("pool", wa[:], P["lru_wa"], writes=[wab])
    S.dma("pool", wx[:], P["lru_wx"], writes=[wxb])
    cv = S.sbuf("lru_cv", [128, 4], F32); cvb = Buf("lru_cv")
    S.op("act", lambda e: e.activation(out=cv[:, 2:3], in_=prm[:, 7:8], func=AF.Exp, scale=-1.0), reads=[prmb], writes=[cvb])
    S.op("dve", lambda e: e.tensor_scalar(out=cv[:, 2:3], in0=cv[:, 2:3], scalar1=1.0, scalar2=None, op0=ALU.add), reads=[cvb], writes=[cvb])
    S.op("act", lambda e: e.activation(out=cv[:, 3:4], in_=cv[:, 2:3], func=AF.Ln), reads=[cvb], writes=[cvb])
    S.op("dve", lambda e: e.tensor_scalar(out=cv[:, 0:1], in0=cv[:, 3:4], scalar1=-8.0, scalar2=None, op0=ALU.mult), reads=[cvb], writes=[cvb])
    S.op("dve", lambda e: e.tensor_scalar(out=cv[:, 1:2], in0=cv[:, 3:4], scalar1=-16.0, scalar2=None, op0=ALU.mult), reads=[cvb], writes=[cvb])
    xin = [S.sbuf("lru_x%d" % i, [128, 3 + TP], F32) for i in range(2)]
    xinb = [Buf("lru_x%d" % i) for i in range(2)]
    xc = S.sbuf("lru_xc", [128, TP], F32); xcb = Buf("lru_xc")
    xcbf = S.sbuf("lru_xcbf", [128, TP], BF16); xcbfb = Buf("lru_xcbf")
    r = S.sbuf("lru_r", [128, TP], F32); rb = Buf("lru_r")
    gi = S.sbuf("lru_gi", [128, TP], F32); gib = Buf("lru_gi")
    a = S.sbuf("lru_a", [128, TP], F32); ab = Buf("lru_a")
    hh = [S.sbuf("lru_h%d" % i, [128, TP], F32) for i in range(2)]
    hb = [Buf("lru_h%d" % i) for i in range(2)]
    ho = [S.sbuf("lru_ho%d" % i, [128, TP], BF16) for i in range(2)]
    hob = [Buf("lru_ho%d" % i) for i in range(2)]
    S.op("pool", lambda e: e.memset(xin[0][:, 0:3], 0.0), writes=[xinb[0]])
    NP = SEQ // TP
    for pc in range(NP):
        i = pc % 2
        t0 = pc * TP
        S.dma("sp", xin[i][:, 3:3 + TP], P["cxT"][:, t0:t0 + TP], writes=[xinb[i]])
        if pc + 1 < NP:
            S.op("pool", lambda e, i=i: e.tensor_copy(out=xin[1 - i][:, 0:3], in_=xin[i][:, TP:TP + 3]), reads=[xinb[i]], writes=[xinb[1 - i]])
        S.op("dve", lambda e, i=i: e.tensor_scalar(out=xc[:, :], in0=xin[i][:, 3:3 + TP], scalar1=prm[:, 3:4], scalar2=prm[:, 4:5], op0=ALU.mult, op1=ALU.add),
             reads=[xinb[i], prmb], writes=[xcb])
        for j in range(3):
            S.op("dve", lambda e, i=i, j=j: e.scalar_tensor_tensor(out=xc[:, :], in0=xin[i][:, j:j + TP], scalar=prm[:, j:j + 1], in1=xc[:, :], op0=ALU.mult, op1=ALU.add),
                 reads=[xinb[i], prmb, xcb], writes=[xcb])
        S.op("act", lambda e: e.copy(out=xcbf[:, :], in_=xc[:, :]), reads=[xcb], writes=[xcbfb])
        for tt in range(TP // 512):
            sl = slice(tt * 512, (tt + 1) * 512)
            S.op("pe", lambda e, sl=sl: e.matmul(B.ps[6][:, :], lhsT=wa[:, :], rhs=xcbf[:, sl], start=True, stop=True), reads=[wab, xcbfb], writes=[B.psb[6]])
            S.op("pe", lambda e, sl=sl: e.matmul(B.ps[7][:, :], lhsT=wx[:, :], rhs=xcbf[:, sl], start=True, stop=True), reads=[wxb, xcbfb], writes=[B.psb[7]])
            S.op("act", lambda e, sl=sl: e.activation(out=r[:, sl], in_=B.ps[6][:, :], func=AF.Sigmoid, bias=prm[:, 5:6]), reads=[B.psb[6], prmb], writes=[rb])
            S.op("act", lambda e, sl=sl: e.activation(out=gi[:, sl], in_=B.ps[7][:, :], func=AF.Sigmoid, bias=prm[:, 6:7]), reads=[B.psb[7], prmb], writes=[gib])
        S.op("act", lambda e: e.activation(out=a[:, :], in_=r[:, :], func=AF.Exp, scale=cv[:, 0:1]), reads=[rb, cvb], writes=[ab])
        S.op("act", lambda e: e.activation(out=r[:, :], in_=r[:, :], func=AF.Exp, scale=cv[:, 1:2]), reads=[rb, cvb], writes=[rb])
        S.op("dve", lambda e: e.tensor_scalar(out=r[:, :], in0=r[:, :], scalar1=-1.0, scalar2=1.0, op0=ALU.mult, op1=ALU.add), reads=[rb], writes=[rb])
        S.op("act", lambda e: e.activation(out=r[:, :], in_=r[:, :], func=AF.Sqrt), reads=[rb], writes=[rb])
        S.op("dve", lambda e: e.tensor_tensor(out=gi[:, :], in0=gi[:, :], in1=xc[:, :], op=ALU.mult), reads=[gib, xcb], writes=[gib])
        S.op("dve", lambda e: e.tensor_tensor(out=gi[:, :], in0=gi[:, :], in1=r[:, :], op=ALU.mult), reads=[gib, rb], writes=[gib])
        if pc == 0:
            S.op("dve", lambda e, i=i: e.tensor_tensor_scan(out=hh[i][:, :], data0=a[:, :], data1=gi[:, :], initial=0.0, op0=ALU.mult, op1=ALU.add),
                 reads=[ab, gib], writes=[hb[i]])
        else:
            S.op("dve", lambda e, i=i: e.tensor_tensor_scan(out=hh[i][:, :], data0=a[:, :], data1=gi[:, :], initial=hh[1 - i][:, TP - 1:TP], op0=ALU.mult, op1=ALU.add),
                 reads=[ab, gib, hb[1 - i]], writes=[hb[i]])
        S.op("act", lambda e, i=i: e.copy(out=ho[i][:, :], in_=hh[i][:, :]), reads=[hb[i]], writes=[hob[i]])
        S.dma("sp", P["ycT"][:, t0:t0 + TP], ho[i][:, :], reads=[hob[i]])
        yield


def gen_mlstm(S, B, P, NCP=16):
    SEQ = B.SEQ
    NCH = SEQ // 128
    NCP = min(NCP, NCH)
    tri, trib = load_pp(S, "ml_tri", P["tri"], [128, 128])
    gbias, gbb = load_pp(S, "ml_gb", P["gbias"], [128, 2])
    ones = S.sbuf("ml_ones", [128, 128], F32); onesb = Buf("ml_ones")
    S.op("pool", lambda e: e.memset(ones[:], 1.0), writes=[onesb])
    S.op("dve", lambda e: e.tensor_scalar(out=gbias[:, 1:2], in0=gbias[:, 1:2], scalar1=-1.0, scalar2=None, op0=ALU.mult), reads=[gbb], writes=[gbb])
    qT = [S.sbuf("ml_qT%d" % i, [128, NCP * 128], BF16) for i in range(2)]; qTb = [Buf("ml_qT%d" % i) for i in range(2)]
    kT = [S.sbuf("ml_kT%d" % i, [128, NCP * 128], BF16) for i in range(2)]; kTb = [Buf("ml_kT%d" % i) for i in range(2)]
    kk = [S.sbuf("ml_k%d" % i, [128, NCP, 128], BF16) for i in range(2)]; kkb = [Buf("ml_k%d" % i) for i in range(2)]
    vv = [S.sbuf("ml_v%d" % i, [128, NCP, 128], BF16) for i in range(2)]; vvb = [Buf("ml_v%d" % i) for i in range(2)]
    gg = [S.sbuf("ml_g%d" % i, [128, NCP, 2], F32) for i in range(2)]; ggb = [Buf("ml_g%d" % i) for i in range(2)]
    nlf = S.sbuf("ml_nlf", [128, NCP], F32); nlfb = Buf("ml_nlf")
    ii = S.sbuf("ml_i", [128, NCP], F32); iib = Buf("ml_i")
    wv = S.sbuf("ml_wv", [128, NCP], F32); wvb = Buf("ml_wv")
    wq = S.sbuf("ml_wq", [128, NCP], F32); wqb = Buf("ml_wq")
    wL = S.sbuf("ml_wL", [128, NCP], F32); wLb = Buf("ml_wL")
    vs = S.sbuf("ml_vs", [128, NCP, 129], BF16); vsb = Buf("ml_vs")
    stb = S.sbuf("ml_st", [128, NCP, 128], BF16); stbb = [Buf("ml_st%d" % c) for c in range(NCP)]
    C = S.sbuf("ml_C", [128, 129], F32); Cb = Buf("ml_C")
    Cbf = S.sbuf("ml_Cbf", [128, NCP + 1, 129], BF16); Cbfb = [Buf("ml_Cbf%d" % c) for c in range(NCP + 1)]
    oall = S.sbuf("ml_oall", [128, NCP, 129], F32); oallb = Buf("ml_oall")
    t1 = S.sbuf("ml_t1", [128, NCP], F32); t1b = Buf("ml_t1")
    _h0 = S.sbuf("ml_ho0", [128, NCP, 128], F32); _hb0 = Buf("ml_ho0")
    hout = [_h0, _h0]; houtb = [_hb0, _hb0]
    S.op("pool", lambda e: e.memset(C[:], 0.0), writes=[Cb])
    PB = (6, 7, 6, 7)
    for pc in range(NCH // NCP):
        i = pc % 2
        t0 = pc * NCP * 128
        T = NCP * 128
        S.dma("sp", qT[i][:, :], P["qaT"][:, t0:t0 + T], writes=[qTb[i]])
        S.dma("sp", kT[i][:, :], P["kaT"][:, t0:t0 + T], writes=[kTb[i]])
        S.dma("sp", kk[i][:, :, :], P["ka"][t0:t0 + T, :].rearrange("(b p) d -> p b d", p=128), writes=[kkb[i]])
        S.dma("sp", vv[i][:, :, :], P["va"][t0:t0 + T, :].rearrange("(b p) d -> p b d", p=128), writes=[vvb[i]])
        S.dma("sp", gg[i][:, :, :], P["gif"][t0:t0 + T, :].rearrange("(b p) d -> p b d", p=128), writes=[ggb[i]], slow=True)
        S.op("act", lambda e, i=i: e.activation(out=nlf[:, :], in_=gg[i][:, :, 1], func=AF.Exp, scale=-1.0, bias=gbias[:, 1:2]), reads=[ggb[i], gbb], writes=[nlfb])
        S.op("dve", lambda e: e.tensor_scalar(out=nlf[:, :], in0=nlf[:, :], scalar1=1.0, scalar2=None, op0=ALU.add), reads=[nlfb], writes=[nlfb])
        S.op("act", lambda e: e.activation(out=nlf[:, :], in_=nlf[:, :], func=AF.Ln), reads=[nlfb], writes=[nlfb])
        S.op("dve", lambda e, i=i: e.tensor_scalar(out=ii[:, :], in0=gg[i][:, :, 0], scalar1=gbias[:, 0:1], scalar2=None, op0=ALU.add), reads=[ggb[i], gbb], writes=[iib])
        pcum = B.ps[PB[1]]; pcumb = B.psb[PB[1]]
        S.op("pe", lambda e: e.matmul(pcum[:, 0:NCP], lhsT=tri[:, :], rhs=nlf[:, :], start=True, stop=True), reads=[trib, nlfb], writes=[pcumb])
        S.op("pe", lambda e: e.matmul(pcum[:, 32:32 + NCP], lhsT=ones[:, :], rhs=nlf[:, :], start=True, stop=True), reads=[onesb, nlfb], writes=[pcumb])
        S.op("act", lambda e: e.activation(out=wq[:, :], in_=pcum[:, 0:NCP], func=AF.Exp, scale=-1.0), reads=[pcumb], writes=[wqb])
        S.op("act", lambda e: e.activation(out=wL[:, :], in_=pcum[:, 32:32 + NCP], func=AF.Exp, scale=-1.0), reads=[pcumb], writes=[wLb])
        S.op("act", lambda e: e.copy(out=wv[:, :], in_=pcum[:, 0:NCP]), reads=[pcumb], writes=[wvb])
        S.op("dve", lambda e: e.tensor_tensor(out=wv[:, :], in0=wv[:, :], in1=ii[:, :], op=ALU.add), reads=[wvb, iib], writes=[wvb])
        S.op("act", lambda e: e.activation(out=wv[:, :], in_=wv[:, :], func=AF.Exp), reads=[wvb], writes=[wvb])
        S.op("dve", lambda e, i=i: e.tensor_tensor(out=vs[:, :, 0:128], in0=vv[i][:, :, :], in1=wv[:, :].unsqueeze(2).to_broadcast([128, NCP, 128]), op=ALU.mult),
             reads=[vvb[i], wvb], writes=[vsb])
        S.op("dve", lambda e: e.tensor_copy(out=vs[:, :, 128:129], in_=wv[:, :].unsqueeze(2)), reads=[wvb], writes=[vsb])
        S.op("dve", lambda e: e.tensor_copy(out=Cbf[:, 0, :], in_=C[:, :]), reads=[Cb], writes=[Cbfb[0]])
        for c in range(NCP):
            pb = PB[c % 2]
            S.op("pe", lambda e, i=i, c=c, pb=pb: e.matmul(B.ps[pb][:, 0:129], lhsT=kk[i][:, c, :], rhs=vs[:, c, :], start=True, stop=True), reads=[kkb[i], vsb], writes=[B.psb[pb]])
            S.op("dve", lambda e, c=c: e.tensor_scalar(out=C[:, :], in0=C[:, :], scalar1=wL[:, c:c + 1], scalar2=None, op0=ALU.mult), reads=[Cb, wLb], writes=[Cb])
            S.op("dve", lambda e, c=c, pb=pb: e.scalar_tensor_tensor(out=C[:, :], in0=B.ps[pb][:, 0:129], scalar=wL[:, c:c + 1], in1=C[:, :], op0=ALU.mult, op1=ALU.add),
                 reads=[B.psb[pb], wLb, Cb], writes=[Cb])
            S.op("dve", lambda e, c=c: e.tensor_copy(out=Cbf[:, c + 1, :], in_=C[:, :]), reads=[Cb], writes=[Cbfb[c + 1]])
            if c % 4 == 3:
                yield
        for c in range(NCP):
            pb = PB[2 + c % 2]
            sl = slice(c * 128, (c + 1) * 128)
            S.op("pe", lambda e, i=i, sl=sl, pb=pb: e.matmul(B.ps[pb][:, 0:128], lhsT=kT[i][:, sl], rhs=qT[i][:, sl], start=True, stop=True), reads=[kTb[i], qTb[i]], writes=[B.psb[pb]])
            S.op("dve", lambda e, c=c, pb=pb: e.tensor_tensor(out=stb[:, c, :], in0=B.ps[pb][:, 0:128], in1=tri[:, :], op=ALU.mult), reads=[B.psb[pb], trib], writes=[stbb[c]])
            if c % 4 == 3:
                yield
        for c in range(NCP):
            pb = PB[c % 2]
            sl = slice(c * 128, (c + 1) * 128)
            S.op("pe", lambda e, c=c, pb=pb: e.matmul(B.ps[pb][:, 0:129], lhsT=stb[:, c, :], rhs=vs[:, c, :], start=True, stop=False), reads=[stbb[c], vsb], writes=[B.psb[pb]])
            S.op("pe", lambda e, i=i, c=c, sl=sl, pb=pb: e.matmul(B.ps[pb][:, 0:129], lhsT=qT[i][:, sl], rhs=Cbf[:, c, :], start=False, stop=True), reads=[qTb[i], Cbfb[c]], writes=[B.psb[pb]])
            S.op("act", lambda e, c=c, pb=pb: e.copy(out=oall[:, c, :], in_=B.ps[pb][:, 0:129]), reads=[B.psb[pb]], writes=[oallb])
            if c % 4 == 3:
                yield
        S.op("dve", lambda e: e.tensor_tensor(out=t1[:, :], in0=oall[:, :, 128], in1=wq[:, :], op=ALU.mult), reads=[oallb, wqb], writes=[t1b])
        S.op("act", lambda e: e.activation(out=t1[:, :], in_=t1[:, :], func=AF.Abs), reads=[t1b], writes=[t1b])
        S.op("dve", lambda e: e.tensor_scalar_max(out=t1[:, :], in0=t1[:, :], scalar1=1.0), reads=[t1b], writes=[t1b])
        S.op("dve", lambda e: e.reciprocal(out=t1[:, :], in_=t1[:, :]), reads=[t1b], writes=[t1b])
        S.op("dve", lambda e: e.tensor_tensor(out=t1[:, :], in0=t1[:, :], in1=wq[:, :], op=ALU.mult), reads=[t1b, wqb], writes=[t1b])
        S.op("dve", lambda e, i=i: e.tensor_tensor(out=hout[i][:, :, :], in0=oall[:, :, 0:128], in1=t1[:, :].unsqueeze(2).to_broadcast([128, NCP, 128]), op=ALU.mult),
             reads=[oallb, t1b], writes=[houtb[i]])
        S.dma("sp", P["ha"][t0:t0 + T, :].rearrange("(b p) d -> p b d", p=128), hout[i][:, :, :], reads=[houtb[i]])
        yield


def gen_mla(S, B, P):
    SEQ = B.SEQ
    NKB = SEQ // 128
    NQT = SEQ // 512
    kT = S.sbuf("at_kT", [128, SEQ], BF16); kTb = Buf("at_kT")
    kr = S.sbuf("at_kr", [128, SEQ], BF16); krb = Buf("at_kr")
    S.op("pool", lambda e: e.memset(kr[64:128, :], 0.0), writes=[krb])
    V = S.sbuf("at_V", [128, NKB, 128], BF16); Vb = Buf("at_V")
    msk = S.sbuf("at_msk", [128, 4, 512], BF16); mskb = Buf("at_msk")
    S.dma("sp", msk[:, :, :], P["masks"].rearrange("o k q -> k o q"), writes=[mskb])
    QCH = 4096
    for c0 in range(0, SEQ, QCH):
        c1 = min(SEQ, c0 + QCH)
        S.dma("sp", kT[:, c0:c1], P["kbT"][:, c0:c1], writes=[kTb], key=kTb)
        S.dma("sp", kr[0:64, c0:c1], P["krT"][:, c0:c1], writes=[krb], key=krb)
    vb_v = P["vb"].rearrange("(b p) d -> p b d", p=128)
    for b0 in range(0, NKB, 16):
        b1 = min(NKB, b0 + 16)
        S.dma("sp", V[:, b0:b1, 0:128], vb_v[:, b0:b1, :], writes=[Vb], key=Vb)
    qT = [S.sbuf("at_q%d" % i, [128, 512], BF16) for i in range(2)]; qTb = [Buf("at_q%d" % i) for i in range(2)]
    qr = [S.sbuf("at_qr%d" % i, [128, 512], BF16) for i in range(2)]; qrb = [Buf("at_qr%d" % i) for i in range(2)]
    for i_ in range(2):
        S.op("pool", lambda e, i_=i_: e.memset(qr[i_][64:128, :], 0.0), writes=[qrb[i_]])
    pT = [S.sbuf("at_p%d" % i, [128, 1024], BF16) for i in range(3)]; pTb = [Buf("at_p%d" % i) for i in range(3)]
    PS_O, PS_SUM = 4, 5
    ones = S.sbuf("at_ones", [128, 1], BF16); onesb = Buf("at_ones")
    S.op("pool", lambda e: e.memset(ones[:], 1.0), writes=[onesb])
    onesf = S.sbuf("at_onesf", [1, 128], F32); onesfb = Buf("at_onesf")
    S.op("pool", lambda e: e.memset(onesf[:], 1.0), writes=[onesfb])
    rs = S.sbuf("at_rs", [1, 512], F32); rsb = Buf("at_rs")
    acc = S.sbuf("at_acc", [128, 512], F32); accb = Buf("at_acc")
    s2 = [S.sbuf("at_s2_%d" % i_, [128, 512], BF16) for i_ in range(2)]; s2b = [Buf("at_s2_%d" % i_) for i_ in range(2)]
    onescol = S.sbuf("at_onescol", [128, 1], F32); onescolb = Buf("at_onescol")
    S.op("pool", lambda e: e.memset(onescol[:], 1.0), writes=[onescolb])
    yT = [S.sbuf("at_yT%d" % i, [128, 512], BF16) for i in range(2)]; yTb = [Buf("at_yT%d" % i) for i in range(2)]
    steps = [(j, k2) for j in range(NQT) for k2 in range(2 * j + 2)]

    def emit_st(t):
        j, k2 = steps[t]
        i = j % 2
        if k2 == 0:
            q0 = j * 512
            S.dma("sp", qT[i][:, :], P["qbT"][:, q0:q0 + 512], writes=[qTb[i]])
            S.dma("sp", qr[i][0:64, :], P["qrT"][:, q0:q0 + 512], writes=[qrb[i]])
        w = t % 2
        for hf in range(2):
            kb = 2 * k2 + hf
            ks = slice(kb * 128, (kb + 1) * 128)
            pss = B.psw[w][:, hf * 512:(hf + 1) * 512]
            S.op("pe", lambda e, ks=ks, pss=pss: e.matmul(pss, lhsT=kT[:, ks], rhs=qT[i][:, :], start=True, stop=False), reads=[kTb, qTb[i]], writes=[B.pswb[w]])
            S.op("pe", lambda e, ks=ks, pss=pss: e.matmul(pss, lhsT=kr[:, ks], rhs=qr[i][:, :], start=False, stop=True), reads=[krb, qrb[i]], writes=[B.pswb[w]])
        pi = t % 3
        S.op("act", lambda e: e.activation(out=pT[pi][:, :], in_=B.psw[w][:, :], func=AF.Exp), reads=[B.pswb[w]], writes=[pTb[pi]])
        for hf in range(2):
            o = 2 * k2 + hf - 4 * j
            if o >= 0:
                S.op("dve", lambda e, hf=hf, o=o: e.tensor_tensor(out=pT[pi][:, hf * 512 + o * 128:(hf + 1) * 512], in0=pT[pi][:, hf * 512 + o * 128:(hf + 1) * 512], in1=msk[:, o, o * 128:512], op=ALU.mult),
                     reads=[pTb[pi], mskb], writes=[pTb[pi]])

    def emit_pv(t):
        j, k2 = steps[t]
        i = j % 2
        pi = t % 3
        full = (2 * k2 + 1 < 4 * j)
        for hf in range(2):
            kb = 2 * k2 + hf
            o = max(0, kb - 4 * j)
            c0 = o * 128
            last = (kb == 4 * j + 3)
            mv = pT[pi][:, hf * 512 + c0:(hf + 1) * 512]
            S.op("pe", lambda e, kb=kb, c0=c0, last=last, mv=mv: e.matmul(B.ps[PS_O][:, c0:512], lhsT=V[:, kb, 0:128], rhs=mv, start=(kb == 0), stop=last, skip_group_check=True),
                 reads=[pTb[pi], Vb], writes=[B.psb[PS_O]])
            if full:
                continue
            if kb == 0:
                S.op("dve", lambda e, mv=mv: e.tensor_copy(out=acc[:, :], in_=mv), reads=[pTb[pi]], writes=[accb])
            else:
                S.op("dve", lambda e, c0=c0, mv=mv: e.tensor_tensor(out=acc[:, c0:512], in0=acc[:, c0:512], in1=mv, op=ALU.add), reads=[pTb[pi], accb], writes=[accb])
        if full:
            if k2 == 0:
                S.op("dve", lambda e: e.tensor_tensor(out=acc[:, :], in0=pT[pi][:, 0:512], in1=pT[pi][:, 512:1024], op=ALU.add), reads=[pTb[pi]], writes=[accb])
            else:
                si = t % 2
                S.op("dve", lambda e, si=si: e.tensor_tensor(out=s2[si][:, :], in0=pT[pi][:, 0:512], in1=pT[pi][:, 512:1024], op=ALU.add), reads=[pTb[pi]], writes=[s2b[si]])
                S.op("dve", lambda e, si=si: e.tensor_tensor(out=acc[:, :], in0=acc[:, :], in1=s2[si][:, :], op=ALU.add), reads=[s2b[si], accb], writes=[accb])
        if k2 == 2 * j + 1:
            q0 = j * 512
            S.op("pe", lambda e: e.matmul(B.ps[PS_SUM][0:1, :], lhsT=onescol[:, 0:1], rhs=acc[:, :], start=True, stop=True), reads=[onescolb, accb], writes=[B.psb[PS_SUM]])
            S.op("dve", lambda e: e.reciprocal(out=rs[:, :], in_=B.ps[PS_SUM][0:1, :]), reads=[B.psb[PS_SUM]], writes=[rsb])
            S.op("pe", lambda e: e.matmul(B.ps[PS_SUM][:, :], lhsT=onesf[:, :], rhs=rs[:, :], start=True, stop=True), reads=[onesfb, rsb], writes=[B.psb[PS_SUM]])
            S.op("act", lambda e: e.copy(out=acc[:, :], in_=B.ps[PS_SUM][:, :]), reads=[B.psb[PS_SUM]], writes=[accb])
            S.op("dve", lambda e: e.tensor_tensor(out=yT[i][:, :], in0=B.ps[PS_O][:, :], in1=acc[:, :], op=ALU.mult), reads=[B.psb[PS_O], accb], writes=[yTb[i]])
            S.dma("sp", P["ybT"][:, q0:q0 + 512], yT[i][:, :], reads=[yTb[i]])

    emit_st(0)
    for t in range(len(steps)):
        if t + 1 < len(steps):
            emit_st(t + 1)
        emit_pv(t)
        if t % 2 == 1:
            yield
    yield


def drive(gens, weights):
    alive = list(gens)
    w = list(weights)
    while alive:
        for g, k in list(zip(alive, w)):
            for _ in range(k):
                try:
                    next(g)
                except StopIteration:
                    idx = alive.index(g)
                    alive.pop(idx); w.pop(idx)
                    break

from concourse.bass_utils import run_bass_kernel_spmd
import ml_dtypes

SEQ = 16384
NCORE = 8
TC = SEQ // NCORE
TT_TILE = 1024

A_OUTS = dict(x1T=([D, None], F32), qaT=([512, None], BF16), kaT=([512, None], BF16), ka=([None, 512], BF16), va=([None, 1024], BF16),
              gif=([None, 8], F32), qbnT=([1024, None], BF16), qbr1T=([256, None], BF16), qbr2T=([256, None], BF16), kbnT=([1024, None], BF16),
              kbr1T=([32, None], BF16), kbr2T=([32, None], BF16), vb=([None, 1024], BF16), cxT=([1024, None], F32))


def _declare_proj(nc, P, Tc, sfx=""):
    def inp(name, shape, dt=F32):
        P[name] = nc.dram_tensor(name, shape, dt, kind="ExternalInput").ap()
    inp("gm" + sfx, [D]); inp("w_in" + sfx, [D, N_IN]); inp("gq" + sfx, [384]); inp("gkv" + sfx, [256])
    inp("w_uq" + sfx, [384, 1536]); inp("w_ukv" + sfx, [256, 2048])
    if "cs" not in P:
        inp("cs", [32, 2, Tc])
    for k, (shp, dt) in A_OUTS.items():
        P[k + sfx] = nc.dram_tensor(k + sfx, [Tc if s is None else s for s in shp], dt, kind="ExternalOutput").ap()


def _proj_P(P, sfx):
    Q = {k: P[k + sfx] for k in A_OUTS}
    Q.update(w_in=P["w_in" + sfx], w_uq=P["w_uq" + sfx], w_ukv=P["w_ukv" + sfx], cs=P["cs"])
    return Q


def build_prog(kind, Tc=TC, T=TT_TILE):
    nc = bass.Bass("TRN2", target_bir_lowering=False)
    P = {}

    def inp(name, shape, dt=F32):
        P[name] = nc.dram_tensor(name, shape, dt, kind="ExternalInput").ap()

    def ffn_in(sfx):
        inp("g" + sfx, [D]); inp("wg" + sfx, [D, DFF]); inp("wu" + sfx, [D, DFF]); inp("wd" + sfx, [DFF, D])

    if kind == "A":
        inp("xT", [D, Tc]); ffn_in("1"); _declare_proj(nc, P, Tc)
    else:
        inp("x1T_in", [D, Tc]); inp("haT", [1024, Tc]); inp("ybT", [1024, Tc], BF16); inp("ycT", [1024, Tc], BF16)
        inp("gm_m", [D]); inp("w_in_m", [D, N_IN]); inp("gon", [1024]); inp("w_branch", [3, 1024, D]); inp("w_out", [D, D])
        ffn_in("2")
        if kind == "C":
            ffn_in("1"); _declare_proj(nc, P, Tc)
        else:
            inp("gfin", [D])
            P["outT"] = nc.dram_tensor("outT", [D, Tc], F32, kind="ExternalOutput").ap()
    with contextlib.ExitStack() as stack:
        S = Sched(nc, stack)
        R = Res(S, T)
        if kind in ("A", "C"):
            A = ResA(S, R)
            g1_t, g1_b = load_vec_pp(S, "g1", P["g1"], KC)
            gm_t, gm_b = load_vec_pp(S, "gm", P["gm"], KC)
            A.gq, A.gqb = load_vec_pp(S, "gq", P["gq"], 3, scale=192 ** -0.5)
            A.gkv, A.gkvb = load_vec_pp(S, "gkv", P["gkv"], 2)
        if kind in ("C", "D"):
            g2_t, g2_b = load_vec_pp(S, "g2", P["g2"], KC)
            gmm_t, gmm_b = load_vec_pp(S, "gm_m", P["gm_m"], KC)
            gon_t, gon_b = load_vec_pp(S, "gon", P["gon"], 8)
        if kind == "D":
            gf_t, gf_b = load_vec_pp(S, "gfin", P["gfin"], KC)
        for t0 in range(0, Tc, T):
            src = P["xT"] if kind == "A" else P["x1T_in"]
            for d in range(KC):
                S.dma("sp", R.x[:, d, :], src[d * 128:(d + 1) * 128, t0:t0 + T], writes=[R.xb[d]])
            if kind in ("C", "D"):
                PM = dict(w_in=P["w_in_m"], w_branch=P["w_branch"], w_out=P["w_out"], x1T=P["x1T_in"], haT=P["haT"], ybT=P["ybT"], ycT=P["ycT"])
                emit_merge(S, R, PM, t0, gmm_t, gmm_b, gon_t, gon_b)
                emit_rmsnorm_fm(S, R, g2_t, g2_b)
                emit_ffn(S, R, P["wg2"], P["wu2"], P["wd2"])
            if kind in ("A", "C"):
                emit_rmsnorm_fm(S, R, g1_t, g1_b)
                emit_ffn(S, R, P["wg1"], P["wu1"], P["wd1"])
                for d in range(KC):
                    S.dma("sp", P["x1T"][d * 128:(d + 1) * 128, t0:t0 + T], R.x[:, d, :], reads=[R.xb[d]])
                emit_rmsnorm_fm(S, R, gm_t, gm_b)
                emit_phase_a_proj(S, R, A, _proj_P(P, ""), t0)
            else:
                xs = [R.x[:, d, :] for d in range(KC)]
                emit_rmsnorm_fm(S, R, gf_t, gf_b, src=xs, srcb=R.xb, dst=xs, dstb=R.xb)
                for d in range(KC):
                    S.dma("sp", P["outT"][d * 128:(d + 1) * 128, t0:t0 + T], R.x[:, d, :], reads=[R.xb[d]])
        info = S.emit()
    return nc, info


def build_prog_b(seq=SEQ):
    nc = bass.Bass("TRN2", target_bir_lowering=False)
    P = {}

    def inp(name, shape, dt=F32):
        P[name] = nc.dram_tensor(name, shape, dt, kind="ExternalInput").ap()

    def out(name, shape, dt=F32):
        P[name] = nc.dram_tensor(name, shape, dt, kind="ExternalOutput").ap()
    inp("cxT", [128, seq]); inp("lru_prm", [128, 8]); inp("lru_wa", [128, 128]); inp("lru_wx", [128, 128]); out("ycT", [128, seq], BF16)
    inp("qaT", [128, seq], BF16); inp("kaT", [128, seq], BF16); inp("ka", [seq, 128], BF16); inp("va", [seq, 128], BF16); inp("gif", [seq, 2])
    inp("gbias", [128, 2]); inp("tri", [128, 128]); out("ha", [seq, 128])
    inp("qbT", [128, seq], BF16); inp("qrT", [64, seq], BF16); inp("kbT", [128, seq], BF16); inp("krT", [64, seq], BF16); inp("vb", [seq, 128], BF16)
    inp("masks", [4, 128, 512], BF16); out("ybT", [128, seq], BF16)
    with contextlib.ExitStack() as stack:
        S = Sched(nc, stack)
        B = ResB(S, seq)
        drive([gen_lru(S, B, P), gen_mlstm(S, B, P), gen_mla(S, B, P)], [1, 2, 9])
        info = S.emit()
    return nc, info


def _rope_tables(pos):
    inv = np.power(np.float32(10000.0), -np.arange(0, 64, 2, dtype=np.float32) / np.float32(64)).astype(np.float32)
    ang = (pos.astype(np.float32)[:, None] * inv[None, :]).astype(np.float32)
    return np.ascontiguousarray(np.stack([np.cos(ang).astype(np.float32).T, np.sin(ang).astype(np.float32).T], axis=1))


def _consts():
    tri = (np.arange(128)[:, None] <= np.arange(128)[None, :]).astype(np.float32)
    k = np.arange(128)[:, None]; q = np.arange(512)[None, :]
    masks = np.stack([(q >= k + 128 * o) for o in range(4)]).astype(np.float32).astype(ml_dtypes.bfloat16)
    return tri, masks


_PROGS = {}


def _prog(kind):
    if kind not in _PROGS:
        _PROGS[kind] = build_prog_b()[0] if kind == "B" else build_prog(kind)[0]
    return _PROGS[kind]


def _run(kind, in_maps):
    res = run_bass_kernel_spmd(_prog(kind), in_maps, core_ids=list(range(NCORE)))
    return res.results


def _ffn_w(inp, l, which, sfx):
    if which == 1:
        g, wg, wu, wd = inp["ffn1_norm"], inp["ffn1_w_gate"], inp["ffn1_w_up"], inp["ffn1_w_down"]
    else:
        g, wg, wu, wd = inp["ffn2_norm"], inp["ffn2_w_gate"], inp["ffn2_w_up"], inp["ffn2_w_down"]
    return {"g" + sfx: g[l], "wg" + sfx: wg[l], "wu" + sfx: wu[l], "wd" + sfx: wd[l]}


def _proj_w(inp, l):
    return {"gm": inp["mix_norm"][l], "w_in": inp["w_in"][l], "gq": inp["mla_q_norm"][l], "gkv": inp["mla_kv_norm"][l],
            "w_uq": inp["mla_w_uq"][l], "w_ukv": inp["mla_w_ukv"][l]}


def _mixer_maps(inp, l, ra):
    tri, masks = _consts()
    cat1 = lambda k: np.concatenate([r[k] for r in ra], axis=1)
    cat0 = lambda k: np.concatenate([r[k] for r in ra], axis=0)
    qaT, kaT, ka, va, gif = cat1("qaT"), cat1("kaT"), cat0("ka"), cat0("va"), cat0("gif")
    qbnT, qbr1T, qbr2T, kbnT, kbr1T, kbr2T, vb, cxT = cat1("qbnT"), cat1("qbr1T"), cat1("qbr2T"), cat1("kbnT"), cat1("kbr1T"), cat1("kbr2T"), cat0("vb"), cat1("cxT")
    krT = np.ascontiguousarray(np.concatenate([kbr1T, kbr2T], axis=0))
    gb = inp["mlstm_gate_bias"][l]
    maps = []
    for c in range(NCORE):
        h, half = c // 2, c % 2
        ch = slice(c * 128, (c + 1) * 128)
        prm = np.concatenate([inp["lru_conv_w"][l][:, ch].T, inp["lru_conv_b"][l][ch, None], inp["lru_b_a"][l][ch, None],
                              inp["lru_b_x"][l][ch, None], inp["lru_lambda"][l][ch, None]], axis=1)
        m = dict(
            cxT=np.ascontiguousarray(cxT[ch]), lru_prm=np.ascontiguousarray(prm.astype(np.float32)),
            lru_wa=np.ascontiguousarray(inp["lru_w_a"][l][c]), lru_wx=np.ascontiguousarray(inp["lru_w_x"][l][c]),
            qaT=np.ascontiguousarray(qaT[h * 128:(h + 1) * 128]), kaT=np.ascontiguousarray(kaT[h * 128:(h + 1) * 128]),
            ka=np.ascontiguousarray(ka[:, h * 128:(h + 1) * 128]), va=np.ascontiguousarray(va[:, h * 256 + half * 128:h * 256 + (half + 1) * 128]),
            gif=np.ascontiguousarray(gif[:, [h, 4 + h]]), gbias=np.ascontiguousarray(np.tile(gb[[h, 4 + h]][None, :], (128, 1)).astype(np.float32)),
            tri=tri, masks=masks,
            qbT=np.ascontiguousarray(qbnT[ch]), qrT=np.ascontiguousarray(np.concatenate([qbr1T[c * 32:(c + 1) * 32], qbr2T[c * 32:(c + 1) * 32]], axis=0)),
            kbT=np.ascontiguousarray(kbnT[ch]), krT=krT, vb=np.ascontiguousarray(vb[:, ch]))
        maps.append(m)
    return maps


def _merge_maps(inp, l, ra, rb):
    haT = np.empty((1024, SEQ), np.float32)
    ybT = np.empty((1024, SEQ), ml_dtypes.bfloat16)
    ycT = np.empty((1024, SEQ), ml_dtypes.bfloat16)
    for c in range(NCORE):
        h, half = c // 2, c % 2
        haT[h * 256 + half * 128:h * 256 + (half + 1) * 128] = np.asarray(rb[c]["ha"]).T
        ybT[c * 128:(c + 1) * 128] = np.asarray(rb[c]["ybT"])
        ycT[c * 128:(c + 1) * 128] = np.asarray(rb[c]["ycT"])
    maps = []
    for c in range(NCORE):
        tk = slice(c * TC, (c + 1) * TC)
        m = dict(x1T_in=np.asarray(ra[c]["x1T"]), haT=np.ascontiguousarray(haT[:, tk]), ybT=np.ascontiguousarray(ybT[:, tk]), ycT=np.ascontiguousarray(ycT[:, tk]),
                 gm_m=inp["mix_norm"][l], w_in_m=inp["w_in"][l], gon=inp["mlstm_out_norm"][l], w_branch=inp["w_branch"][l], w_out=inp["w_out"][l])
        m.update(_ffn_w(inp, l, 2, "2"))
        maps.append(m)
    return maps


def kernel(**inputs):
    inp = {k: np.asarray(v) for k, v in inputs.items()}
    x = inp["x"][0]
    cs = [_rope_tables(np.arange(c * TC, (c + 1) * TC)) for c in range(NCORE)]
    maps = []
    for c in range(NCORE):
        m = dict(xT=np.ascontiguousarray(x[c * TC:(c + 1) * TC].T), cs=cs[c])
        m.update(_ffn_w(inp, 0, 1, "1")); m.update(_proj_w(inp, 0))
        maps.append(m)
    ra = _run("A", maps)
    rb = _run("B", _mixer_maps(inp, 0, ra))
    maps = _merge_maps(inp, 0, ra, rb)
    for c in range(NCORE):
        maps[c].update(_ffn_w(inp, 1, 1, "1")); maps[c].update(_proj_w(inp, 1)); maps[c]["cs"] = cs[c]
    ra = _run("C", maps)
    rb = _run("B", _mixer_maps(inp, 1, ra))
    maps = _merge_maps(inp, 1, ra, rb)
    for c in range(NCORE):
        maps[c]["gfin"] = inp["final_norm"]
    rd = _run("D", maps)
    out = np.concatenate([np.asarray(rd[c]["outT"]).T for c in range(NCORE)], axis=0)
    return np.ascontiguousarray(out[None].astype(np.float32))
```

```python
import contextlib
import numpy as np
import concourse.bass as bass
import concourse.mybir as mybir

F32 = mybir.dt.float32
BF16 = mybir.dt.bfloat16
AF = mybir.ActivationFunctionType
ALU = mybir.AluOpType
AX = mybir.AxisListType


class Buf:
    __slots__ = ("name", "w", "r", "rd", "excl")

    def __init__(self, name="", excl=False):
        self.name = name
        self.excl = excl
        self.w = None
        self.r = {}
        self.rd = []


class Op:
    __slots__ = ("eng", "fn", "deps", "is_dma", "key", "cnt", "marked", "raw")


class Sched:
    ENG = ("pe", "act", "dve", "pool", "sp")

    def __init__(self, nc, stack, same_engine_sync=True):
        self.nc = nc
        self.stack = stack
        self.ops = []
        self.engs = {"pe": nc.tensor, "act": nc.scalar, "dve": nc.vector,
                     "pool": nc.gpsimd, "sp": nc.sync}
        self.same_engine_sync = same_engine_sync
        self.dma_keys = {}
        self.nsb = 0

    def sbuf(self, name, shape, dtype):
        return self.stack.enter_context(self.nc.sbuf_tensor("sb_" + name, list(shape), dtype))

    def psum(self, name, shape, dtype=F32):
        return self.stack.enter_context(self.nc.psum_tensor("pp_" + name, list(shape), dtype))

    def op(self, eng, fn, reads=(), writes=(), dma=False):
        idx = len(self.ops)
        deps = {}
        if any(b.excl for b in reads):
            writes = tuple(writes) + tuple(b for b in reads if b.excl)
            reads = tuple(b for b in reads if not b.excl)
        for b in reads:
            if b.w is not None:
                deps[b.w] = True
        for b in writes:
            if b.w is not None:
                deps[b.w] = True
            for i in b.r.values():
                deps.setdefault(i, False)
            for i in b.rd:
                deps.setdefault(i, False)
        deps.pop(idx, None)
        o = Op()
        o.eng = eng; o.fn = fn; o.deps = deps; o.is_dma = dma; o.key = None
        o.cnt = 0; o.marked = False
        self.ops.append(o)
        for b in reads:
            if dma:
                b.rd.append(idx)
            else:
                b.r[eng] = idx
        for b in writes:
            b.w = idx; b.r = {}; b.rd = []
        return idx

    def dma(self, eng, out, in_, reads=(), writes=(), key=None, slow=False):
        if key is None:
            key = writes[0] if writes else reads[0]
        kk = (id(key), eng)
        kb = self.dma_keys.get(kk)
        if kb is None:
            kb = [Buf("k_" + key.name), None, 0, key]
            self.dma_keys[kk] = kb
        idx = self.op(eng, (lambda e: e.dma_start(out=out, in_=in_, allow_slow_non_contiguous=True)) if slow else (lambda e: e.dma_start(out=out, in_=in_)), reads=reads,
                      writes=tuple(writes) + (kb[0],), dma=True)
        self.ops[idx].key = kb
        return idx

    def _needs_wait(self, o, d, raw):
        od = self.ops[d]
        if od.is_dma:
            return True
        if od.eng != o.eng:
            return True
        if o.is_dma:
            return False
        if self.same_engine_sync and o.eng != "pe":
            return True
        return False

    def emit(self):
        nc = self.nc
        ops = self.ops
        for o in ops:
            for d, raw in o.deps.items():
                od = ops[d]
                need = self._needs_wait(o, d, raw)
                if o.is_dma and (not od.is_dma) and od.eng == o.eng:
                    need = True
                if need and not od.is_dma:
                    od.marked = True
        esem = {e: self.stack.enter_context(nc.semaphore("sem_" + e)) for e in self.ENG}
        ecnt = {e: 0 for e in self.ENG}
        for kb in self.dma_keys.values():
            kb[1] = self.stack.enter_context(nc.semaphore("semd%d" % len([1 for k in self.dma_keys.values() if k[1] is not None])))
        for o in ops:
            if o.is_dma:
                o.key[2] += 16
                o.cnt = o.key[2]
            elif o.marked:
                ecnt[o.eng] += 1
                o.cnt = ecnt[o.eng]
        seen = {e: {} for e in self.ENG}
        nwait = 0
        for o in ops:
            e = self.engs[o.eng]
            sn = seen[o.eng]
            waits = {}
            for d, raw in o.deps.items():
                od = ops[d]
                need = self._needs_wait(o, d, raw)
                if o.is_dma and (not od.is_dma) and od.eng == o.eng:
                    need = True
                if not need:
                    continue
                if od.is_dma:
                    sem = od.key[1]
                else:
                    sem = esem[od.eng]
                k = id(sem)
                if sn.get(k, 0) >= od.cnt:
                    continue
                if k not in waits or waits[k][1] < od.cnt:
                    waits[k] = (sem, od.cnt)
            for k, (sem, val) in waits.items():
                e.wait_ge(sem, val)
                sn[k] = val
                nwait += 1
            ins = o.fn(e)
            if o.is_dma:
                ins.then_inc(o.key[1], 16)
            elif o.marked:
                ins.then_inc(esem[o.eng], 1)
        sp = self.engs["sp"]
        for kb in self.dma_keys.values():
            if kb[2] > 0:
                sp.wait_ge(kb[1], kb[2])
        return dict(n_ops=len(ops), n_wait=nwait, counts=ecnt, n_dma_keys=len(self.dma_keys))

import math
import numpy as np

D = 2048
DFF = 5632
KC = D // 128
FC = DFF // 128
EPS = 1e-6


class Res:
    def __init__(self, S, T):
        self.S = S
        self.T = T
        self.TT = T // 512
        nc = S.nc
        self.x = S.sbuf("x", [128, KC, T], F32)
        self.xb = [Buf("x%d" % d) for d in range(KC)]
        self.u = S.sbuf("u", [128, KC, T], BF16)
        self.ub = [Buf("u%d" % d) for d in range(KC)]
        self.sq = [S.sbuf("sq%d" % i, [128, 512], F32) for i in range(2)]
        self.sqb = [Buf("sq%d" % i) for i in range(2)]
        self.rstd = S.sbuf("rstd", [128, T], F32)
        self.rstdb = Buf("rstd")
        self.ps = [S.psum("ps%d" % i, [128, 512]) for i in range(8)]
        self.psb = [Buf("ps%d" % i, excl=True) for i in range(8)]
        self.ones = S.sbuf("ones", [128, 128], F32)
        self.onesb = Buf("ones")
        S.op("pool", lambda e: e.memset(self.ones[:], 1.0), writes=[self.onesb])
        self.wgu = [[S.sbuf("wgu%d_%d" % (i, j), [128, KC, 256], BF16) for j in range(2)] for i in range(2)]
        self.wgub = [[Buf("wgu%d_%d" % (i, j)) for j in range(2)] for i in range(2)]
        self.wd = [S.sbuf("wd%d" % i, [128, 2, D], BF16) for i in range(2)]
        self.wdb = [Buf("wd%d" % i) for i in range(2)]
        self.h = [S.sbuf("h%d" % i, [128, 4, T], BF16) for i in range(2)]
        self.hb = [[Buf("h%d_%d" % (i, j)) for j in range(4)] for i in range(2)]
        self.sil = [S.sbuf("sil%d" % i, [128, 512], F32) for i in range(2)]
        self.silb = [Buf("sil%d" % i) for i in range(2)]
        self.cnt = {}

    def rot(self, name, n):
        v = self.cnt.get(name, 0)
        self.cnt[name] = v + 1
        return v % n


def load_vec_pp(S, name, dram_vec_ap, nchunks, scale=None):
    t = S.sbuf(name + "_sb", [128, nchunks], F32)
    b = Buf(name)
    S.dma("sp", t[:], dram_vec_ap.rearrange("(k p) -> p k", p=128), writes=[b], slow=True)
    if scale is not None:
        S.op("dve", lambda e: e.tensor_scalar(out=t[:], in0=t[:], scalar1=float(scale), scalar2=None, op0=ALU.mult),
             reads=[b], writes=[b])
    return t, b


def emit_rmsnorm_fm(S, R, g_t, g_b, src=None, srcb=None, dst=None, dstb=None, ps_id=6):
    T, TT = R.T, R.TT
    if src is None:
        src = [R.x[:, d, :] for d in range(KC)]; srcb = R.xb
    if dst is None:
        dst = [R.u[:, d, :] for d in range(KC)]; dstb = R.ub
    nchunks = len(src)
    Dn = nchunks * 128
    for tt in range(TT):
        sl = slice(tt * 512, (tt + 1) * 512)
        p = ps_id + (tt % 2)
        for d in range(nchunks):
            i = R.rot("sq", 2)
            S.op("act", lambda e, d=d, i=i, sl=sl: e.activation(out=R.sq[i][:, :], in_=src[d][:, sl], func=AF.Square),
                 reads=[srcb[d]], writes=[R.sqb[i]])
            S.op("pe", lambda e, d=d, i=i, p=p: e.matmul(R.ps[p][:, :], lhsT=R.ones[:, :], rhs=R.sq[i][:, :],
                                                         start=(d == 0), stop=(d == nchunks - 1)),
                 reads=[R.onesb, R.sqb[i]], writes=[R.psb[p]])
        S.op("dve", lambda e, sl=sl, p=p: e.tensor_scalar(out=R.rstd[:, sl], in0=R.ps[p][:, :],
                                                           scalar1=float(1.0 / Dn), scalar2=float(EPS), op0=ALU.mult, op1=ALU.add),
             reads=[R.psb[p]], writes=[R.rstdb])
        S.op("act", lambda e, sl=sl: e.activation(out=R.rstd[:, sl], in_=R.rstd[:, sl], func=AF.Sqrt),
             reads=[R.rstdb], writes=[R.rstdb])
        S.op("dve", lambda e, sl=sl: e.reciprocal(out=R.rstd[:, sl], in_=R.rstd[:, sl]),
             reads=[R.rstdb], writes=[R.rstdb])
    for d in range(nchunks):
        S.op("dve", lambda e, d=d: e.scalar_tensor_tensor(out=dst[d], in0=src[d], scalar=g_t[:, d:d + 1], in1=R.rstd[:, :],
                                                          op0=ALU.mult, op1=ALU.mult),
             reads=[srcb[d], g_b, R.rstdb], writes=[dstb[d]])


def emit_ffn(S, R, wg, wu, wd):
    T, TT = R.T, R.TT
    NG = FC // 4
    wg_v = wg.rearrange("(k p) f -> p k f", p=128)
    wu_v = wu.rearrange("(k p) f -> p k f", p=128)
    wd_v = wd.rearrange("(c p) d -> p c d", p=128)
    for g in range(NG):
        hi_ = R.rot("h", 2)
        for half in range(2):
            wi = R.rot("wgu", 2)
            c0 = g * 512 + half * 256
            S.dma("pool", R.wgu[wi][0][:], wg_v[:, :, c0:c0 + 256], writes=[R.wgub[wi][0]])
            S.dma("pool", R.wgu[wi][1][:], wu_v[:, :, c0:c0 + 256], writes=[R.wgub[wi][1]])
            for c2 in range(2):
                fc = half * 2 + c2
                for tt in range(TT):
                    pg = R.rot("psg", 2)
                    pu = 2 + pg
                    for k in range(KC):
                        S.op("pe", lambda e, k=k, wi=wi, c2=c2, tt=tt, pg=pg: e.matmul(
                            R.ps[pg][:, :], lhsT=R.wgu[wi][0][:, k, c2 * 128:(c2 + 1) * 128], rhs=R.u[:, k, tt * 512:(tt + 1) * 512],
                            start=(k == 0), stop=(k == KC - 1)),
                            reads=[R.wgub[wi][0], R.ub[k]], writes=[R.psb[pg]])
                    for k in range(KC):
                        S.op("pe", lambda e, k=k, wi=wi, c2=c2, tt=tt, pu=pu: e.matmul(
                            R.ps[pu][:, :], lhsT=R.wgu[wi][1][:, k, c2 * 128:(c2 + 1) * 128], rhs=R.u[:, k, tt * 512:(tt + 1) * 512],
                            start=(k == 0), stop=(k == KC - 1)),
                            reads=[R.wgub[wi][1], R.ub[k]], writes=[R.psb[pu]])
                    si = R.rot("sil", 2)
                    S.op("act", lambda e, si=si, pg=pg: e.activation(out=R.sil[si][:, :], in_=R.ps[pg][:, :], func=AF.Silu),
                         reads=[R.psb[pg]], writes=[R.silb[si]])
                    S.op("dve", lambda e, si=si, pu=pu, hi_=hi_, fc=fc, tt=tt: e.tensor_tensor(
                        out=R.h[hi_][:, fc, tt * 512:(tt + 1) * 512], in0=R.ps[pu][:, :], in1=R.sil[si][:, :], op=ALU.mult),
                        reads=[R.psb[pu], R.silb[si]], writes=[R.hb[hi_][fc]])
        for half in range(2):
            r0 = g * 4 + half * 2
            S.dma("pool", R.wd[half][:], wd_v[:, r0:r0 + 2, :], writes=[R.wdb[half]])
        for d in range(KC):
            for tt in range(TT):
                pd = 4 + R.rot("psd", 2)
                for fc in range(4):
                    S.op("pe", lambda e, d=d, tt=tt, pd=pd, fc=fc, hi_=hi_: e.matmul(
                        R.ps[pd][:, :], lhsT=R.wd[fc // 2][:, fc % 2, d * 128:(d + 1) * 128], rhs=R.h[hi_][:, fc, tt * 512:(tt + 1) * 512],
                        start=(fc == 0), stop=(fc == 3)),
                        reads=[R.wdb[fc // 2], R.hb[hi_][fc]], writes=[R.psb[pd]])
                S.op("dve", lambda e, d=d, tt=tt, pd=pd: e.scalar_tensor_tensor(
                    out=R.x[:, d, tt * 512:(tt + 1) * 512], in0=R.ps[pd][:, :], scalar=0.5, in1=R.x[:, d, tt * 512:(tt + 1) * 512],
                    op0=ALU.mult, op1=ALU.add),
                    reads=[R.psb[pd], R.xb[d]], writes=[R.xb[d]])


O_AQ, O_AK, O_AV, O_AI, O_AF, O_AO = 0, 512, 1024, 2048, 2052, 2056
O_BCQ, O_BCKV, O_BKR, O_CX, O_G = 3080, 3464, 3720, 3784, 4808
N_IN = 10952


class ResA:
    def __init__(self, S, R):
        T = R.T
        self.wt = [R.wgu[0][0], R.wgu[0][1], R.wgu[1][0], R.wgu[1][1]]
        self.wtb = [R.wgub[0][0], R.wgub[0][1], R.wgub[1][0], R.wgub[1][1]]
        self.stf = R.sil
        self.stfb = R.silb
        self.stb = [S.sbuf("stb%d" % i, [128, 512], BF16) for i in range(3)]
        self.stbb = [Buf("stb%d" % i) for i in range(3)]
        self.stm = [R.h[0][:, 0:2, :].rearrange("p a (b c) -> p (a b) c", c=256), R.h[0][:, 2:4, :].rearrange("p a (b c) -> p (a b) c", c=256)]
        self.stmb = [[R.hb[0][0], R.hb[0][1]], [R.hb[0][2], R.hb[0][3]]]
        self.gst = S.sbuf("gst", [128, T // 128, 8], F32)
        self.gstb = Buf("gst")
        lat4 = S.sbuf("lat4", [128, T], F32)
        latn4 = S.sbuf("latn4", [128, T], BF16)
        self.lat = [R.wd[0][:, 0, :].bitcast(F32), R.wd[0][:, 1, :].bitcast(F32), R.wd[1][:, 0, :].bitcast(F32), R.wd[1][:, 1, :].bitcast(F32), lat4[:, :]]
        self.latb = [R.wdb[0], R.wdb[0], R.wdb[1], R.wdb[1], Buf("lat4")]
        self.latn = [R.h[1][:, 0, :], R.h[1][:, 1, :], R.h[1][:, 2, :], R.h[1][:, 3, :], latn4[:, :]]
        self.latnb = [R.hb[1][0], R.hb[1][1], R.hb[1][2], R.hb[1][3], Buf("latn4")]
        self.cs = S.sbuf("cs_sb", [32, 2, 512], F32)
        self.csb = Buf("cs")
        rt2 = [S.sbuf("rt%d" % i, [32, 512], F32) for i in range(2)]
        self.rt = [R.sq[0][:32, :], R.sq[1][:32, :], rt2[0][:, :], rt2[1][:, :]]
        self.rtb = [R.sqb[0], R.sqb[1], Buf("rt2"), Buf("rt3")]


def emit_proj_fm(S, R, A, w_in, col0, ncols, evac, msz=128):
    w_v = w_in.rearrange("(k p) f -> p k f", p=128)
    c = 0
    ci = 0
    while c < ncols:
        wcols = min(256, ncols - c)
        wi = R.rot("wt", 4)
        S.dma("pool", A.wt[wi][:, :, :wcols], w_v[:, :, col0 + c:col0 + c + wcols], writes=[A.wtb[wi]])
        cc = 0
        while cc < wcols:
            m = min(msz, wcols - cc)
            for tt in range(R.TT):
                p = R.rot("psA", 4)
                for k in range(KC):
                    S.op("pe", lambda e, k=k, wi=wi, cc=cc, m=m, tt=tt, p=p: e.matmul(
                        R.ps[p][:m, :], lhsT=A.wt[wi][:, k, cc:cc + m], rhs=R.u[:, k, tt * 512:(tt + 1) * 512],
                        start=(k == 0), stop=(k == KC - 1)),
                        reads=[A.wtb[wi], R.ub[k]], writes=[R.psb[p]])
                evac(R.ps[p], R.psb[p], ci, tt, m)
            cc += m
            ci += 1
        c += wcols


def emit_proj_tm(S, R, A, w_in, col0, ncols, out_dram, t0):
    w_v = w_in.rearrange("(k p) f -> p k f", p=128)
    NB = R.T // 128
    for c in range(0, ncols, 256):
        wi = R.rot("wt", 4)
        S.dma("pool", A.wt[wi][:, :, :], w_v[:, :, col0 + c:col0 + c + 256], writes=[A.wtb[wi]])
        si = R.rot("stm", 2)
        for b in range(NB):
            p = R.rot("psA", 4)
            for k in range(KC):
                S.op("pe", lambda e, k=k, wi=wi, b=b, p=p: e.matmul(
                    R.ps[p][:, :256], lhsT=R.u[:, k, b * 128:(b + 1) * 128], rhs=A.wt[wi][:, k, :],
                    start=(k == 0), stop=(k == KC - 1)),
                    reads=[A.wtb[wi], R.ub[k]], writes=[R.psb[p]])
            if b % 2 == 0:
                S.op("act", lambda e, b=b, p=p, si=si: e.copy(out=A.stm[si][:, b, :], in_=R.ps[p][:, :256]),
                     reads=[R.psb[p]], writes=A.stmb[si])
            else:
                S.op("dve", lambda e, b=b, p=p, si=si: e.tensor_copy(out=A.stm[si][:, b, :], in_=R.ps[p][:, :256]),
                     reads=[R.psb[p]], writes=A.stmb[si])
        S.dma("sp", out_dram[t0:t0 + R.T, c:c + 256].rearrange("(b p) c -> p b c", p=128), A.stm[si],
              reads=A.stmb[si], key=A.stmb[si][0])


def emit_phase_a_proj(S, R, A, P, t0):
    T, TT = R.T, R.TT
    w_in = P["w_in"]
    w_v = w_in.rearrange("(k p) f -> p k f", p=128)

    def evac_bf16_out(dram, scale=None):
        def f(ps, psb, ci, tt, m):
            si = R.rot("stb", 3)
            if scale is None:
                S.op("act", lambda e: e.copy(out=A.stb[si][:m, :], in_=ps[:m, :]), reads=[psb], writes=[A.stbb[si]])
            else:
                S.op("act", lambda e: e.mul(out=A.stb[si][:m, :], in_=ps[:m, :], mul=float(scale)), reads=[psb], writes=[A.stbb[si]])
            S.dma("sp", dram[ci * 128:ci * 128 + m, t0 + tt * 512:t0 + (tt + 1) * 512], A.stb[si][:m, :], reads=[A.stbb[si]])
        return f

    emit_proj_fm(S, R, A, w_in, O_AQ, 512, evac_bf16_out(P["qaT"], 128 ** -0.5))
    emit_proj_fm(S, R, A, w_in, O_AK, 512, evac_bf16_out(P["kaT"]))
    emit_proj_tm(S, R, A, w_in, O_AK, 512, P["ka"], t0)
    emit_proj_tm(S, R, A, w_in, O_AV, 1024, P["va"], t0)
    wi = R.rot("wt", 4)
    S.dma("pool", A.wt[wi][:, :, :8], w_v[:, :, O_AI:O_AI + 8], writes=[A.wtb[wi]], slow=True)
    for b in range(T // 128):
        p = R.rot("psA", 4)
        for k in range(KC):
            S.op("pe", lambda e, k=k, wi=wi, b=b, p=p: e.matmul(
                R.ps[p][:, :8], lhsT=R.u[:, k, b * 128:(b + 1) * 128], rhs=A.wt[wi][:, k, :8],
                start=(k == 0), stop=(k == KC - 1)),
                reads=[A.wtb[wi], R.ub[k]], writes=[R.psb[p]])
        S.op("dve", lambda e, b=b, p=p: e.tensor_copy(out=A.gst[:, b, :], in_=R.ps[p][:, :8]), reads=[R.psb[p]], writes=[A.gstb])
    S.dma("sp", P["gif"][t0:t0 + T, :].rearrange("(b p) c -> p b c", p=128), A.gst[:, :, :], reads=[A.gstb], slow=True)

    def evac_cx(ps, psb, ci, tt, m):
        si = R.rot("stf", 2)
        S.op("act", lambda e: e.copy(out=A.stf[si][:m, :], in_=ps[:m, :]), reads=[psb], writes=[A.stfb[si]])
        S.dma("sp", P["cxT"][ci * 128:ci * 128 + m, t0 + tt * 512:t0 + (tt + 1) * 512], A.stf[si][:m, :], reads=[A.stfb[si]])
    emit_proj_fm(S, R, A, w_in, O_CX, 1024, evac_cx)

    def evac_lat(ps, psb, ci, tt, m):
        S.op("act", lambda e: e.copy(out=A.lat[ci][:, tt * 512:(tt + 1) * 512], in_=ps[:, :]), reads=[psb], writes=[A.latb[ci]])
    emit_proj_fm(S, R, A, w_in, O_BCQ, 640, evac_lat)
    emit_rmsnorm_fm(S, R, A.gq, A.gqb, src=A.lat[0:3], srcb=A.latb[0:3], dst=A.latn[0:3], dstb=A.latnb[0:3])
    emit_rmsnorm_fm(S, R, A.gkv, A.gkvb, src=A.lat[3:5], srcb=A.latb[3:5], dst=A.latn[3:5], dstb=A.latnb[3:5])

    def rope(psA, psAb, psB, psBb, tt, dst1, dst2, h):
        S.op("dve", lambda e: e.tensor_tensor(out=A.rt[0][:, :], in0=psA[:32, :], in1=A.cs[:, 0, :], op=ALU.mult), reads=[psAb, A.csb], writes=[A.rtb[0]])
        S.op("dve", lambda e: e.tensor_tensor(out=A.rt[1][:, :], in0=psB[:32, :], in1=A.cs[:, 1, :], op=ALU.mult), reads=[psBb, A.csb], writes=[A.rtb[1]])
        S.op("dve", lambda e: e.tensor_tensor(out=A.rt[2][:, :], in0=psA[:32, :], in1=A.cs[:, 1, :], op=ALU.mult), reads=[psAb, A.csb], writes=[A.rtb[2]])
        S.op("dve", lambda e: e.tensor_tensor(out=A.rt[3][:, :], in0=psB[:32, :], in1=A.cs[:, 0, :], op=ALU.mult), reads=[psBb, A.csb], writes=[A.rtb[3]])
        s1 = R.rot("stb", 3)
        S.op("dve", lambda e: e.tensor_tensor(out=A.stb[s1][:32, :], in0=A.rt[0][:, :], in1=A.rt[1][:, :], op=ALU.subtract), reads=[A.rtb[0], A.rtb[1]], writes=[A.stbb[s1]])
        S.dma("sp", dst1[h * 32:(h + 1) * 32, t0 + tt * 512:t0 + (tt + 1) * 512], A.stb[s1][:32, :], reads=[A.stbb[s1]])
        s2 = R.rot("stb", 3)
        S.op("dve", lambda e: e.tensor_tensor(out=A.stb[s2][:32, :], in0=A.rt[2][:, :], in1=A.rt[3][:, :], op=ALU.add), reads=[A.rtb[2], A.rtb[3]], writes=[A.stbb[s2]])
        S.dma("sp", dst2[h * 32:(h + 1) * 32, t0 + tt * 512:t0 + (tt + 1) * 512], A.stb[s2][:32, :], reads=[A.stbb[s2]])

    wuq_v = P["w_uq"].rearrange("(k p) f -> p k f", p=128)
    wukv_v = P["w_ukv"].rearrange("(k p) f -> p k f", p=128)

    def wt_view(wi, k, c):
        return A.wt[wi].rearrange("p k c -> p (k c)")[:, :k * c].rearrange("p (k c) -> p k c", k=k)

    for tt in range(TT):
        sl = slice(tt * 512, (tt + 1) * 512)
        S.dma("sp", A.cs[:, :, :], P["cs"][:, :, t0 + tt * 512:t0 + (tt + 1) * 512], writes=[A.csb])
        wk = R.rot("wt", 4)
        S.dma("pool", A.wt[wk][:, :, :64], w_v[:, :, O_BKR:O_BKR + 64], writes=[A.wtb[wk]])
        pa = R.rot("psA", 4); pb = R.rot("psA", 4)
        for (pp, c0) in ((pa, 0), (pb, 32)):
            for k in range(KC):
                S.op("pe", lambda e, k=k, pp=pp, c0=c0, sl=sl, wk=wk: e.matmul(
                    R.ps[pp][:32, :], lhsT=A.wt[wk][:, k, c0:c0 + 32], rhs=R.u[:, k, sl],
                    start=(k == 0), stop=(k == KC - 1)), reads=[A.wtb[wk], R.ub[k]], writes=[R.psb[pp]])
        rope(R.ps[pa], R.psb[pa], R.ps[pb], R.psb[pb], tt, P["kbr1T"], P["kbr2T"], 0)
        for hp in range(4):
            wq = R.rot("wt", 4)
            wqv = wt_view(wq, 3, 384)
            S.dma("pool", wqv, wuq_v[:, :, hp * 384:(hp + 1) * 384], writes=[A.wtb[wq]])
            for h2 in range(2):
                h = hp * 2 + h2
                p = R.rot("psA", 4)
                for k in range(3):
                    S.op("pe", lambda e, k=k, p=p, h2=h2, sl=sl, wqv=wqv: e.matmul(R.ps[p][:, :], lhsT=wqv[:, k, h2 * 192:h2 * 192 + 128], rhs=A.latn[k][:, sl],
                                                                     start=(k == 0), stop=(k == 2)), reads=[A.wtb[wq], A.latnb[k]], writes=[R.psb[p]])
                evac_bf16_out(P["qbnT"])(R.ps[p], R.psb[p], h, tt, 128)
                pa = R.rot("psA", 4); pb = R.rot("psA", 4)
                for (pp, c0) in ((pa, 128), (pb, 160)):
                    for k in range(3):
                        S.op("pe", lambda e, k=k, pp=pp, c0=c0, h2=h2, sl=sl, wqv=wqv: e.matmul(R.ps[pp][:32, :], lhsT=wqv[:, k, h2 * 192 + c0:h2 * 192 + c0 + 32], rhs=A.latn[k][:, sl],
                                                                                 start=(k == 0), stop=(k == 2)), reads=[A.wtb[wq], A.latnb[k]], writes=[R.psb[pp]])
                rope(R.ps[pa], R.psb[pa], R.ps[pb], R.psb[pb], tt, P["qbr1T"], P["qbr2T"], h)
    wkv = R.rot("wt", 4)
    wkvv = wt_view(wkv, 2, 2048)
    S.dma("pool", wkvv, wukv_v, writes=[A.wtb[wkv]])
    for h in range(8):
        for tt in range(TT):
            sl = slice(tt * 512, (tt + 1) * 512)
            p = R.rot("psA", 4)
            for k in range(2):
                S.op("pe", lambda e, k=k, p=p, h=h, sl=sl: e.matmul(R.ps[p][:, :], lhsT=wkvv[:, k, h * 256:h * 256 + 128], rhs=A.latn[3 + k][:, sl],
                                                                 start=(k == 0), stop=(k == 1)), reads=[A.wtb[wkv], A.latnb[3 + k]], writes=[R.psb[p]])
            evac_bf16_out(P["kbnT"])(R.ps[p], R.psb[p], h, tt, 128)
    wv = wkvv.rearrange("p k (h two c) -> p k h two c", two=2, c=128)
    for hq in range(4):
        si = R.rot("stm", 2)
        for b in range(T // 128):
            p = R.rot("psA", 4)
            for k in range(2):
                S.op("pe", lambda e, k=k, p=p, b=b, hq=hq: e.matmul(R.ps[p][:, :256], lhsT=A.latn[3 + k][:, b * 128:(b + 1) * 128], rhs=wv[:, k, 2 * hq:2 * hq + 2, 1, :],
                                                                 start=(k == 0), stop=(k == 1)), reads=[A.wtb[wkv], A.latnb[3 + k]], writes=[R.psb[p]])
            S.op("act", lambda e, b=b, p=p, si=si: e.copy(out=A.stm[si][:, b, :], in_=R.ps[p][:, :256]), reads=[R.psb[p]], writes=A.stmb[si])
        S.dma("sp", P["vb"][t0:t0 + T, hq * 256:(hq + 1) * 256].rearrange("(b p) c -> p b c", p=128), A.stm[si], reads=A.stmb[si], key=A.stmb[si][0])


def emit_merge(S, R, P, t0, gm_t, gm_b, gon_t, gon_b):
    T, TT = R.T, R.TT
    w_v = P["w_in"].rearrange("(k p) f -> p k f", p=128)
    wt = [R.wgu[0][0], R.wgu[0][1], R.wgu[1][0], R.wgu[1][1]]
    wtb = [R.wgub[0][0], R.wgub[0][1], R.wgub[1][0], R.wgub[1][1]]
    y = [R.h[0][:, c, :] for c in range(4)] + [R.h[1][:, c, :] for c in range(4)]
    yb = [R.hb[0][c] for c in range(4)] + [R.hb[1][c] for c in range(4)]
    ha = [R.wd[0][:, 0, :].bitcast(F32), R.wd[0][:, 1, :].bitcast(F32)]
    hab = [R.wdb[0], R.wdb[0]]
    hn = [R.wd[1][:, 0, :].bitcast(F32), R.wd[1][:, 1, :].bitcast(F32)]
    hnb = [R.wdb[1], R.wdb[1]]
    emit_rmsnorm_fm(S, R, gm_t, gm_b)
    for h in range(4):
        for c in range(2):
            S.dma("sp", ha[c], P["haT"][h * 256 + c * 128:h * 256 + (c + 1) * 128, t0:t0 + T], writes=[hab[c]], key=hab[c])
        emit_rmsnorm_fm(S, R, gon_t[:, 2 * h:2 * h + 2], gon_b, src=ha, srcb=hab, dst=hn, dstb=hnb)
        wi = R.rot("wt", 4)
        S.dma("pool", wt[wi][:, :, :], w_v[:, :, O_AO + h * 256:O_AO + (h + 1) * 256], writes=[wtb[wi]])
        for c in range(2):
            for tt in range(TT):
                sl = slice(tt * 512, (tt + 1) * 512)
                p = R.rot("psC", 2)
                for k in range(KC):
                    S.op("pe", lambda e, k=k, wi=wi, c=c, sl=sl, p=p: e.matmul(R.ps[p][:, :], lhsT=wt[wi][:, k, c * 128:(c + 1) * 128], rhs=R.u[:, k, sl],
                                                                          start=(k == 0), stop=(k == KC - 1)), reads=[wtb[wi], R.ub[k]], writes=[R.psb[p]])
                si = R.rot("sil", 2)
                S.op("act", lambda e, si=si, p=p: e.activation(out=R.sil[si][:, :], in_=R.ps[p][:, :], func=AF.Sigmoid), reads=[R.psb[p]], writes=[R.silb[si]])
                S.op("dve", lambda e, si=si, h=h, c=c, sl=sl: e.tensor_tensor(out=y[2 * h + c][:, sl], in0=R.sil[si][:, :], in1=hn[c][:, sl], op=ALU.mult),
                     reads=[R.silb[si], hnb[c]], writes=[yb[2 * h + c]])
    for j in range(3):
        if j > 0:
            src = P["ybT"] if j == 1 else P["ycT"]
            for c in range(8):
                S.dma("sp", y[c], src[c * 128:(c + 1) * 128, t0:t0 + T], writes=[yb[c]], key=yb[c])
        wb_v = P["w_branch"][j].rearrange("(k p) f -> p k f", p=128)
        for d2 in range(KC // 2):
            wg_i = R.rot("wt", 4)
            S.dma("pool", wt[wg_i][:, :, :], w_v[:, :, O_G + j * D + d2 * 256:O_G + j * D + (d2 + 1) * 256], writes=[wtb[wg_i]])
            wb_i = R.rot("wt", 4)
            wbv = wt[wb_i].rearrange("p k c -> p (k c)")[:, :8 * 256].rearrange("p (k c) -> p k c", k=8)
            S.dma("pool", wbv, wb_v[:, :, d2 * 256:(d2 + 1) * 256], writes=[wtb[wb_i]])
            for c in range(2):
                d = d2 * 2 + c
                for tt in range(TT):
                    sl = slice(tt * 512, (tt + 1) * 512)
                    pg = R.rot("psC", 2)
                    pp = 2 + R.rot("psC2", 2)
                    for k in range(KC):
                        S.op("pe", lambda e, k=k, wg_i=wg_i, c=c, sl=sl, pg=pg: e.matmul(R.ps[pg][:, :], lhsT=wt[wg_i][:, k, c * 128:(c + 1) * 128], rhs=R.u[:, k, sl],
                                                                                   start=(k == 0), stop=(k == KC - 1)), reads=[wtb[wg_i], R.ub[k]], writes=[R.psb[pg]])
                    for k in range(8):
                        S.op("pe", lambda e, k=k, wbv=wbv, c=c, sl=sl, pp=pp: e.matmul(R.ps[pp][:, :], lhsT=wbv[:, k, c * 128:(c + 1) * 128], rhs=y[k][:, sl],
                                                                                 start=(k == 0), stop=(k == 7)), reads=[wtb[wb_i], yb[k]], writes=[R.psb[pp]])
                    si = R.rot("sil", 2)
                    S.op("act", lambda e, si=si, pg=pg: e.activation(out=R.sil[si][:, :], in_=R.ps[pg][:, :], func=AF.Sigmoid), reads=[R.psb[pg]], writes=[R.silb[si]])
                    if j == 0:
                        S.op("dve", lambda e, si=si, pp=pp, d=d, sl=sl: e.tensor_tensor(out=R.x[:, d, sl], in0=R.ps[pp][:, :], in1=R.sil[si][:, :], op=ALU.mult),
                             reads=[R.psb[pp], R.silb[si]], writes=[R.xb[d]])
                    else:
                        qi = R.rot("sq", 2)
                        S.op("dve", lambda e, si=si, pp=pp, qi=qi: e.tensor_tensor(out=R.sq[qi][:, :], in0=R.ps[pp][:, :], in1=R.sil[si][:, :], op=ALU.mult),
                             reads=[R.psb[pp], R.silb[si]], writes=[R.sqb[qi]])
                        S.op("dve", lambda e, qi=qi, d=d, sl=sl: e.tensor_tensor(out=R.x[:, d, sl], in0=R.x[:, d, sl], in1=R.sq[qi][:, :], op=ALU.add),
                             reads=[R.sqb[qi], R.xb[d]], writes=[R.xb[d]])
    for d in range(KC):
        S.op("act", lambda e, d=d: e.copy(out=R.u[:, d, :], in_=R.x[:, d, :]), reads=[R.xb[d]], writes=[R.ub[d]])
    for d in range(KC):
        S.dma("sp", R.x[:, d, :], P["x1T"][d * 128:(d + 1) * 128, t0:t0 + T], writes=[R.xb[d]])
    wo_v = P["w_out"].rearrange("(k p) f -> p k f", p=128)
    for d2 in range(KC // 2):
        wi = R.rot("wt", 4)
        S.dma("pool", wt[wi][:, :, :], wo_v[:, :, d2 * 256:(d2 + 1) * 256], writes=[wtb[wi]])
        for c in range(2):
            d = d2 * 2 + c
            for tt in range(TT):
                sl = slice(tt * 512, (tt + 1) * 512)
                p = R.rot("psC", 2)
                for k in range(KC):
                    S.op("pe", lambda e, k=k, wi=wi, c=c, sl=sl, p=p: e.matmul(R.ps[p][:, :], lhsT=wt[wi][:, k, c * 128:(c + 1) * 128], rhs=R.u[:, k, sl],
                                                                          start=(k == 0), stop=(k == KC - 1)), reads=[wtb[wi], R.ub[k]], writes=[R.psb[p]])
                S.op("dve", lambda e, p=p, d=d, sl=sl: e.tensor_tensor(out=R.x[:, d, sl], in0=R.ps[p][:, :], in1=R.x[:, d, sl], op=ALU.add),
                     reads=[R.psb[p], R.xb[d]], writes=[R.xb[d]])

import math
import numpy as np


class ResB:
    def __init__(self, S, SEQ):
        self.S = S
        self.SEQ = SEQ
        self.psw = [S.psum("psw%d" % i, [128, 1024]) for i in range(2)]
        self.pswb = [Buf("psw%d" % i, excl=True) for i in range(2)]
        ps4 = [S.psum("psb%d" % i, [128, 512]) for i in range(4, 8)]
        self.ps = [self.psw[0][:, 0:512], self.psw[0][:, 512:1024], self.psw[1][:, 0:512], self.psw[1][:, 512:1024]] + ps4
        self.psb = [self.pswb[0], self.pswb[0], self.pswb[1], self.pswb[1]] + [Buf("psb%d" % i, excl=True) for i in range(4, 8)]
        self.cnt = {}

    def rot(self, name, n):
        v = self.cnt.get(name, 0)
        self.cnt[name] = v + 1
        return v % n


def load_pp(S, name, ap2d, shape, eng="sp"):
    t = S.sbuf(name + "_sb", shape, F32)
    b = Buf(name)
    S.dma(eng, t[:], ap2d, writes=[b], slow=True)
    return t, b


def gen_lru(S, B, P, TP=512):
    SEQ = B.SEQ
    nc = S.nc
    prm, prmb = load_pp(S, "lru_prm", P["lru_prm"], [128, 8])
    wa = S.sbuf("lru_wa_sb", [128, 128], BF16); wab = Buf("lru_wa")
    wx = S.sbuf("lru_wx_sb", [128, 128], BF16); wxb = Buf("lru_wx")
    S.dma("pool", wa[:], P["lru_wa"], writes=[wab])
    S.dma("pool", wx[:], P["lru_wx"], writes=[wxb])
    cv = S.sbuf("lru_cv", [128, 4], F32); cvb = Buf("lru_cv")
    S.op("act", lambda e: e.activation(out=cv[:, 2:3], in_=prm[:, 7:8], func=AF.Exp, scale=-1.0), reads=[prmb], writes=[cvb])
    S.op("dve", lambda e: e.tensor_scalar(out=cv[:, 2:3], in0=cv[:, 2:3], scalar1=1.0, scalar2=None, op0=ALU.add), reads=[cvb], writes=[cvb])
    S.op("act", lambda e: e.activation(out=cv[:, 3:4], in_=cv[:, 2:3], func=AF.Ln), reads=[cvb], writes=[cvb])
    S.op("dve", lambda e: e.tensor_scalar(out=cv[:, 0:1], in0=cv[:, 3:4], scalar1=-8.0, scalar2=None, op0=ALU.mult), reads=[cvb], writes=[cvb])
    S.op("dve", lambda e: e.tensor_scalar(out=cv[:, 1:2], in0=cv[:, 3:4], scalar1=-16.0, scalar2=None, op0=ALU.mult), reads=[cvb], writes=[cvb])
    xin = [S.sbuf("lru_x%d" % i, [128, 3 + TP], F32) for i in range(2)]
    xinb = [Buf("lru_x%d" % i) for i in range(2)]
    xc = S.sbuf("lru_xc", [128, TP], F32); xcb = Buf("lru_xc")
    xcbf = S.sbuf("lru_xcbf", [128, TP], BF16); xcbfb = Buf("lru_xcbf")
    r = S.sbuf("lru_r", [128, TP], F32); rb = Buf("lru_r")
    gi = S.sbuf("lru_gi", [128, TP], F32); gib = Buf("lru_gi")
    a = S.sbuf("lru_a", [128, TP], F32); ab = Buf("lru_a")
    hh = [S.sbuf("lru_h%d" % i, [128, TP], F32) for i in range(2)]
    hb = [Buf("lru_h%d" % i) for i in range(2)]
    ho = [S.sbuf("lru_ho%d" % i, [128, TP], BF16) for i in range(2)]
    hob = [Buf("lru_ho%d" % i) for i in range(2)]
    S.op("pool", lambda e: e.memset(xin[0][:, 0:3], 0.0), writes=[xinb[0]])
    NP = SEQ // TP
    for pc in range(NP):
        i = pc % 2
        t0 = pc * TP
        S.dma("sp", xin[i][:, 3:3 + TP], P["cxT"][:, t0:t0 + TP], writes=[xinb[i]])
        if pc + 1 < NP:
            S.op("pool", lambda e, i=i: e.tensor_copy(out=xin[1 - i][:, 0:3], in_=xin[i][:, TP:TP + 3]), reads=[xinb[i]], writes=[xinb[1 - i]])
        S.op("dve", lambda e, i=i: e.tensor_scalar(out=xc[:, :], in0=xin[i][:, 3:3 + TP], scalar1=prm[:, 3:4], scalar2=prm[:, 4:5], op0=ALU.mult, op1=ALU.add),
             reads=[xinb[i], prmb], writes=[xcb])
        for j in range(3):
            S.op("dve", lambda e, i=i, j=j: e.scalar_tensor_tensor(out=xc[:, :], in0=xin[i][:, j:j + TP], scalar=prm[:, j:j + 1], in1=xc[:, :], op0=ALU.mult, op1=ALU.add),
                 reads=[xinb[i], prmb, xcb], writes=[xcb])
        S.op("act", lambda e: e.copy(out=xcbf[:, :], in_=xc[:, :]), reads=[xcb], writes=[xcbfb])
        for tt in range(TP // 512):
            sl = slice(tt * 512, (tt + 1) * 512)
            S.op("pe", lambda e, sl=sl: e.matmul(B.ps[6][:, :], lhsT=wa[:, :], rhs=xcbf[:, sl], start=True, stop=True), reads=[wab, xcbfb], writes=[B.psb[6]])
            S.op("pe", lambda e, sl=sl: e.matmul(B.ps[7][:, :], lhsT=wx[:, :], rhs=xcbf[:, sl], start=True, stop=True), reads=[wxb, xcbfb], writes=[B.psb[7]])
            S.op("act", lambda e, sl=sl: e.activation(out=r[:, sl], in_=B.ps[6][:, :], func=AF.Sigmoid, bias=prm[:, 5:6]), reads=[B.psb[6], prmb], writes=[rb])
            S.op("act", lambda e, sl=sl: e.activation(out=gi[:, sl], in_=B.ps[7][:, :], func=AF.Sigmoid, bias=prm[:, 6:7]), reads=[B.psb[7], prmb], writes=[gib])
        S.op("act", lambda e: e.activation(out=a[:, :], in_=r[:, :], func=AF.Exp, scale=cv[:, 0:1]), reads=[rb, cvb], writes=[ab])
        S.op("act", lambda e: e.activation(out=r[:, :], in_=r[:, :], func=AF.Exp, scale=cv[:, 1:2]), reads=[rb, cvb], writes=[rb])
        S.op("dve", lambda e: e.tensor_scalar(out=r[:, :], in0=r[:, :], scalar1=-1.0, scalar2=1.0, op0=ALU.mult, op1=ALU.add), reads=[rb], writes=[rb])
        S.op("act", lambda e: e.activation(out=r[:, :], in_=r[:, :], func=AF.Sqrt), reads=[rb], writes=[rb])
        S.op("dve", lambda e: e.tensor_tensor(out=gi[:, :], in0=gi[:, :], in1=xc[:, :], op=ALU.mult), reads=[gib, xcb], writes=[gib])
        S.op("dve", lambda e: e.tensor_tensor(out=gi[:, :], in0=gi[:, :], in1=r[:, :], op=ALU.mult), reads=[gib, rb], writes=[gib])
        if pc == 0:
            S.op("dve", lambda e, i=i: e.tensor_tensor_scan(out=hh[i][:, :], data0=a[:, :], data1=gi[:, :], initial=0.0, op0=ALU.mult, op1=ALU.add),
                 reads=[ab, gib], writes=[hb[i]])
        else:
            S.op("dve", lambda e, i=i: e.tensor_tensor_scan(out=hh[i][:, :], data0=a[:, :], data1=gi[:, :], initial=hh[1 - i][:, TP - 1:TP], op0=ALU.mult, op1=ALU.add),
                 reads=[ab, gib, hb[1 - i]], writes=[hb[i]])
        S.op("act", lambda e, i=i: e.copy(out=ho[i][:, :], in_=hh[i][:, :]), reads=[hb[i]], writes=[hob[i]])
        S.dma("sp", P["ycT"][:, t0:t0 + TP], ho[i][:, :], reads=[hob[i]])
        yield


def gen_mlstm(S, B, P, NCP=16):
    SEQ = B.SEQ
    NCH = SEQ // 128
    NCP = min(NCP, NCH)
    tri, trib = load_pp(S, "ml_tri", P["tri"], [128, 128])
    gbias, gbb = load_pp(S, "ml_gb", P["gbias"], [128, 2])
    ones = S.sbuf("ml_ones", [128, 128], F32); onesb = Buf("ml_ones")
    S.op("pool", lambda e: e.memset(ones[:], 1.0), writes=[onesb])
    S.op("dve", lambda e: e.tensor_scalar(out=gbias[:, 1:2], in0=gbias[:, 1:2], scalar1=-1.0, scalar2=None, op0=ALU.mult), reads=[gbb], writes=[gbb])
    qT = [S.sbuf("ml_qT%d" % i, [128, NCP * 128], BF16) for i in range(2)]; qTb = [Buf("ml_qT%d" % i) for i in range(2)]
    kT = [S.sbuf("ml_kT%d" % i, [128, NCP * 128], BF16) for i in range(2)]; kTb = [Buf("ml_kT%d" % i) for i in range(2)]
    kk = [S.sbuf("ml_k%d" % i, [128, NCP, 128], BF16) for i in range(2)]; kkb = [Buf("ml_k%d" % i) for i in range(2)]
    vv = [S.sbuf("ml_v%d" % i, [128, NCP, 128], BF16) for i in range(2)]; vvb = [Buf("ml_v%d" % i) for i in range(2)]
    gg = [S.sbuf("ml_g%d" % i, [128, NCP, 2], F32) for i in range(2)]; ggb = [Buf("ml_g%d" % i) for i in range(2)]
    nlf = S.sbuf("ml_nlf", [128, NCP], F32); nlfb = Buf("ml_nlf")
    ii = S.sbuf("ml_i", [128, NCP], F32); iib = Buf("ml_i")
    wv = S.sbuf("ml_wv", [128, NCP], F32); wvb = Buf("ml_wv")
    wq = S.sbuf("ml_wq", [128, NCP], F32); wqb = Buf("ml_wq")
    wL = S.sbuf("ml_wL", [128, NCP], F32); wLb = Buf("ml_wL")
    vs = S.sbuf("ml_vs", [128, NCP, 129], BF16); vsb = Buf("ml_vs")
    stb = S.sbuf("ml_st", [128, NCP, 128], BF16); stbb = [Buf("ml_st%d" % c) for c in range(NCP)]
    C = S.sbuf("ml_C", [128, 129], F32); Cb = Buf("ml_C")
    Cbf = S.sbuf("ml_Cbf", [128, NCP + 1, 129], BF16); Cbfb = [Buf("ml_Cbf%d" % c) for c in range(NCP + 1)]
    oall = S.sbuf("ml_oall", [128, NCP, 129], F32); oallb = Buf("ml_oall")
    t1 = S.sbuf("ml_t1", [128, NCP], F32); t1b = Buf("ml_t1")
    _h0 = S.sbuf("ml_ho0", [128, NCP, 128], F32); _hb0 = Buf("ml_ho0")
    hout = [_h0, _h0]; houtb = [_hb0, _hb0]
    S.op("pool", lambda e: e.memset(C[:], 0.0), writes=[Cb])
    PB = (6, 7, 6, 7)
    for pc in range(NCH // NCP):
        i = pc % 2
        t0 = pc * NCP * 128
        T = NCP * 128
        S.dma("sp", qT[i][:, :], P["qaT"][:, t0:t0 + T], writes=[qTb[i]])
        S.dma("sp", kT[i][:, :], P["kaT"][:, t0:t0 + T], writes=[kTb[i]])
        S.dma("sp", kk[i][:, :, :], P["ka"][t0:t0 + T, :].rearrange("(b p) d -> p b d", p=128), writes=[kkb[i]])
        S.dma("sp", vv[i][:, :, :], P["va"][t0:t0 + T, :].rearrange("(b p) d -> p b d", p=128), writes=[vvb[i]])
        S.dma("sp", gg[i][:, :, :], P["gif"][t0:t0 + T, :].rearrange("(b p) d -> p b d", p=128), writes=[ggb[i]], slow=True)
        S.op("act", lambda e, i=i: e.activation(out=nlf[:, :], in_=gg[i][:, :, 1], func=AF.Exp, scale=-1.0, bias=gbias[:, 1:2]), reads=[ggb[i], gbb], writes=[nlfb])
        S.op("dve", lambda e: e.tensor_scalar(out=nlf[:, :], in0=nlf[:, :], scalar1=1.0, scalar2=None, op0=ALU.add), reads=[nlfb], writes=[nlfb])
        S.op("act", lambda e: e.activation(out=nlf[:, :], in_=nlf[:, :], func=AF.Ln), reads=[nlfb], writes=[nlfb])
        S.op("dve", lambda e, i=i: e.tensor_scalar(out=ii[:, :], in0=gg[i][:, :, 0], scalar1=gbias[:, 0:1], scalar2=None, op0=ALU.add), reads=[ggb[i], gbb], writes=[iib])
        pcum = B.ps[PB[1]]; pcumb = B.psb[PB[1]]
        S.op("pe", lambda e: e.matmul(pcum[:, 0:NCP], lhsT=tri[:, :], rhs=nlf[:, :], start=True, stop=True), reads=[trib, nlfb], writes=[pcumb])
        S.op("pe", lambda e: e.matmul(pcum[:, 32:32 + NCP], lhsT=ones[:, :], rhs=nlf[:, :], start=True, stop=True), reads=[onesb, nlfb], writes=[pcumb])
        S.op("act", lambda e: e.activation(out=wq[:, :], in_=pcum[:, 0:NCP], func=AF.Exp, scale=-1.0), reads=[pcumb], writes=[wqb])
        S.op("act", lambda e: e.activation(out=wL[:, :], in_=pcum[:, 32:32 + NCP], func=AF.Exp, scale=-1.0), reads=[pcumb], writes=[wLb])
        S.op("act", lambda e: e.copy(out=wv[:, :], in_=pcum[:, 0:NCP]), reads=[pcumb], writes=[wvb])
        S.op("dve", lambda e: e.tensor_tensor(out=wv[:, :], in0=wv[:, :], in1=ii[:, :], op=ALU.add), reads=[wvb, iib], writes=[wvb])
        S.op("act", lambda e: e.activation(out=wv[:, :], in_=wv[:, :], func=AF.Exp), reads=[wvb], writes=[wvb])
        S.op("dve", lambda e, i=i: e.tensor_tensor(out=vs[:, :, 0:128], in0=vv[i][:, :, :], in1=wv[:, :].unsqueeze(2).to_broadcast([128, NCP, 128]), op=ALU.mult),
             reads=[vvb[i], wvb], writes=[vsb])
        S.op("dve", lambda e: e.tensor_copy(out=vs[:, :, 128:129], in_=wv[:, :].unsqueeze(2)), reads=[wvb], writes=[vsb])
        S.op("dve", lambda e: e.tensor_copy(out=Cbf[:, 0, :], in_=C[:, :]), reads=[Cb], writes=[Cbfb[0]])
        for c in range(NCP):
            pb = PB[c % 2]
            S.op("pe", lambda e, i=i, c=c, pb=pb: e.matmul(B.ps[pb][:, 0:129], lhsT=kk[i][:, c, :], rhs=vs[:, c, :], start=True, stop=True), reads=[kkb[i], vsb], writes=[B.psb[pb]])
            S.op("dve", lambda e, c=c: e.tensor_scalar(out=C[:, :], in0=C[:, :], scalar1=wL[:, c:c + 1], scalar2=None, op0=ALU.mult), reads=[Cb, wLb], writes=[Cb])
            S.op("dve", lambda e, c=c, pb=pb: e.scalar_tensor_tensor(out=C[:, :], in0=B.ps[pb][:, 0:129], scalar=wL[:, c:c + 1], in1=C[:, :], op0=ALU.mult, op1=ALU.add),
                 reads=[B.psb[pb], wLb, Cb], writes=[Cb])
            S.op("dve", lambda e, c=c: e.tensor_copy(out=Cbf[:, c + 1, :], in_=C[:, :]), reads=[Cb], writes=[Cbfb[c + 1]])
            if c % 4 == 3:
                yield
        for c in range(NCP):
            pb = PB[2 + c % 2]
            sl = slice(c * 128, (c + 1) * 128)
            S.op("pe", lambda e, i=i, sl=sl, pb=pb: e.matmul(B.ps[pb][:, 0:128], lhsT=kT[i][:, sl], rhs=qT[i][:, sl], start=True, stop=True), reads=[kTb[i], qTb[i]], writes=[B.psb[pb]])
            S.op("dve", lambda e, c=c, pb=pb: e.tensor_tensor(out=stb[:, c, :], in0=B.ps[pb][:, 0:128], in1=tri[:, :], op=ALU.mult), reads=[B.psb[pb], trib], writes=[stbb[c]])
            if c % 4 == 3:
                yield
        for c in range(NCP):
            pb = PB[c % 2]
            sl = slice(c * 128, (c + 1) * 128)
            S.op("pe", lambda e, c=c, pb=pb: e.matmul(B.ps[pb][:, 0:129], lhsT=stb[:, c, :], rhs=vs[:, c, :], start=True, stop=False), reads=[stbb[c], vsb], writes=[B.psb[pb]])
            S.op("pe", lambda e, i=i, c=c, sl=sl, pb=pb: e.matmul(B.ps[pb][:, 0:129], lhsT=qT[i][:, sl], rhs=Cbf[:, c, :], start=False, stop=True), reads=[qTb[i], Cbfb[c]], writes=[B.psb[pb]])
            S.op("act", lambda e, c=c, pb=pb: e.copy(out=oall[:, c, :], in_=B.ps[pb][:, 0:129]), reads=[B.psb[pb]], writes=[oallb])
            if c % 4 == 3:
                yield
        S.op("dve", lambda e: e.tensor_tensor(out=t1[:, :], in0=oall[:, :, 128], in1=wq[:, :], op=ALU.mult), reads=[oallb, wqb], writes=[t1b])
        S.op("act", lambda e: e.activation(out=t1[:, :], in_=t1[:, :], func=AF.Abs), reads=[t1b], writes=[t1b])
        S.op("dve", lambda e: e.tensor_scalar_max(out=t1[:, :], in0=t1[:, :], scalar1=1.0), reads=[t1b], writes=[t1b])
        S.op("dve", lambda e: e.reciprocal(out=t1[:, :], in_=t1[:, :]), reads=[t1b], writes=[t1b])
        S.op("dve", lambda e: e.tensor_tensor(out=t1[:, :], in0=t1[:, :], in1=wq[:, :], op=ALU.mult), reads=[t1b, wqb], writes=[t1b])
        S.op("dve", lambda e, i=i: e.tensor_tensor(out=hout[i][:, :, :], in0=oall[:, :, 0:128], in1=t1[:, :].unsqueeze(2).to_broadcast([128, NCP, 128]), op=ALU.mult),
             reads=[oallb, t1b], writes=[houtb[i]])
        S.dma("sp", P["ha"][t0:t0 + T, :].rearrange("(b p) d -> p b d", p=128), hout[i][:, :, :], reads=[houtb[i]])
        yield


def gen_mla(S, B, P):
    SEQ = B.SEQ
    NKB = SEQ // 128
    NQT = SEQ // 512
    kT = S.sbuf("at_kT", [128, SEQ], BF16); kTb = Buf("at_kT")
    kr = S.sbuf("at_kr", [128, SEQ], BF16); krb = Buf("at_kr")
    S.op("pool", lambda e: e.memset(kr[64:128, :], 0.0), writes=[krb])
    V = S.sbuf("at_V", [128, NKB, 128], BF16); Vb = Buf("at_V")
    msk = S.sbuf("at_msk", [128, 4, 512], BF16); mskb = Buf("at_msk")
    S.dma("sp", msk[:, :, :], P["masks"].rearrange("o k q -> k o q"), writes=[mskb])
    QCH = 4096
    for c0 in range(0, SEQ, QCH):
        c1 = min(SEQ, c0 + QCH)
        S.dma("sp", kT[:, c0:c1], P["kbT"][:, c0:c1], writes=[kTb], key=kTb)
        S.dma("sp", kr[0:64, c0:c1], P["krT"][:, c0:c1], writes=[krb], key=krb)
    vb_v = P["vb"].rearrange("(b p) d -> p b d", p=128)
    for b0 in range(0, NKB, 16):
        b1 = min(NKB, b0 + 16)
        S.dma("sp", V[:, b0:b1, 0:128], vb_v[:, b0:b1, :], writes=[Vb], key=Vb)
    qT = [S.sbuf("at_q%d" % i, [128, 512], BF16) for i in range(2)]; qTb = [Buf("at_q%d" % i) for i in range(2)]
    qr = [S.sbuf("at_qr%d" % i, [128, 512], BF16) for i in range(2)]; qrb = [Buf("at_qr%d" % i) for i in range(2)]
    for i_ in range(2):
        S.op("pool", lambda e, i_=i_: e.memset(qr[i_][64:128, :], 0.0), writes=[qrb[i_]])
    pT = [S.sbuf("at_p%d" % i, [128, 1024], BF16) for i in range(3)]; pTb = [Buf("at_p%d" % i) for i in range(3)]
    PS_O, PS_SUM = 4, 5
    ones = S.sbuf("at_ones", [128, 1], BF16); onesb = Buf("at_ones")
    S.op("pool", lambda e: e.memset(ones[:], 1.0), writes=[onesb])
    onesf = S.sbuf("at_onesf", [1, 128], F32); onesfb = Buf("at_onesf")
    S.op("pool", lambda e: e.memset(onesf[:], 1.0), writes=[onesfb])
    rs = S.sbuf("at_rs", [1, 512], F32); rsb = Buf("at_rs")
    acc = S.sbuf("at_acc", [128, 512], F32); accb = Buf("at_acc")
    s2 = [S.sbuf("at_s2_%d" % i_, [128, 512], BF16) for i_ in range(2)]; s2b = [Buf("at_s2_%d" % i_) for i_ in range(2)]
    onescol = S.sbuf("at_onescol", [128, 1], F32); onescolb = Buf("at_onescol")
    S.op("pool", lambda e: e.memset(onescol[:], 1.0), writes=[onescolb])
    yT = [S.sbuf("at_yT%d" % i, [128, 512], BF16) for i in range(2)]; yTb = [Buf("at_yT%d" % i) for i in range(2)]
    steps = [(j, k2) for j in range(NQT) for k2 in range(2 * j + 2)]

    def emit_st(t):
        j, k2 = steps[t]
        i = j % 2
        if k2 == 0:
            q0 = j * 512
            S.dma("sp", qT[i][:, :], P["qbT"][:, q0:q0 + 512], writes=[qTb[i]])
            S.dma("sp", qr[i][0:64, :], P["qrT"][:, q0:q0 + 512], writes=[qrb[i]])
        w = t % 2
        for hf in range(2):
            kb = 2 * k2 + hf
            ks = slice(kb * 128, (kb + 1) * 128)
            pss = B.psw[w][:, hf * 512:(hf + 1) * 512]
            S.op("pe", lambda e, ks=ks, pss=pss: e.matmul(pss, lhsT=kT[:, ks], rhs=qT[i][:, :], start=True, stop=False), reads=[kTb, qTb[i]], writes=[B.pswb[w]])
            S.op("pe", lambda e, ks=ks, pss=pss: e.matmul(pss, lhsT=kr[:, ks], rhs=qr[i][:, :], start=False, stop=True), reads=[krb, qrb[i]], writes=[B.pswb[w]])
        pi = t % 3
        S.op("act", lambda e: e.activation(out=pT[pi][:, :], in_=B.psw[w][:, :], func=AF.Exp), reads=[B.pswb[w]], writes=[pTb[pi]])
        for hf in range(2):
            o = 2 * k2 + hf - 4 * j
            if o >= 0:
                S.op("dve", lambda e, hf=hf, o=o: e.tensor_tensor(out=pT[pi][:, hf * 512 + o * 128:(hf + 1) * 512], in0=pT[pi][:, hf * 512 + o * 128:(hf + 1) * 512], in1=msk[:, o, o * 128:512], op=ALU.mult),
                     reads=[pTb[pi], mskb], writes=[pTb[pi]])

    def emit_pv(t):
        j, k2 = steps[t]
        i = j % 2
        pi = t % 3
        full = (2 * k2 + 1 < 4 * j)
        for hf in range(2):
            kb = 2 * k2 + hf
            o = max(0, kb - 4 * j)
            c0 = o * 128
            last = (kb == 4 * j + 3)
            mv = pT[pi][:, hf * 512 + c0:(hf + 1) * 512]
            S.op("pe", lambda e, kb=kb, c0=c0, last=last, mv=mv: e.matmul(B.ps[PS_O][:, c0:512], lhsT=V[:, kb, 0:128], rhs=mv, start=(kb == 0), stop=last, skip_group_check=True),
                 reads=[pTb[pi], Vb], writes=[B.psb[PS_O]])
            if full:
                continue
            if kb == 0:
                S.op("dve", lambda e, mv=mv: e.tensor_copy(out=acc[:, :], in_=mv), reads=[pTb[pi]], writes=[accb])
            else:
                S.op("dve", lambda e, c0=c0, mv=mv: e.tensor_tensor(out=acc[:, c0:512], in0=acc[:, c0:512], in1=mv, op=ALU.add), reads=[pTb[pi], accb], writes=[accb])
        if full:
            if k2 == 0:
                S.op("dve", lambda e: e.tensor_tensor(out=acc[:, :], in0=pT[pi][:, 0:512], in1=pT[pi][:, 512:1024], op=ALU.add), reads=[pTb[pi]], writes=[accb])
            else:
                si = t % 2
                S.op("dve", lambda e, si=si: e.tensor_tensor(out=s2[si][:, :], in0=pT[pi][:, 0:512], in1=pT[pi][:, 512:1024], op=ALU.add), reads=[pTb[pi]], writes=[s2b[si]])
                S.op("dve", lambda e, si=si: e.tensor_tensor(out=acc[:, :], in0=acc[:, :], in1=s2[si][:, :], op=ALU.add), reads=[s2b[si], accb], writes=[accb])
        if k2 == 2 * j + 1:
            q0 = j * 512
            S.op("pe", lambda e: e.matmul(B.ps[PS_SUM][0:1, :], lhsT=onescol[:, 0:1], rhs=acc[:, :], start=True, stop=True), reads=[onescolb, accb], writes=[B.psb[PS_SUM]])
            S.op("dve", lambda e: e.reciprocal(out=rs[:, :], in_=B.ps[PS_SUM][0:1, :]), reads=[B.psb[PS_SUM]], writes=[rsb])
            S.op("pe", lambda e: e.matmul(B.ps[PS_SUM][:, :], lhsT=onesf[:, :], rhs=rs[:, :], start=True, stop=True), reads=[onesfb, rsb], writes=[B.psb[PS_SUM]])
            S.op("act", lambda e: e.copy(out=acc[:, :], in_=B.ps[PS_SUM][:, :]), reads=[B.psb[PS_SUM]], writes=[accb])
            S.op("dve", lambda e: e.tensor_tensor(out=yT[i][:, :], in0=B.ps[PS_O][:, :], in1=acc[:, :], op=ALU.mult), reads=[B.psb[PS_O], accb], writes=[yTb[i]])
            S.dma("sp", P["ybT"][:, q0:q0 + 512], yT[i][:, :], reads=[yTb[i]])

    emit_st(0)
    for t in range(len(steps)):
        if t + 1 < len(steps):
            emit_st(t + 1)
        emit_pv(t)
        if t % 2 == 1:
            yield
    yield


def drive(gens, weights):
    alive = list(gens)
    w = list(weights)
    while alive:
        for g, k in list(zip(alive, w)):
            for _ in range(k):
                try:
                    next(g)
                except StopIteration:
                    idx = alive.index(g)
                    alive.pop(idx); w.pop(idx)
                    break

from concourse.bass_utils import run_bass_kernel_spmd
import ml_dtypes

SEQ = 16384
NCORE = 8
TC = SEQ // NCORE
TT_TILE = 1024

A_OUTS = dict(x1T=([D, None], F32), qaT=([512, None], BF16), kaT=([512, None], BF16), ka=([None, 512], BF16), va=([None, 1024], BF16),
              gif=([None, 8], F32), qbnT=([1024, None], BF16), qbr1T=([256, None], BF16), qbr2T=([256, None], BF16), kbnT=([1024, None], BF16),
              kbr1T=([32, None], BF16), kbr2T=([32, None], BF16), vb=([None, 1024], BF16), cxT=([1024, None], F32))


def _declare_proj(nc, P, Tc, sfx=""):
    def inp(name, shape, dt=F32):
        P[name] = nc.dram_tensor(name, shape, dt, kind="ExternalInput").ap()
    inp("gm" + sfx, [D]); inp("w_in" + sfx, [D, N_IN]); inp("gq" + sfx, [384]); inp("gkv" + sfx, [256])
    inp("w_uq" + sfx, [384, 1536]); inp("w_ukv" + sfx, [256, 2048])
    if "cs" not in P:
        inp("cs", [32, 2, Tc])
    for k, (shp, dt) in A_OUTS.items():
        P[k + sfx] = nc.dram_tensor(k + sfx, [Tc if s is None else s for s in shp], dt, kind="ExternalOutput").ap()


def _proj_P(P, sfx):
    Q = {k: P[k + sfx] for k in A_OUTS}
    Q.update(w_in=P["w_in" + sfx], w_uq=P["w_uq" + sfx], w_ukv=P["w_ukv" + sfx], cs=P["cs"])
    return Q


def build_prog(kind, Tc=TC, T=TT_TILE):
    nc = bass.Bass("TRN2", target_bir_lowering=False)
    P = {}

    def inp(name, shape, dt=F32):
        P[name] = nc.dram_tensor(name, shape, dt, kind="ExternalInput").ap()

    def ffn_in(sfx):
        inp("g" + sfx, [D]); inp("wg" + sfx, [D, DFF]); inp("wu" + sfx, [D, DFF]); inp("wd" + sfx, [DFF, D])

    if kind == "A":
        inp("xT", [D, Tc]); ffn_in("1"); _declare_proj(nc, P, Tc)
    else:
        inp("x1T_in", [D, Tc]); inp("haT", [1024, Tc]); inp("ybT", [1024, Tc], BF16); inp("ycT", [1024, Tc], BF16)
        inp("gm_m", [D]); inp("w_in_m", [D, N_IN]); inp("gon", [1024]); inp("w_branch", [3, 1024, D]); inp("w_out", [D, D])
        ffn_in("2")
        if kind == "C":
            ffn_in("1"); _declare_proj(nc, P, Tc)
        else:
            inp("gfin", [D])
            P["outT"] = nc.dram_tensor("outT", [D, Tc], F32, kind="ExternalOutput").ap()
    with contextlib.ExitStack() as stack:
        S = Sched(nc, stack)
        R = Res(S, T)
        if kind in ("A", "C"):
            A = ResA(S, R)
            g1_t, g1_b = load_vec_pp(S, "g1", P["g1"], KC)
            gm_t, gm_b = load_vec_pp(S, "gm", P["gm"], KC)
            A.gq, A.gqb = load_vec_pp(S, "gq", P["gq"], 3, scale=192 ** -0.5)
            A.gkv, A.gkvb = load_vec_pp(S, "gkv", P["gkv"], 2)
        if kind in ("C", "D"):
            g2_t, g2_b = load_vec_pp(S, "g2", P["g2"], KC)
            gmm_t, gmm_b = load_vec_pp(S, "gm_m", P["gm_m"], KC)
            gon_t, gon_b = load_vec_pp(S, "gon", P["gon"], 8)
        if kind == "D":
            gf_t, gf_b = load_vec_pp(S, "gfin", P["gfin"], KC)
        for t0 in range(0, Tc, T):
            src = P["xT"] if kind == "A" else P["x1T_in"]
            for d in range(KC):
                S.dma("sp", R.x[:, d, :], src[d * 128:(d + 1) * 128, t0:t0 + T], writes=[R.xb[d]])
            if kind in ("C", "D"):
                PM = dict(w_in=P["w_in_m"], w_branch=P["w_branch"], w_out=P["w_out"], x1T=P["x1T_in"], haT=P["haT"], ybT=P["ybT"], ycT=P["ycT"])
                emit_merge(S, R, PM, t0, gmm_t, gmm_b, gon_t, gon_b)
                emit_rmsnorm_fm(S, R, g2_t, g2_b)
                emit_ffn(S, R, P["wg2"], P["wu2"], P["wd2"])
            if kind in ("A", "C"):
                emit_rmsnorm_fm(S, R, g1_t, g1_b)
                emit_ffn(S, R, P["wg1"], P["wu1"], P["wd1"])
                for d in range(KC):
                    S.dma("sp", P["x1T"][d * 128:(d + 1) * 128, t0:t0 + T], R.x[:, d, :], reads=[R.xb[d]])
                emit_rmsnorm_fm(S, R, gm_t, gm_b)
                emit_phase_a_proj(S, R, A, _proj_P(P, ""), t0)
            else:
                xs = [R.x[:, d, :] for d in range(KC)]
                emit_rmsnorm_fm(S, R, gf_t, gf_b, src=xs, srcb=R.xb, dst=xs, dstb=R.xb)
                for d in range(KC):
                    S.dma("sp", P["outT"][d * 128:(d + 1) * 128, t0:t0 + T], R.x[:, d, :], reads=[R.xb[d]])
        info = S.emit()
    return nc, info


def build_prog_b(seq=SEQ):
    nc = bass.Bass("TRN2", target_bir_lowering=False)
    P = {}

    def inp(name, shape, dt=F32):
        P[name] = nc.dram_tensor(name, shape, dt, kind="ExternalInput").ap()

    def out(name, shape, dt=F32):
        P[name] = nc.dram_tensor(name, shape, dt, kind="ExternalOutput").ap()
    inp("cxT", [128, seq]); inp("lru_prm", [128, 8]); inp("lru_wa", [128, 128]); inp("lru_wx", [128, 128]); out("ycT", [128, seq], BF16)
    inp("qaT", [128, seq], BF16); inp("kaT", [128, seq], BF16); inp("ka", [seq, 128], BF16); inp("va", [seq, 128], BF16); inp("gif", [seq, 2])
    inp("gbias", [128, 2]); inp("tri", [128, 128]); out("ha", [seq, 128])
    inp("qbT", [128, seq], BF16); inp("qrT", [64, seq], BF16); inp("kbT", [128, seq], BF16); inp("krT", [64, seq], BF16); inp("vb", [seq, 128], BF16)
    inp("masks", [4, 128, 512], BF16); out("ybT", [128, seq], BF16)
    with contextlib.ExitStack() as stack:
        S = Sched(nc, stack)
        B = ResB(S, seq)
        drive([gen_lru(S, B, P), gen_mlstm(S, B, P), gen_mla(S, B, P)], [1, 2, 9])
        info = S.emit()
    return nc, info


def _rope_tables(pos):
    inv = np.power(np.float32(10000.0), -np.arange(0, 64, 2, dtype=np.float32) / np.float32(64)).astype(np.float32)
    ang = (pos.astype(np.float32)[:, None] * inv[None, :]).astype(np.float32)
    return np.ascontiguousarray(np.stack([np.cos(ang).astype(np.float32).T, np.sin(ang).astype(np.float32).T], axis=1))


def _consts():
    tri = (np.arange(128)[:, None] <= np.arange(128)[None, :]).astype(np.float32)
    k = np.arange(128)[:, None]; q = np.arange(512)[None, :]
    masks = np.stack([(q >= k + 128 * o) for o in range(4)]).astype(np.float32).astype(ml_dtypes.bfloat16)
    return tri, masks


_PROGS = {}


def _prog(kind):
    if kind not in _PROGS:
        _PROGS[kind] = build_prog_b()[0] if kind == "B" else build_prog(kind)[0]
    return _PROGS[kind]


def _run(kind, in_maps):
    res = run_bass_kernel_spmd(_prog(kind), in_maps, core_ids=list(range(NCORE)))
    return res.results


def _ffn_w(inp, l, which, sfx):
    if which == 1:
        g, wg, wu, wd = inp["ffn1_norm"], inp["ffn1_w_gate"], inp["ffn1_w_up"], inp["ffn1_w_down"]
    else:
        g, wg, wu, wd = inp["ffn2_norm"], inp["ffn2_w_gate"], inp["ffn2_w_up"], inp["ffn2_w_down"]
    return {"g" + sfx: g[l], "wg" + sfx: wg[l], "wu" + sfx: wu[l], "wd" + sfx: wd[l]}


def _proj_w(inp, l):
    return {"gm": inp["mix_norm"][l], "w_in": inp["w_in"][l], "gq": inp["mla_q_norm"][l], "gkv": inp["mla_kv_norm"][l],
            "w_uq": inp["mla_w_uq"][l], "w_ukv": inp["mla_w_ukv"][l]}


def _mixer_maps(inp, l, ra):
    tri, masks = _consts()
    cat1 = lambda k: np.concatenate([r[k] for r in ra], axis=1)
    cat0 = lambda k: np.concatenate([r[k] for r in ra], axis=0)
    qaT, kaT, ka, va, gif = cat1("qaT"), cat1("kaT"), cat0("ka"), cat0("va"), cat0("gif")
    qbnT, qbr1T, qbr2T, kbnT, kbr1T, kbr2T, vb, cxT = cat1("qbnT"), cat1("qbr1T"), cat1("qbr2T"), cat1("kbnT"), cat1("kbr1T"), cat1("kbr2T"), cat0("vb"), cat1("cxT")
    krT = np.ascontiguousarray(np.concatenate([kbr1T, kbr2T], axis=0))
    gb = inp["mlstm_gate_bias"][l]
    maps = []
    for c in range(NCORE):
        h, half = c // 2, c % 2
        ch = slice(c * 128, (c + 1) * 128)
        prm = np.concatenate([inp["lru_conv_w"][l][:, ch].T, inp["lru_conv_b"][l][ch, None], inp["lru_b_a"][l][ch, None],
                              inp["lru_b_x"][l][ch, None], inp["lru_lambda"][l][ch, None]], axis=1)
        m = dict(
            cxT=np.ascontiguousarray(cxT[ch]), lru_prm=np.ascontiguousarray(prm.astype(np.float32)),
            lru_wa=np.ascontiguousarray(inp["lru_w_a"][l][c]), lru_wx=np.ascontiguousarray(inp["lru_w_x"][l][c]),
            qaT=np.ascontiguousarray(qaT[h * 128:(h + 1) * 128]), kaT=np.ascontiguousarray(kaT[h * 128:(h + 1) * 128]),
            ka=np.ascontiguousarray(ka[:, h * 128:(h + 1) * 128]), va=np.ascontiguousarray(va[:, h * 256 + half * 128:h * 256 + (half + 1) * 128]),
            gif=np.ascontiguousarray(gif[:, [h, 4 + h]]), gbias=np.ascontiguousarray(np.tile(gb[[h, 4 + h]][None, :], (128, 1)).astype(np.float32)),
            tri=tri, masks=masks,
            qbT=np.ascontiguousarray(qbnT[ch]), qrT=np.ascontiguousarray(np.concatenate([qbr1T[c * 32:(c + 1) * 32], qbr2T[c * 32:(c + 1) * 32]], axis=0)),
            kbT=np.ascontiguousarray(kbnT[ch]), krT=krT, vb=np.ascontiguousarray(vb[:, ch]))
        maps.append(m)
    return maps


def _merge_maps(inp, l, ra, rb):
    haT = np.empty((1024, SEQ), np.float32)
    ybT = np.empty((1024, SEQ), ml_dtypes.bfloat16)
    ycT = np.empty((1024, SEQ), ml_dtypes.bfloat16)
    for c in range(NCORE):
        h, half = c // 2, c % 2
        haT[h * 256 + half * 128:h * 256 + (half + 1) * 128] = np.asarray(rb[c]["ha"]).T
        ybT[c * 128:(c + 1) * 128] = np.asarray(rb[c]["ybT"])
        ycT[c * 128:(c + 1) * 128] = np.asarray(rb[c]["ycT"])
    maps = []
    for c in range(NCORE):
        tk = slice(c * TC, (c + 1) * TC)
        m = dict(x1T_in=np.asarray(ra[c]["x1T"]), haT=np.ascontiguousarray(haT[:, tk]), ybT=np.ascontiguousarray(ybT[:, tk]), ycT=np.ascontiguousarray(ycT[:, tk]),
                 gm_m=inp["mix_norm"][l], w_in_m=inp["w_in"][l], gon=inp["mlstm_out_norm"][l], w_branch=inp["w_branch"][l], w_out=inp["w_out"][l])
        m.update(_ffn_w(inp, l, 2, "2"))
        maps.append(m)
    return maps


def kernel(**inputs):
    inp = {k: np.asarray(v) for k, v in inputs.items()}
    x = inp["x"][0]
    cs = [_rope_tables(np.arange(c * TC, (c + 1) * TC)) for c in range(NCORE)]
    maps = []
    for c in range(NCORE):
        m = dict(xT=np.ascontiguousarray(x[c * TC:(c + 1) * TC].T), cs=cs[c])
        m.update(_ffn_w(inp, 0, 1, "1")); m.update(_proj_w(inp, 0))
        maps.append(m)
    ra = _run("A", maps)
    rb = _run("B", _mixer_maps(inp, 0, ra))
    maps = _merge_maps(inp, 0, ra, rb)
    for c in range(NCORE):
        maps[c].update(_ffn_w(inp, 1, 1, "1")); maps[c].update(_proj_w(inp, 1)); maps[c]["cs"] = cs[c]
    ra = _run("C", maps)
    rb = _run("B", _mixer_maps(inp, 1, ra))
    maps = _merge_maps(inp, 1, ra, rb)
    for c in range(NCORE):
        maps[c]["gfin"] = inp["final_norm"]
    rd = _run("D", maps)
    out = np.concatenate([np.asarray(rd[c]["outT"]).T for c in range(NCORE)], axis=0)
    return np.ascontiguousarray(out[None].astype(np.float32))
```

```python
import contextlib
import numpy as np
import concourse.bass as bass
import concourse.mybir as mybir

F32 = mybir.dt.float32
BF16 = mybir.dt.bfloat16
AF = mybir.ActivationFunctionType
ALU = mybir.AluOpType
AX = mybir.AxisListType


class Buf:
    __slots__ = ("name", "w", "r", "rd", "excl")

    def __init__(self, name="", excl=False):
        self.name = name
        self.excl = excl
        self.w = None
        self.r = {}
        self.rd = []


class Op:
    __slots__ = ("eng", "fn", "deps", "is_dma", "key", "cnt", "marked", "raw")


class Sched:
    ENG = ("pe", "act", "dve", "pool", "sp")

    def __init__(self, nc, stack, same_engine_sync=True):
        self.nc = nc
        self.stack = stack
        self.ops = []
        self.engs = {"pe": nc.tensor, "act": nc.scalar, "dve": nc.vector,
                     "pool": nc.gpsimd, "sp": nc.sync}
        self.same_engine_sync = same_engine_sync
        self.dma_keys = {}
        self.nsb = 0

    def sbuf(self, name, shape, dtype):
        return self.stack.enter_context(self.nc.sbuf_tensor("sb_" + name, list(shape), dtype))

    def psum(self, name, shape, dtype=F32):
        return self.stack.enter_context(self.nc.psum_tensor("pp_" + name, list(shape), dtype))

    def op(self, eng, fn, reads=(), writes=(), dma=False):
        idx = len(self.ops)
        deps = {}
        if any(b.excl for b in reads):
            writes = tuple(writes) + tuple(b for b in reads if b.excl)
            reads = tuple(b for b in reads if not b.excl)
        for b in reads:
            if b.w is not None:
                deps[b.w] = True
        for b in writes:
            if b.w is not None:
                deps[b.w] = True
            for i in b.r.values():
                deps.setdefault(i, False)
            for i in b.rd:
                deps.setdefault(i, False)
        deps.pop(idx, None)
        o = Op()
        o.eng = eng; o.fn = fn; o.deps = deps; o.is_dma = dma; o.key = None
        o.cnt = 0; o.marked = False
        self.ops.append(o)
        for b in reads:
            if dma:
                b.rd.append(idx)
            else:
                b.r[eng] = idx
        for b in writes:
            b.w = idx; b.r = {}; b.rd = []
        return idx

    def dma(self, eng, out, in_, reads=(), writes=(), key=None, slow=False):
        if key is None:
            key = writes[0] if writes else reads[0]
        kk = (id(key), eng)
        kb = self.dma_keys.get(kk)
        if kb is None:
            kb = [Buf("k_" + key.name), None, 0, key]
            self.dma_keys[kk] = kb
        idx = self.op(eng, (lambda e: e.dma_start(out=out, in_=in_, allow_slow_non_contiguous=True)) if slow else (lambda e: e.dma_start(out=out, in_=in_)), reads=reads,
                      writes=tuple(writes) + (kb[0],), dma=True)
        self.ops[idx].key = kb
        return idx

    def _needs_wait(self, o, d, raw):
        od = self.ops[d]
        if od.is_dma:
            return True
        if od.eng != o.eng:
            return True
        if o.is_dma:
            return False
        if self.same_engine_sync and o.eng != "pe":
            return True
        return False

    def emit(self):
        nc = self.nc
        ops = self.ops
        for o in ops:
            for d, raw in o.deps.items():
                od = ops[d]
                need = self._needs_wait(o, d, raw)
                if o.is_dma and (not od.is_dma) and od.eng == o.eng:
                    need = True
                if need and not od.is_dma:
                    od.marked = True
        esem = {e: self.stack.enter_context(nc.semaphore("sem_" + e)) for e in self.ENG}
        ecnt = {e: 0 for e in self.ENG}
        for kb in self.dma_keys.values():
            kb[1] = self.stack.enter_context(nc.semaphore("semd%d" % len([1 for k in self.dma_keys.values() if k[1] is not None])))
        for o in ops:
            if o.is_dma:
                o.key[2] += 16
                o.cnt = o.key[2]
            elif o.marked:
                ecnt[o.eng] += 1
                o.cnt = ecnt[o.eng]
        seen = {e: {} for e in self.ENG}
        nwait = 0
        for o in ops:
            e = self.engs[o.eng]
            sn = seen[o.eng]
            waits = {}
            for d, raw in o.deps.items():
                od = ops[d]
                need = self._needs_wait(o, d, raw)
                if o.is_dma and (not od.is_dma) and od.eng == o.eng:
                    need = True
                if not need:
                    continue
                if od.is_dma:
                    sem = od.key[1]
                else:
                    sem = esem[od.eng]
                k = id(sem)
                if sn.get(k, 0) >= od.cnt:
                    continue
                if k not in waits or waits[k][1] < od.cnt:
                    waits[k] = (sem, od.cnt)
            for k, (sem, val) in waits.items():
                e.wait_ge(sem, val)
                sn[k] = val
                nwait += 1
            ins = o.fn(e)
            if o.is_dma:
                ins.then_inc(o.key[1], 16)
            elif o.marked:
                ins.then_inc(esem[o.eng], 1)
        sp = self.engs["sp"]
        for kb in self.dma_keys.values():
            if kb[2] > 0:
                sp.wait_ge(kb[1], kb[2])
        return dict(n_ops=len(ops), n_wait=nwait, counts=ecnt, n_dma_keys=len(self.dma_keys))

import math
import numpy as np

D = 2048
DFF = 5632
KC = D // 128
FC = DFF // 128
EPS = 1e-6


class Res:
    def __init__(self, S, T):
        self.S = S
        self.T = T
        self.TT = T // 512
        nc = S.nc
        self.x = S.sbuf("x", [128, KC, T], F32)
        self.xb = [Buf("x%d" % d) for d in range(KC)]
        self.u = S.sbuf("u", [128, KC, T], BF16)
        self.ub = [Buf("u%d" % d) for d in range(KC)]
        self.sq = [S.sbuf("sq%d" % i, [128, 512], F32) for i in range(2)]
        self.sqb = [Buf("sq%d" % i) for i in range(2)]
        self.rstd = S.sbuf("rstd", [128, T], F32)
        self.rstdb = Buf("rstd")
        self.ps = [S.psum("ps%d" % i, [128, 512]) for i in range(8)]
        self.psb = [Buf("ps%d" % i, excl=True) for i in range(8)]
        self.ones = S.sbuf("ones", [128, 128], F32)
        self.onesb = Buf("ones")
        S.op("pool", lambda e: e.memset(self.ones[:], 1.0), writes=[self.onesb])
        self.wgu = [[S.sbuf("wgu%d_%d" % (i, j), [128, KC, 256], BF16) for j in range(2)] for i in range(2)]
        self.wgub = [[Buf("wgu%d_%d" % (i, j)) for j in range(2)] for i in range(2)]
        self.wd = [S.sbuf("wd%d" % i, [128, 2, D], BF16) for i in range(2)]
        self.wdb = [Buf("wd%d" % i) for i in range(2)]
        self.h = [S.sbuf("h%d" % i, [128, 4, T], BF16) for i in range(2)]
        self.hb = [[Buf("h%d_%d" % (i, j)) for j in range(4)] for i in range(2)]
        self.sil = [S.sbuf("sil%d" % i, [128, 512], F32) for i in range(2)]
        self.silb = [Buf("sil%d" % i) for i in range(2)]
        self.cnt = {}

    def rot(self, name, n):
        v = self.cnt.get(name, 0)
        self.cnt[name] = v + 1
        return v % n


def load_vec_pp(S, name, dram_vec_ap, nchunks, scale=None):
    t = S.sbuf(name + "_sb", [128, nchunks], F32)
    b = Buf(name)
    S.dma("sp", t[:], dram_vec_ap.rearrange("(k p) -> p k", p=128), writes=[b], slow=True)
    if scale is not None:
        S.op("dve", lambda e: e.tensor_scalar(out=t[:], in0=t[:], scalar1=float(scale), scalar2=None, op0=ALU.mult),
             reads=[b], writes=[b])
    return t, b


def emit_rmsnorm_fm(S, R, g_t, g_b, src=None, srcb=None, dst=None, dstb=None, ps_id=6):
    T, TT = R.T, R.TT
    if src is None:
        src = [R.x[:, d, :] for d in range(KC)]; srcb = R.xb
    if dst is None:
        dst = [R.u[:, d, :] for d in range(KC)]; dstb = R.ub
    nchunks = len(src)
    Dn = nchunks * 128
    for tt in range(TT):
        sl = slice(tt * 512, (tt + 1) * 512)
        p = ps_id + (tt % 2)
        for d in range(nchunks):
            i = R.rot("sq", 2)
            S.op("act", lambda e, d=d, i=i, sl=sl: e.activation(out=R.sq[i][:, :], in_=src[d][:, sl], func=AF.Square),
                 reads=[srcb[d]], writes=[R.sqb[i]])
            S.op("pe", lambda e, d=d, i=i, p=p: e.matmul(R.ps[p][:, :], lhsT=R.ones[:, :], rhs=R.sq[i][:, :],
                                                         start=(d == 0), stop=(d == nchunks - 1)),
                 reads=[R.onesb, R.sqb[i]], writes=[R.psb[p]])
        S.op("dve", lambda e, sl=sl, p=p: e.tensor_scalar(out=R.rstd[:, sl], in0=R.ps[p][:, :],
                                                           scalar1=float(1.0 / Dn), scalar2=float(EPS), op0=ALU.mult, op1=ALU.add),
             reads=[R.psb[p]], writes=[R.rstdb])
        S.op("act", lambda e, sl=sl: e.activation(out=R.rstd[:, sl], in_=R.rstd[:, sl], func=AF.Sqrt),
             reads=[R.rstdb], writes=[R.rstdb])
        S.op("dve", lambda e, sl=sl: e.reciprocal(out=R.rstd[:, sl], in_=R.rstd[:, sl]),
             reads=[R.rstdb], writes=[R.rstdb])
    for d in range(nchunks):
        S.op("dve", lambda e, d=d: e.scalar_tensor_tensor(out=dst[d], in0=src[d], scalar=g_t[:, d:d + 1], in1=R.rstd[:, :],
                                                          op0=ALU.mult, op1=ALU.mult),
             reads=[srcb[d], g_b, R.rstdb], writes=[dstb[d]])


def emit_ffn(S, R, wg, wu, wd):
    T, TT = R.T, R.TT
    NG = FC // 4
    wg_v = wg.rearrange("(k p) f -> p k f", p=128)
    wu_v = wu.rearrange("(k p) f -> p k f", p=128)
    wd_v = wd.rearrange("(c p) d -> p c d", p=128)
    for g in range(NG):
        hi_ = R.rot("h", 2)
        for half in range(2):
            wi = R.rot("wgu", 2)
            c0 = g * 512 + half * 256
            S.dma("pool", R.wgu[wi][0][:], wg_v[:, :, c0:c0 + 256], writes=[R.wgub[wi][0]])
            S.dma("pool", R.wgu[wi][1][:], wu_v[:, :, c0:c0 + 256], writes=[R.wgub[wi][1]])
            for c2 in range(2):
                fc = half * 2 + c2
                for tt in range(TT):
                    pg = R.rot("psg", 2)
                    pu = 2 + pg
                    for k in range(KC):
                        S.op("pe", lambda e, k=k, wi=wi, c2=c2, tt=tt, pg=pg: e.matmul(
                            R.ps[pg][:, :], lhsT=R.wgu[wi][0][:, k, c2 * 128:(c2 + 1) * 128], rhs=R.u[:, k, tt * 512:(tt + 1) * 512],
                            start=(k == 0), stop=(k == KC - 1)),
                            reads=[R.wgub[wi][0], R.ub[k]], writes=[R.psb[pg]])
                    for k in range(KC):
                        S.op("pe", lambda e, k=k, wi=wi, c2=c2, tt=tt, pu=pu: e.matmul(
                            R.ps[pu][:, :], lhsT=R.wgu[wi][1][:, k, c2 * 128:(c2 + 1) * 128], rhs=R.u[:, k, tt * 512:(tt + 1) * 512],
                            start=(k == 0), stop=(k == KC - 1)),
                            reads=[R.wgub[wi][1], R.ub[k]], writes=[R.psb[pu]])
                    si = R.rot("sil", 2)
                    S.op("act", lambda e, si=si, pg=pg: e.activation(out=R.sil[si][:, :], in_=R.ps[pg][:, :], func=AF.Silu),
                         reads=[R.psb[pg]], writes=[R.silb[si]])
                    S.op("dve", lambda e, si=si, pu=pu, hi_=hi_, fc=fc, tt=tt: e.tensor_tensor(
                        out=R.h[hi_][:, fc, tt * 512:(tt + 1) * 512], in0=R.ps[pu][:, :], in1=R.sil[si][:, :], op=ALU.mult),
                        reads=[R.psb[pu], R.silb[si]], writes=[R.hb[hi_][fc]])
        for half in range(2):
            r0 = g * 4 + half * 2
            S.dma("pool", R.wd[half][:], wd_v[:, r0:r0 + 2, :], writes=[R.wdb[half]])
        for d in range(KC):
            for tt in range(TT):
                pd = 4 + R.rot("psd", 2)
                for fc in range(4):
                    S.op("pe", lambda e, d=d, tt=tt, pd=pd, fc=fc, hi_=hi_: e.matmul(
                        R.ps[pd][:, :], lhsT=R.wd[fc // 2][:, fc % 2, d * 128:(d + 1) * 128], rhs=R.h[hi_][:, fc, tt * 512:(tt + 1) * 512],
                        start=(fc == 0), stop=(fc == 3)),
                        reads=[R.wdb[fc // 2], R.hb[hi_][fc]], writes=[R.psb[pd]])
                S.op("dve", lambda e, d=d, tt=tt, pd=pd: e.scalar_tensor_tensor(
                    out=R.x[:, d, tt * 512:(tt + 1) * 512], in0=R.ps[pd][:, :], scalar=0.5, in1=R.x[:, d, tt * 512:(tt + 1) * 512],
                    op0=ALU.mult, op1=ALU.add),
                    reads=[R.psb[pd], R.xb[d]], writes=[R.xb[d]])


O_AQ, O_AK, O_AV, O_AI, O_AF, O_AO = 0, 512, 1024, 2048, 2052, 2056
O_BCQ, O_BCKV, O_BKR, O_CX, O_G = 3080, 3464, 3720, 3784, 4808
N_IN = 10952


class ResA:
    def __init__(self, S, R):
        T = R.T
        self.wt = [R.wgu[0][0], R.wgu[0][1], R.wgu[1][0], R.wgu[1][1]]
        self.wtb = [R.wgub[0][0], R.wgub[0][1], R.wgub[1][0], R.wgub[1][1]]
        self.stf = R.sil
        self.stfb = R.silb
        self.stb = [S.sbuf("stb%d" % i, [128, 512], BF16) for i in range(3)]
        self.stbb = [Buf("stb%d" % i) for i in range(3)]
        self.stm = [R.h[0][:, 0:2, :].rearrange("p a (b c) -> p (a b) c", c=256), R.h[0][:, 2:4, :].rearrange("p a (b c) -> p (a b) c", c=256)]
        self.stmb = [[R.hb[0][0], R.hb[0][1]], [R.hb[0][2], R.hb[0][3]]]
        self.gst = S.sbuf("gst", [128, T // 128, 8], F32)
        self.gstb = Buf("gst")
        lat4 = S.sbuf("lat4", [128, T], F32)
        latn4 = S.sbuf("latn4", [128, T], BF16)
        self.lat = [R.wd[0][:, 0, :].bitcast(F32), R.wd[0][:, 1, :].bitcast(F32), R.wd[1][:, 0, :].bitcast(F32), R.wd[1][:, 1, :].bitcast(F32), lat4[:, :]]
        self.latb = [R.wdb[0], R.wdb[0], R.wdb[1], R.wdb[1], Buf("lat4")]
        self.latn = [R.h[1][:, 0, :], R.h[1][:, 1, :], R.h[1][:, 2, :], R.h[1][:, 3, :], latn4[:, :]]
        self.latnb = [R.hb[1][0], R.hb[1][1], R.hb[1][2], R.hb[1][3], Buf("latn4")]
        self.cs = S.sbuf("cs_sb", [32, 2, 512], F32)
        self.csb = Buf("cs")
        rt2 = [S.sbuf("rt%d" % i, [32, 512], F32) for i in range(2)]
        self.rt = [R.sq[0][:32, :], R.sq[1][:32, :], rt2[0][:, :], rt2[1][:, :]]
        self.rtb = [R.sqb[0], R.sqb[1], Buf("rt2"), Buf("rt3")]


def emit_proj_fm(S, R, A, w_in, col0, ncols, evac, msz=128):
    w_v = w_in.rearrange("(k p) f -> p k f", p=128)
    c = 0
    ci = 0
    while c < ncols:
        wcols = min(256, ncols - c)
        wi = R.rot("wt", 4)
        S.dma("pool", A.wt[wi][:, :, :wcols], w_v[:, :, col0 + c:col0 + c + wcols], writes=[A.wtb[wi]])
        cc = 0
        while cc < wcols:
            m = min(msz, wcols - cc)
            for tt in range(R.TT):
                p = R.rot("psA", 4)
                for k in range(KC):
                    S.op("pe", lambda e, k=k, wi=wi, cc=cc, m=m, tt=tt, p=p: e.matmul(
                        R.ps[p][:m, :], lhsT=A.wt[wi][:, k, cc:cc + m], rhs=R.u[:, k, tt * 512:(tt + 1) * 512],
                        start=(k == 0), stop=(k == KC - 1)),
                        reads=[A.wtb[wi], R.ub[k]], writes=[R.psb[p]])
                evac(R.ps[p], R.psb[p], ci, tt, m)
            cc += m
            ci += 1
        c += wcols


def emit_proj_tm(S, R, A, w_in, col0, ncols, out_dram, t0):
    w_v = w_in.rearrange("(k p) f -> p k f", p=128)
    NB = R.T // 128
    for c in range(0, ncols, 256):
        wi = R.rot("wt", 4)
        S.dma("pool", A.wt[wi][:, :, :], w_v[:, :, col0 + c:col0 + c + 256], writes=[A.wtb[wi]])
        si = R.rot("stm", 2)
        for b in range(NB):
            p = R.rot("psA", 4)
            for k in range(KC):
                S.op("pe", lambda e, k=k, wi=wi, b=b, p=p: e.matmul(
                    R.ps[p][:, :256], lhsT=R.u[:, k, b * 128:(b + 1) * 128], rhs=A.wt[wi][:, k, :],
                    start=(k == 0), stop=(k == KC - 1)),
                    reads=[A.wtb[wi], R.ub[k]], writes=[R.psb[p]])
            if b % 2 == 0:
                S.op("act", lambda e, b=b, p=p, si=si: e.copy(out=A.stm[si][:, b, :], in_=R.ps[p][:, :256]),
                     reads=[R.psb[p]], writes=A.stmb[si])
            else:
                S.op("dve", lambda e, b=b, p=p, si=si: e.tensor_copy(out=A.stm[si][:, b, :], in_=R.ps[p][:, :256]),
                     reads=[R.psb[p]], writes=A.stmb[si])
        S.dma("sp", out_dram[t0:t0 + R.T, c:c + 256].rearrange("(b p) c -> p b c", p=128), A.stm[si],
              reads=A.stmb[si], key=A.stmb[si][0])


def emit_phase_a_proj(S, R, A, P, t0):
    T, TT = R.T, R.TT
    w_in = P["w_in"]
    w_v = w_in.rearrange("(k p) f -> p k f", p=128)

    def evac_bf16_out(dram, scale=None):
        def f(ps, psb, ci, tt, m):
            si = R.rot("stb", 3)
            if scale is None:
                S.op("act", lambda e: e.copy(out=A.stb[si][:m, :], in_=ps[:m, :]), reads=[psb], writes=[A.stbb[si]])
            else:
                S.op("act", lambda e: e.mul(out=A.stb[si][:m, :], in_=ps[:m, :], mul=float(scale)), reads=[psb], writes=[A.stbb[si]])
            S.dma("sp", dram[ci * 128:ci * 128 + m, t0 + tt * 512:t0 + (tt + 1) * 512], A.stb[si][:m, :], reads=[A.stbb[si]])
        return f

    emit_proj_fm(S, R, A, w_in, O_AQ, 512, evac_bf16_out(P["qaT"], 128 ** -0.5))
    emit_proj_fm(S, R, A, w_in, O_AK, 512, evac_bf16_out(P["kaT"]))
    emit_proj_tm(S, R, A, w_in, O_AK, 512, P["ka"], t0)
    emit_proj_tm(S, R, A, w_in, O_AV, 1024, P["va"], t0)
    wi = R.rot("wt", 4)
    S.dma("pool", A.wt[wi][:, :, :8], w_v[:, :, O_AI:O_AI + 8], writes=[A.wtb[wi]], slow=True)
    for b in range(T // 128):
        p = R.rot("psA", 4)
        for k in range(KC):
            S.op("pe", lambda e, k=k, wi=wi, b=b, p=p: e.matmul(
                R.ps[p][:, :8], lhsT=R.u[:, k, b * 128:(b + 1) * 128], rhs=A.wt[wi][:, k, :8],
                start=(k == 0), stop=(k == KC - 1)),
                reads=[A.wtb[wi], R.ub[k]], writes=[R.psb[p]])
        S.op("dve", lambda e, b=b, p=p: e.tensor_copy(out=A.gst[:, b, :], in_=R.ps[p][:, :8]), reads=[R.psb[p]], writes=[A.gstb])
    S.dma("sp", P["gif"][t0:t0 + T, :].rearrange("(b p) c -> p b c", p=128), A.gst[:, :, :], reads=[A.gstb], slow=True)

    def evac_cx(ps, psb, ci, tt, m):
        si = R.rot("stf", 2)
        S.op("act", lambda e: e.copy(out=A.stf[si][:m, :], in_=ps[:m, :]), reads=[psb], writes=[A.stfb[si]])
        S.dma("sp", P["cxT"][ci * 128:ci * 128 + m, t0 + tt * 512:t0 + (tt + 1) * 512], A.stf[si][:m, :], reads=[A.stfb[si]])
    emit_proj_fm(S, R, A, w_in, O_CX, 1024, evac_cx)

    def evac_lat(ps, psb, ci, tt, m):
        S.op("act", lambda e: e.copy(out=A.lat[ci][:, tt * 512:(tt + 1) * 512], in_=ps[:, :]), reads=[psb], writes=[A.latb[ci]])
    emit_proj_fm(S, R, A, w_in, O_BCQ, 640, evac_lat)
    emit_rmsnorm_fm(S, R, A.gq, A.gqb, src=A.lat[0:3], srcb=A.latb[0:3], dst=A.latn[0:3], dstb=A.latnb[0:3])
    emit_rmsnorm_fm(S, R, A.gkv, A.gkvb, src=A.lat[3:5], srcb=A.latb[3:5], dst=A.latn[3:5], dstb=A.latnb[3:5])

    def rope(psA, psAb, psB, psBb, tt, dst1, dst2, h):
        S.op("dve", lambda e: e.tensor_tensor(out=A.rt[0][:, :], in0=psA[:32, :], in1=A.cs[:, 0, :], op=ALU.mult), reads=[psAb, A.csb], writes=[A.rtb[0]])
        S.op("dve", lambda e: e.tensor_tensor(out=A.rt[1][:, :], in0=psB[:32, :], in1=A.cs[:, 1, :], op=ALU.mult), reads=[psBb, A.csb], writes=[A.rtb[1]])
        S.op("dve", lambda e: e.tensor_tensor(out=A.rt[2][:, :], in0=psA[:32, :], in1=A.cs[:, 1, :], op=ALU.mult), reads=[psAb, A.csb], writes=[A.rtb[2]])
        S.op("dve", lambda e: e.tensor_tensor(out=A.rt[3][:, :], in0=psB[:32, :], in1=A.cs[:, 0, :], op=ALU.mult), reads=[psBb, A.csb], writes=[A.rtb[3]])
        s1 = R.rot("stb", 3)
        S.op("dve", lambda e: e.tensor_tensor(out=A.stb[s1][:32, :], in0=A.rt[0][:, :], in1=A.rt[1][:, :], op=ALU.subtract), reads=[A.rtb[0], A.rtb[1]], writes=[A.stbb[s1]])
        S.dma("sp", dst1[h * 32:(h + 1) * 32, t0 + tt * 512:t0 + (tt + 1) * 512], A.stb[s1][:32, :], reads=[A.stbb[s1]])
        s2 = R.rot("stb", 3)
        S.op("dve", lambda e: e.tensor_tensor(out=A.stb[s2][:32, :], in0=A.rt[2][:, :], in1=A.rt[3][:, :], op=ALU.add), reads=[A.rtb[2], A.rtb[3]], writes=[A.stbb[s2]])
        S.dma("sp", dst2[h * 32:(h + 1) * 32, t0 + tt * 512:t0 + (tt + 1) * 512], A.stb[s2][:32, :], reads=[A.stbb[s2]])

    wuq_v = P["w_uq"].rearrange("(k p) f -> p k f", p=128)
    wukv_v = P["w_ukv"].rearrange("(k p) f -> p k f", p=128)

    def wt_view(wi, k, c):
        return A.wt[wi].rearrange("p k c -> p (k c)")[:, :k * c].rearrange("p (k c) -> p k c", k=k)

    for tt in range(TT):
        sl = slice(tt * 512, (tt + 1) * 512)
        S.dma("sp", A.cs[:, :, :], P["cs"][:, :, t0 + tt * 512:t0 + (tt + 1) * 512], writes=[A.csb])
        wk = R.rot("wt", 4)
        S.dma("pool", A.wt[wk][:, :, :64], w_v[:, :, O_BKR:O_BKR + 64], writes=[A.wtb[wk]])
        pa = R.rot("psA", 4); pb = R.rot("psA", 4)
        for (pp, c0) in ((pa, 0), (pb, 32)):
            for k in range(KC):
                S.op("pe", lambda e, k=k, pp=pp, c0=c0, sl=sl, wk=wk: e.matmul(
                    R.ps[pp][:32, :], lhsT=A.wt[wk][:, k, c0:c0 + 32], rhs=R.u[:, k, sl],
                    start=(k == 0), stop=(k == KC - 1)), reads=[A.wtb[wk], R.ub[k]], writes=[R.psb[pp]])
        rope(R.ps[pa], R.psb[pa], R.ps[pb], R.psb[pb], tt, P["kbr1T"], P["kbr2T"], 0)
        for hp in range(4):
            wq = R.rot("wt", 4)
            wqv = wt_view(wq, 3, 384)
            S.dma("pool", wqv, wuq_v[:, :, hp * 384:(hp + 1) * 384], writes=[A.wtb[wq]])
            for h2 in range(2):
                h = hp * 2 + h2
                p = R.rot("psA", 4)
                for k in range(3):
                    S.op("pe", lambda e, k=k, p=p, h2=h2, sl=sl, wqv=wqv: e.matmul(R.ps[p][:, :], lhsT=wqv[:, k, h2 * 192:h2 * 192 + 128], rhs=A.latn[k][:, sl],
                                                                     start=(k == 0), stop=(k == 2)), reads=[A.wtb[wq], A.latnb[k]], writes=[R.psb[p]])
                evac_bf16_out(P["qbnT"])(R.ps[p], R.psb[p], h, tt, 128)
                pa = R.rot("psA", 4); pb = R.rot("psA", 4)
                for (pp, c0) in ((pa, 128), (pb, 160)):
                    for k in range(3):
                        S.op("pe", lambda e, k=k, pp=pp, c0=c0, h2=h2, sl=sl, wqv=wqv: e.matmul(R.ps[pp][:32, :], lhsT=wqv[:, k, h2 * 192 + c0:h2 * 192 + c0 + 32], rhs=A.latn[k][:, sl],
                                                                                 start=(k == 0), stop=(k == 2)), reads=[A.wtb[wq], A.latnb[k]], writes=[R.psb[pp]])
                rope(R.ps[pa], R.psb[pa], R.ps[pb], R.psb[pb], tt, P["qbr1T"], P["qbr2T"], h)
    wkv = R.rot("wt", 4)
    wkvv = wt_view(wkv, 2, 2048)
    S.dma("pool", wkvv, wukv_v, writes=[A.wtb[wkv]])
    for h in range(8):
        for tt in range(TT):
            sl = slice(tt * 512, (tt + 1) * 512)
            p = R.rot("psA", 4)
            for k in range(2):
                S.op("pe", lambda e, k=k, p=p, h=h, sl=sl: e.matmul(R.ps[p][:, :], lhsT=wkvv[:, k, h * 256:h * 256 + 128], rhs=A.latn[3 + k][:, sl],
                                                                 start=(k == 0), stop=(k == 1)), reads=[A.wtb[wkv], A.latnb[3 + k]], writes=[R.psb[p]])
            evac_bf16_out(P["kbnT"])(R.ps[p], R.psb[p], h, tt, 128)
    wv = wkvv.rearrange("p k (h two c) -> p k h two c", two=2, c=128)
    for hq in range(4):
        si = R.rot("stm", 2)
        for b in range(T // 128):
            p = R.rot("psA", 4)
            for k in range(2):
                S.op("pe", lambda e, k=k, p=p, b=b, hq=hq: e.matmul(R.ps[p][:, :256], lhsT=A.latn[3 + k][:, b * 128:(b + 1) * 128], rhs=wv[:, k, 2 * hq:2 * hq + 2, 1, :],
                                                                 start=(k == 0), stop=(k == 1)), reads=[A.wtb[wkv], A.latnb[3 + k]], writes=[R.psb[p]])
            S.op("act", lambda e, b=b, p=p, si=si: e.copy(out=A.stm[si][:, b, :], in_=R.ps[p][:, :256]), reads=[R.psb[p]], writes=A.stmb[si])
        S.dma("sp", P["vb"][t0:t0 + T, hq * 256:(hq + 1) * 256].rearrange("(b p) c -> p b c", p=128), A.stm[si], reads=A.stmb[si], key=A.stmb[si][0])


def emit_merge(S, R, P, t0, gm_t, gm_b, gon_t, gon_b):
    T, TT = R.T, R.TT
    w_v = P["w_in"].rearrange("(k p) f -> p k f", p=128)
    wt = [R.wgu[0][0], R.wgu[0][1], R.wgu[1][0], R.wgu[1][1]]
    wtb = [R.wgub[0][0], R.wgub[0][1], R.wgub[1][0], R.wgub[1][1]]
    y = [R.h[0][:, c, :] for c in range(4)] + [R.h[1][:, c, :] for c in range(4)]
    yb = [R.hb[0][c] for c in range(4)] + [R.hb[1][c] for c in range(4)]
    ha = [R.wd[0][:, 0, :].bitcast(F32), R.wd[0][:, 1, :].bitcast(F32)]
    hab = [R.wdb[0], R.wdb[0]]
    hn = [R.wd[1][:, 0, :].bitcast(F32), R.wd[1][:, 1, :].bitcast(F32)]
    hnb = [R.wdb[1], R.wdb[1]]
    emit_rmsnorm_fm(S, R, gm_t, gm_b)
    for h in range(4):
        for c in range(2):
            S.dma("sp", ha[c], P["haT"][h * 256 + c * 128:h * 256 + (c + 1) * 128, t0:t0 + T], writes=[hab[c]], key=hab[c])
        emit_rmsnorm_fm(S, R, gon_t[:, 2 * h:2 * h + 2], gon_b, src=ha, srcb=hab, dst=hn, dstb=hnb)
        wi = R.rot("wt", 4)
        S.dma("pool", wt[wi][:, :, :], w_v[:, :, O_AO + h * 256:O_AO + (h + 1) * 256], writes=[wtb[wi]])
        for c in range(2):
            for tt in range(TT):
                sl = slice(tt * 512, (tt + 1) * 512)
                p = R.rot("psC", 2)
                for k in range(KC):
                    S.op("pe", lambda e, k=k, wi=wi, c=c, sl=sl, p=p: e.matmul(R.ps[p][:, :], lhsT=wt[wi][:, k, c * 128:(c + 1) * 128], rhs=R.u[:, k, sl],
                                                                          start=(k == 0), stop=(k == KC - 1)), reads=[wtb[wi], R.ub[k]], writes=[R.psb[p]])
                si = R.rot("sil", 2)
                S.op("act", lambda e, si=si, p=p: e.activation(out=R.sil[si][:, :], in_=R.ps[p][:, :], func=AF.Sigmoid), reads=[R.psb[p]], writes=[R.silb[si]])
                S.op("dve", lambda e, si=si, h=h, c=c, sl=sl: e.tensor_tensor(out=y[2 * h + c][:, sl], in0=R.sil[si][:, :], in1=hn[c][:, sl], op=ALU.mult),
                     reads=[R.silb[si], hnb[c]], writes=[yb[2 * h + c]])
    for j in range(3):
        if j > 0:
            src = P["ybT"] if j == 1 else P["ycT"]
            for c in range(8):
                S.dma("sp", y[c], src[c * 128:(c + 1) * 128, t0:t0 + T], writes=[yb[c]], key=yb[c])
        wb_v = P["w_branch"][j].rearrange("(k p) f -> p k f", p=128)
        for d2 in range(KC // 2):
            wg_i = R.rot("wt", 4)
            S.dma("pool", wt[wg_i][:, :, :], w_v[:, :, O_G + j * D + d2 * 256:O_G + j * D + (d2 + 1) * 256], writes=[wtb[wg_i]])
            wb_i = R.rot("wt", 4)
            wbv = wt[wb_i].rearrange("p k c -> p (k c)")[:, :8 * 256].rearrange("p (k c) -> p k c", k=8)
            S.dma("pool", wbv, wb_v[:, :, d2 * 256:(d2 + 1) * 256], writes=[wtb[wb_i]])
            for c in range(2):
                d = d2 * 2 + c
                for tt in range(TT):
                    sl = slice(tt * 512, (tt + 1) * 512)
                    pg = R.rot("psC", 2)
                    pp = 2 + R.rot("psC2", 2)
                    for k in range(KC):
                        S.op("pe", lambda e, k=k, wg_i=wg_i, c=c, sl=sl, pg=pg: e.matmul(R.ps[pg][:, :], lhsT=wt[wg_i][:, k, c * 128:(c + 1) * 128], rhs=R.u[:, k, sl],
                                                                                   start=(k == 0), stop=(k == KC - 1)), reads=[wtb[wg_i], R.ub[k]], writes=[R.psb[pg]])
                    for k in range(8):
                        S.op("pe", lambda e, k=k, wbv=wbv, c=c, sl=sl, pp=pp: e.matmul(R.ps[pp][:, :], lhsT=wbv[:, k, c * 128:(c + 1) * 128], rhs=y[k][:, sl],
                                                                                 start=(k == 0), stop=(k == 7)), reads=[wtb[wb_i], yb[k]], writes=[R.psb[pp]])
                    si = R.rot("sil", 2)
                    S.op("act", lambda e, si=si, pg=pg: e.activation(out=R.sil[si][:, :], in_=R.ps[pg][:, :], func=AF.Sigmoid), reads=[R.psb[pg]], writes=[R.silb[si]])
                    if j == 0:
                        S.op("dve", lambda e, si=si, pp=pp, d=d, sl=sl: e.tensor_tensor(out=R.x[:, d, sl], in0=R.ps[pp][:, :], in1=R.sil[si][:, :], op=ALU.mult),
                             reads=[R.psb[pp], R.silb[si]], writes=[R.xb[d]])
                    else:
                        qi = R.rot("sq", 2)
                        S.op("dve", lambda e, si=si, pp=pp, qi=qi: e.tensor_tensor(out=R.sq[qi][:, :], in0=R.ps[pp][:, :], in1=R.sil[si][:, :], op=ALU.mult),
                             reads=[R.psb[pp], R.silb[si]], writes=[R.sqb[qi]])
                        S.op("dve", lambda e, qi=qi, d=d, sl=sl: e.tensor_tensor(out=R.x[:, d, sl], in0=R.x[:, d, sl], in1=R.sq[qi][:, :], op=ALU.add),
                             reads=[R.sqb[qi], R.xb[d]], writes=[R.xb[d]])
    for d in range(KC):
        S.op("act", lambda e, d=d: e.copy(out=R.u[:, d, :], in_=R.x[:, d, :]), reads=[R.xb[d]], writes=[R.ub[d]])
    for d in range(KC):
        S.dma("sp", R.x[:, d, :], P["x1T"][d * 128:(d + 1) * 128, t0:t0 + T], writes=[R.xb[d]])
    wo_v = P["w_out"].rearrange("(k p) f -> p k f", p=128)
    for d2 in range(KC // 2):
        wi = R.rot("wt", 4)
        S.dma("pool", wt[wi][:, :, :], wo_v[:, :, d2 * 256:(d2 + 1) * 256], writes=[wtb[wi]])
        for c in range(2):
            d = d2 * 2 + c
            for tt in range(TT):
                sl = slice(tt * 512, (tt + 1) * 512)
                p = R.rot("psC", 2)
                for k in range(KC):
                    S.op("pe", lambda e, k=k, wi=wi, c=c, sl=sl, p=p: e.matmul(R.ps[p][:, :], lhsT=wt[wi][:, k, c * 128:(c + 1) * 128], rhs=R.u[:, k, sl],
                                                                          start=(k == 0), stop=(k == KC - 1)), reads=[wtb[wi], R.ub[k]], writes=[R.psb[p]])
                S.op("dve", lambda e, p=p, d=d, sl=sl: e.tensor_tensor(out=R.x[:, d, sl], in0=R.ps[p][:, :], in1=R.x[:, d, sl], op=ALU.add),
                     reads=[R.psb[p], R.xb[d]], writes=[R.xb[d]])

import math
import numpy as np


class ResB:
    def __init__(self, S, SEQ):
        self.S = S
        self.SEQ = SEQ
        self.psw = [S.psum("psw%d" % i, [128, 1024]) for i in range(2)]
        self.pswb = [Buf("psw%d" % i, excl=True) for i in range(2)]
        ps4 = [S.psum("psb%d" % i, [128, 512]) for i in range(4, 8)]
        self.ps = [self.psw[0][:, 0:512], self.psw[0][:, 512:1024], self.psw[1][:, 0:512], self.psw[1][:, 512:1024]] + ps4
        self.psb = [self.pswb[0], self.pswb[0], self.pswb[1], self.pswb[1]] + [Buf("psb%d" % i, excl=True) for i in range(4, 8)]
        self.cnt = {}

    def rot(self, name, n):
        v = self.cnt.get(name, 0)
        self.cnt[name] = v + 1
        return v % n


def load_pp(S, name, ap2d, shape, eng="sp"):
    t = S.sbuf(name + "_sb", shape, F32)
    b = Buf(name)
    S.dma(eng, t[:], ap2d, writes=[b], slow=True)
    return t, b


def gen_lru(S, B, P, TP=512):
    SEQ = B.SEQ
    nc = S.nc
    prm, prmb = load_pp(S, "lru_prm", P["lru_prm"], [128, 8])
    wa = S.sbuf("lru_wa_sb", [128, 128], BF16); wab = Buf("lru_wa")
    wx = S.sbuf("lru_wx_sb", [128, 128], BF16); wxb = Buf("lru_wx")
    S.dma("pool", wa[:], P["lru_wa"], writes=[wab])
    S.dma("pool", wx[:], P["lru_wx"], writes=[wxb])
    cv = S.sbuf("lru_cv", [128, 4], F32); cvb = Buf("lru_cv")
    S.op("act", lambda e: e.activation(out=cv[:, 2:3], in_=prm[:, 7:8], func=AF.Exp, scale=-1.0), reads=[prmb], writes=[cvb])
    S.op("dve", lambda e: e.tensor_scalar(out=cv[:, 2:3], in0=cv[:, 2:3], scalar1=1.0, scalar2=None, op0=ALU.add), reads=[cvb], writes=[cvb])
    S.op("act", lambda e: e.activation(out=cv[:, 3:4], in_=cv[:, 2:3], func=AF.Ln), reads=[cvb], writes=[cvb])
    S.op("dve", lambda e: e.tensor_scalar(out=cv[:, 0:1], in0=cv[:, 3:4], scalar1=-8.0, scalar2=None, op0=ALU.mult), reads=[cvb], writes=[cvb])
    S.op("dve", lambda e: e.tensor_scalar(out=cv[:, 1:2], in0=cv[:, 3:4], scalar1=-16.0, scalar2=None, op0=ALU.mult), reads=[cvb], writes=[cvb])
    xin = [S.sbuf("lru_x%d" % i, [128, 3 + TP], F32) for i in range(2)]
    xinb = [Buf("lru_x%d" % i) for i in range(2)]
    xc = S.sbuf("lru_xc", [128, TP], F32); xcb = Buf("lru_xc")
    xcbf = S.sbuf("lru_xcbf", [128, TP], BF16); xcbfb = Buf("lru_xcbf")
    r = S.sbuf("lru_r", [128, TP], F32); rb = Buf("lru_r")
    gi = S.sbuf("lru_gi", [128, TP], F32); gib = Buf("lru_gi")
    a = S.sbuf("lru_a", [128, TP], F32); ab = Buf("lru_a")
    hh = [S.sbuf("lru_h%d" % i, [128, TP], F32) for i in range(2)]
    hb = [Buf("lru_h%d" % i) for i in range(2)]
    ho = [S.sbuf("lru_ho%d" % i, [128, TP], BF16) for i in range(2)]
    hob = [Buf("lru_ho%d" % i) for i in range(2)]
    S.op("pool", lambda e: e.memset(xin[0][:, 0:3], 0.0), writes=[xinb[0]])
    NP = SEQ // TP
    for pc in range(NP):
        i = pc % 2
        t0 = pc * TP
        S.dma("sp", xin[i][:, 3:3 + TP], P["cxT"][:, t0:t0 + TP], writes=[xinb[i]])
        if pc + 1 < NP:
            S.op("pool", lambda e, i=i: e.tensor_copy(out=xin[1 - i][:, 0:3], in_=xin[i][:, TP:TP + 3]), reads=[xinb[i]], writes=[xinb[1 - i]])
        S.op("dve", lambda e, i=i: e.tensor_scalar(out=xc[:, :], in0=xin[i][:, 3:3 + TP], scalar1=prm[:, 3:4], scalar2=prm[:, 4:5], op0=ALU.mult, op1=ALU.add),
             reads=[xinb[i], prmb], writes=[xcb])
        for j in range(3):
            S.op("dve", lambda e, i=i, j=j: e.scalar_tensor_tensor(out=xc[:, :], in0=xin[i][:, j:j + TP], scalar=prm[:, j:j + 1], in1=xc[:, :], op0=ALU.mult, op1=ALU.add),
                 reads=[xinb[i], prmb, xcb], writes=[xcb])
        S.op("act", lambda e: e.copy(out=xcbf[:, :], in_=xc[:, :]), reads=[xcb], writes=[xcbfb])
        yield
        yield
        for tt in range(TP // 512):
            sl = slice(tt * 512, (tt + 1) * 512)
            S.op("pe", lambda e, sl=sl: e.matmul(B.ps[6][:, :], lhsT=wa[:, :], rhs=xcbf[:, sl], start=True, stop=True), reads=[wab, xcbfb], writes=[B.psb[6]])
            S.op("pe", lambda e, sl=sl: e.matmul(B.ps[7][:, :], lhsT=wx[:, :], rhs=xcbf[:, sl], start=True, stop=True), reads=[wxb, xcbfb], writes=[B.psb[7]])
            S.op("act", lambda e, sl=sl: e.activation(out=r[:, sl], in_=B.ps[6][:, :], func=AF.Sigmoid, bias=prm[:, 5:6]), reads=[B.psb[6], prmb], writes=[rb])
            S.op("act", lambda e, sl=sl: e.activation(out=gi[:, sl], in_=B.ps[7][:, :], func=AF.Sigmoid, bias=prm[:, 6:7]), reads=[B.psb[7], prmb], writes=[gib])
        yield
        yield
        S.op("act", lambda e: e.activation(out=a[:, :], in_=r[:, :], func=AF.Exp, scale=cv[:, 0:1]), reads=[rb, cvb], writes=[ab])
        S.op("act", lambda e: e.activation(out=r[:, :], in_=r[:, :], func=AF.Exp, scale=cv[:, 1:2]), reads=[rb, cvb], writes=[rb])
        S.op("dve", lambda e: e.tensor_scalar(out=r[:, :], in0=r[:, :], scalar1=-1.0, scalar2=1.0, op0=ALU.mult, op1=ALU.add), reads=[rb], writes=[rb])
        S.op("act", lambda e: e.activation(out=r[:, :], in_=r[:, :], func=AF.Sqrt), reads=[rb], writes=[rb])
        S.op("dve", lambda e: e.tensor_tensor(out=gi[:, :], in0=gi[:, :], in1=xc[:, :], op=ALU.mult), reads=[gib, xcb], writes=[gib])
        S.op("dve", lambda e: e.tensor_tensor(out=gi[:, :], in0=gi[:, :], in1=r[:, :], op=ALU.mult), reads=[gib, rb], writes=[gib])
        if pc == 0:
            S.op("dve", lambda e, i=i: e.tensor_tensor_scan(out=hh[i][:, :], data0=a[:, :], data1=gi[:, :], initial=0.0, op0=ALU.mult, op1=ALU.add),
                 reads=[ab, gib], writes=[hb[i]])
        else:
            S.op("dve", lambda e, i=i: e.tensor_tensor_scan(out=hh[i][:, :], data0=a[:, :], data1=gi[:, :], initial=hh[1 - i][:, TP - 1:TP], op0=ALU.mult, op1=ALU.add),
                 reads=[ab, gib, hb[1 - i]], writes=[hb[i]])
        S.op("act", lambda e, i=i: e.copy(out=ho[i][:, :], in_=hh[i][:, :]), reads=[hb[i]], writes=[hob[i]])
        S.dma("sp", P["ycT"][:, t0:t0 + TP], ho[i][:, :], reads=[hob[i]])
        yield


def gen_mlstm(S, B, P, NCP=16):
    SEQ = B.SEQ
    NCH = SEQ // 128
    NCP = min(NCP, NCH)
    tri, trib = load_pp(S, "ml_tri", P["tri"], [128, 128])
    gbias, gbb = load_pp(S, "ml_gb", P["gbias"], [128, 2])
    ones = S.sbuf("ml_ones", [128, 128], F32); onesb = Buf("ml_ones")
    S.op("pool", lambda e: e.memset(ones[:], 1.0), writes=[onesb])
    S.op("dve", lambda e: e.tensor_scalar(out=gbias[:, 1:2], in0=gbias[:, 1:2], scalar1=-1.0, scalar2=None, op0=ALU.mult), reads=[gbb], writes=[gbb])
    qT = [S.sbuf("ml_qT%d" % i, [128, NCP * 128], BF16) for i in range(2)]; qTb = [Buf("ml_qT%d" % i) for i in range(2)]
    kT = [S.sbuf("ml_kT%d" % i, [128, NCP * 128], BF16) for i in range(2)]; kTb = [Buf("ml_kT%d" % i) for i in range(2)]
    kk = [S.sbuf("ml_k%d" % i, [128, NCP, 128], BF16) for i in range(2)]; kkb = [Buf("ml_k%d" % i) for i in range(2)]
    vv = [S.sbuf("ml_v%d" % i, [128, NCP, 128], BF16) for i in range(2)]; vvb = [Buf("ml_v%d" % i) for i in range(2)]
    gg = [S.sbuf("ml_g%d" % i, [128, NCP, 2], F32) for i in range(2)]; ggb = [Buf("ml_g%d" % i) for i in range(2)]
    nlf = S.sbuf("ml_nlf", [128, NCP], F32); nlfb = Buf("ml_nlf")
    ii = S.sbuf("ml_i", [128, NCP], F32); iib = Buf("ml_i")
    wv = S.sbuf("ml_wv", [128, NCP], F32); wvb = Buf("ml_wv")
    wq = S.sbuf("ml_wq", [128, NCP], F32); wqb = Buf("ml_wq")
    wL = S.sbuf("ml_wL", [128, NCP], F32); wLb = Buf("ml_wL")
    vs = S.sbuf("ml_vs", [128, NCP, 129], BF16); vsb = Buf("ml_vs")
    stb = S.sbuf("ml_st", [128, NCP, 128], BF16); stbb = [Buf("ml_st%d" % c) for c in range(NCP)]
    C = S.sbuf("ml_C", [128, 129], F32); Cb = Buf("ml_C")
    Cbf = S.sbuf("ml_Cbf", [128, NCP + 1, 129], BF16); Cbfb = [Buf("ml_Cbf%d" % c) for c in range(NCP + 1)]
    oall = S.sbuf("ml_oall", [128, NCP, 129], F32); oallb = Buf("ml_oall")
    t1 = S.sbuf("ml_t1", [128, NCP], F32); t1b = Buf("ml_t1")
    _h0 = S.sbuf("ml_ho0", [128, NCP, 128], F32); _hb0 = Buf("ml_ho0")
    hout = [_h0, _h0]; houtb = [_hb0, _hb0]
    S.op("pool", lambda e: e.memset(C[:], 0.0), writes=[Cb])
    PB = (6, 7, 6, 7)
    for pc in range(NCH // NCP):
        i = pc % 2
        t0 = pc * NCP * 128
        T = NCP * 128
        S.dma("sp", qT[i][:, :], P["qaT"][:, t0:t0 + T], writes=[qTb[i]])
        S.dma("sp", kT[i][:, :], P["kaT"][:, t0:t0 + T], writes=[kTb[i]])
        S.dma("sp", kk[i][:, :, :], P["ka"][t0:t0 + T, :].rearrange("(b p) d -> p b d", p=128), writes=[kkb[i]])
        S.dma("sp", vv[i][:, :, :], P["va"][t0:t0 + T, :].rearrange("(b p) d -> p b d", p=128), writes=[vvb[i]])
        S.dma("sp", gg[i][:, :, :], P["gif"][t0:t0 + T, :].rearrange("(b p) d -> p b d", p=128), writes=[ggb[i]], slow=True)
        S.op("act", lambda e, i=i: e.activation(out=nlf[:, :], in_=gg[i][:, :, 1], func=AF.Exp, scale=-1.0, bias=gbias[:, 1:2]), reads=[ggb[i], gbb], writes=[nlfb])
        S.op("dve", lambda e: e.tensor_scalar(out=nlf[:, :], in0=nlf[:, :], scalar1=1.0, scalar2=None, op0=ALU.add), reads=[nlfb], writes=[nlfb])
        S.op("act", lambda e: e.activation(out=nlf[:, :], in_=nlf[:, :], func=AF.Ln), reads=[nlfb], writes=[nlfb])
        S.op("dve", lambda e, i=i: e.tensor_scalar(out=ii[:, :], in0=gg[i][:, :, 0], scalar1=gbias[:, 0:1], scalar2=None, op0=ALU.add), reads=[ggb[i], gbb], writes=[iib])
        pcum = B.ps[PB[1]]; pcumb = B.psb[PB[1]]
        S.op("pe", lambda e: e.matmul(pcum[:, 0:NCP], lhsT=tri[:, :], rhs=nlf[:, :], start=True, stop=True), reads=[trib, nlfb], writes=[pcumb])
        S.op("pe", lambda e: e.matmul(pcum[:, 32:32 + NCP], lhsT=ones[:, :], rhs=nlf[:, :], start=True, stop=True), reads=[onesb, nlfb], writes=[pcumb])
        S.op("act", lambda e: e.activation(out=wq[:, :], in_=pcum[:, 0:NCP], func=AF.Exp, scale=-1.0), reads=[pcumb], writes=[wqb])
        S.op("act", lambda e: e.activation(out=wL[:, :], in_=pcum[:, 32:32 + NCP], func=AF.Exp, scale=-1.0), reads=[pcumb], writes=[wLb])
        S.op("act", lambda e: e.copy(out=wv[:, :], in_=pcum[:, 0:NCP]), reads=[pcumb], writes=[wvb])
        S.op("dve", lambda e: e.tensor_tensor(out=wv[:, :], in0=wv[:, :], in1=ii[:, :], op=ALU.add), reads=[wvb, iib], writes=[wvb])
        S.op("act", lambda e: e.activation(out=wv[:, :], in_=wv[:, :], func=AF.Exp), reads=[wvb], writes=[wvb])
        S.op("dve", lambda e, i=i: e.tensor_tensor(out=vs[:, :, 0:128], in0=vv[i][:, :, :], in1=wv[:, :].unsqueeze(2).to_broadcast([128, NCP, 128]), op=ALU.mult),
             reads=[vvb[i], wvb], writes=[vsb])
        S.op("dve", lambda e: e.tensor_copy(out=vs[:, :, 128:129], in_=wv[:, :].unsqueeze(2)), reads=[wvb], writes=[vsb])
        S.op("dve", lambda e: e.tensor_copy(out=Cbf[:, 0, :], in_=C[:, :]), reads=[Cb], writes=[Cbfb[0]])
        for c in range(NCP):
            pb = PB[c % 2]
            S.op("pe", lambda e, i=i, c=c, pb=pb: e.matmul(B.ps[pb][:, 0:129], lhsT=kk[i][:, c, :], rhs=vs[:, c, :], start=True, stop=True), reads=[kkb[i], vsb], writes=[B.psb[pb]])
            S.op("dve", lambda e, c=c: e.tensor_scalar(out=C[:, :], in0=C[:, :], scalar1=wL[:, c:c + 1], scalar2=None, op0=ALU.mult), reads=[Cb, wLb], writes=[Cb])
            S.op("dve", lambda e, c=c, pb=pb: e.scalar_tensor_tensor(out=C[:, :], in0=B.ps[pb][:, 0:129], scalar=wL[:, c:c + 1], in1=C[:, :], op0=ALU.mult, op1=ALU.add),
                 reads=[B.psb[pb], wLb, Cb], writes=[Cb])
            S.op("dve", lambda e, c=c: e.tensor_copy(out=Cbf[:, c + 1, :], in_=C[:, :]), reads=[Cb], writes=[Cbfb[c + 1]])
            yield
        for c in range(NCP):
            pb = PB[2 + c % 2]
            sl = slice(c * 128, (c + 1) * 128)
            S.op("pe", lambda e, i=i, sl=sl, pb=pb: e.matmul(B.ps[pb][:, 0:128], lhsT=kT[i][:, sl], rhs=qT[i][:, sl], start=True, stop=True), reads=[kTb[i], qTb[i]], writes=[B.psb[pb]])
            S.op("dve", lambda e, c=c, pb=pb: e.tensor_tensor(out=stb[:, c, :], in0=B.ps[pb][:, 0:128], in1=tri[:, :], op=ALU.mult), reads=[B.psb[pb], trib], writes=[stbb[c]])
            yield
        for c in range(NCP):
            pb = PB[c % 2]
            sl = slice(c * 128, (c + 1) * 128)
            S.op("pe", lambda e, c=c, pb=pb: e.matmul(B.ps[pb][:, 0:129], lhsT=stb[:, c, :], rhs=vs[:, c, :], start=True, stop=False), reads=[stbb[c], vsb], writes=[B.psb[pb]])
            S.op("pe", lambda e, i=i, c=c, sl=sl, pb=pb: e.matmul(B.ps[pb][:, 0:129], lhsT=qT[i][:, sl], rhs=Cbf[:, c, :], start=False, stop=True), reads=[qTb[i], Cbfb[c]], writes=[B.psb[pb]])
            S.op("act", lambda e, c=c, pb=pb: e.copy(out=oall[:, c, :], in_=B.ps[pb][:, 0:129]), reads=[B.psb[pb]], writes=[oallb])
            yield
        S.op("dve", lambda e: e.tensor_tensor(out=t1[:, :], in0=oall[:, :, 128], in1=wq[:, :], op=ALU.mult), reads=[oallb, wqb], writes=[t1b])
        S.op("act", lambda e: e.activation(out=t1[:, :], in_=t1[:, :], func=AF.Abs), reads=[t1b], writes=[t1b])
        S.op("dve", lambda e: e.tensor_scalar_max(out=t1[:, :], in0=t1[:, :], scalar1=1.0), reads=[t1b], writes=[t1b])
        S.op("dve", lambda e: e.reciprocal(out=t1[:, :], in_=t1[:, :]), reads=[t1b], writes=[t1b])
        S.op("dve", lambda e: e.tensor_tensor(out=t1[:, :], in0=t1[:, :], in1=wq[:, :], op=ALU.mult), reads=[t1b, wqb], writes=[t1b])
        S.op("dve", lambda e, i=i: e.tensor_tensor(out=hout[i][:, :, :], in0=oall[:, :, 0:128], in1=t1[:, :].unsqueeze(2).to_broadcast([128, NCP, 128]), op=ALU.mult),
             reads=[oallb, t1b], writes=[houtb[i]])
        S.dma("sp", P["ha"][t0:t0 + T, :].rearrange("(b p) d -> p b d", p=128), hout[i][:, :, :], reads=[houtb[i]])
        yield


def gen_mla(S, B, P):
    SEQ = B.SEQ
    NKB = SEQ // 128
    NQT = SEQ // 512
    kT = S.sbuf("at_kT", [128, SEQ], BF16); QCH = 4096; NQC = (SEQ + QCH - 1) // QCH; kTb = [Buf("at_kT%d" % c_) for c_ in range(NQC)]
    kr = S.sbuf("at_kr", [128, SEQ], BF16); krb = [Buf("at_kr%d" % c_) for c_ in range(NQC)]
    for c_ in range(NQC):
        S.op("pool", lambda e, c_=c_: e.memset(kr[64:128, c_ * QCH:min(SEQ, (c_ + 1) * QCH)], 0.0), writes=[krb[c_]])
    V = S.sbuf("at_V", [128, NKB, 128], BF16); Vb = [Buf("at_V%d" % c_) for c_ in range((NKB + 15) // 16)]
    msk = S.sbuf("at_msk", [128, 4, 512], BF16); mskb = Buf("at_msk")
    S.dma("sp", msk[:, :, :], P["masks"].rearrange("o k q -> k o q"), writes=[mskb])
    for c0 in range(0, SEQ, QCH):
        c1 = min(SEQ, c0 + QCH)
        S.dma("sp", kT[:, c0:c1], P["kbT"][:, c0:c1], writes=[kTb[c0 // QCH]])
        S.dma("sp", kr[0:64, c0:c1], P["krT"][:, c0:c1], writes=[krb[c0 // QCH]])
    vb_v = P["vb"].rearrange("(b p) d -> p b d", p=128)
    for b0 in range(0, NKB, 16):
        b1 = min(NKB, b0 + 16)
        S.dma("sp", V[:, b0:b1, 0:128], vb_v[:, b0:b1, :], writes=[Vb[b0 // 16]])
    qT = [S.sbuf("at_q%d" % i, [128, 512], BF16) for i in range(2)]; qTb = [Buf("at_q%d" % i) for i in range(2)]
    qr = [S.sbuf("at_qr%d" % i, [128, 512], BF16) for i in range(2)]; qrb = [Buf("at_qr%d" % i) for i in range(2)]
    for i_ in range(2):
        S.op("pool", lambda e, i_=i_: e.memset(qr[i_][64:128, :], 0.0), writes=[qrb[i_]])
    pT = [S.sbuf("at_p%d" % i, [128, 1024], BF16) for i in range(3)]; pTb = [Buf("at_p%d" % i) for i in range(3)]
    PS_O, PS_SUM = 4, 5
    ones = S.sbuf("at_ones", [128, 1], BF16); onesb = Buf("at_ones")
    S.op("pool", lambda e: e.memset(ones[:], 1.0), writes=[onesb])
    onesf = S.sbuf("at_onesf", [1, 128], F32); onesfb = Buf("at_onesf")
    S.op("pool", lambda e: e.memset(onesf[:], 1.0), writes=[onesfb])
    rs = S.sbuf("at_rs", [1, 512], F32); rsb = Buf("at_rs")
    acc = S.sbuf("at_acc", [128, 512], F32); accb = Buf("at_acc")
    s2 = [S.sbuf("at_s2_%d" % i_, [128, 512], BF16) for i_ in range(2)]; s2b = [Buf("at_s2_%d" % i_) for i_ in range(2)]
    onescol = S.sbuf("at_onescol", [128, 1], F32); onescolb = Buf("at_onescol")
    S.op("pool", lambda e: e.memset(onescol[:], 1.0), writes=[onescolb])
    yT = [S.sbuf("at_yT%d" % i, [128, 512], BF16) for i in range(2)]; yTb = [Buf("at_yT%d" % i) for i in range(2)]
    steps = [(j, k2) for j in range(NQT) for k2 in range(2 * j + 2)]

    def emit_st(t):
        j, k2 = steps[t]
        i = j % 2
        if k2 == 0:
            q0 = j * 512
            S.dma("sp", qT[i][:, :], P["qbT"][:, q0:q0 + 512], writes=[qTb[i]])
            S.dma("sp", qr[i][0:64, :], P["qrT"][:, q0:q0 + 512], writes=[qrb[i]])
        w = t % 2
        for hf in range(2):
            kb = 2 * k2 + hf
            ks = slice(kb * 128, (kb + 1) * 128)
            pss = B.psw[w][:, hf * 512:(hf + 1) * 512]
            S.op("pe", lambda e, ks=ks, pss=pss: e.matmul(pss, lhsT=kT[:, ks], rhs=qT[i][:, :], start=True, stop=False), reads=[kTb[kb * 128 // QCH], qTb[i]], writes=[B.pswb[w]])
            S.op("pe", lambda e, ks=ks, pss=pss: e.matmul(pss, lhsT=kr[:, ks], rhs=qr[i][:, :], start=False, stop=True), reads=[krb[kb * 128 // QCH], qrb[i]], writes=[B.pswb[w]])
        pi = t % 3
        S.op("act", lambda e: e.activation(out=pT[pi][:, :], in_=B.psw[w][:, :], func=AF.Exp), reads=[B.pswb[w]], writes=[pTb[pi]])
        for hf in range(2):
            o = 2 * k2 + hf - 4 * j
            if o >= 0:
                S.op("dve", lambda e, hf=hf, o=o: e.tensor_tensor(out=pT[pi][:, hf * 512 + o * 128:(hf + 1) * 512], in0=pT[pi][:, hf * 512 + o * 128:(hf + 1) * 512], in1=msk[:, o, o * 128:512], op=ALU.mult),
                     reads=[pTb[pi], mskb], writes=[pTb[pi]])

    def emit_pv(t):
        j, k2 = steps[t]
        i = j % 2
        pi = t % 3
        full = (2 * k2 + 1 < 4 * j)
        for hf in range(2):
            kb = 2 * k2 + hf
            o = max(0, kb - 4 * j)
            c0 = o * 128
            last = (kb == 4 * j + 3)
            mv = pT[pi][:, hf * 512 + c0:(hf + 1) * 512]
            S.op("pe", lambda e, kb=kb, c0=c0, last=last, mv=mv: e.matmul(B.ps[PS_O][:, c0:512], lhsT=V[:, kb, 0:128], rhs=mv, start=(kb == 0), stop=last, skip_group_check=True),
                 reads=[pTb[pi], Vb[kb // 16]], writes=[B.psb[PS_O]])
            if full:
                continue
            if kb == 0:
                S.op("dve", lambda e, mv=mv: e.tensor_copy(out=acc[:, :], in_=mv), reads=[pTb[pi]], writes=[accb])
            else:
                S.op("dve", lambda e, c0=c0, mv=mv: e.tensor_tensor(out=acc[:, c0:512], in0=acc[:, c0:512], in1=mv, op=ALU.add), reads=[pTb[pi], accb], writes=[accb])
        if full:
            if k2 == 0:
                S.op("dve", lambda e: e.tensor_tensor(out=acc[:, :], in0=pT[pi][:, 0:512], in1=pT[pi][:, 512:1024], op=ALU.add), reads=[pTb[pi]], writes=[accb])
            else:
                si = t % 2
                S.op("dve", lambda e, si=si: e.tensor_tensor(out=s2[si][:, :], in0=pT[pi][:, 0:512], in1=pT[pi][:, 512:1024], op=ALU.add), reads=[pTb[pi]], writes=[s2b[si]])
                S.op("dve", lambda e, si=si: e.tensor_tensor(out=acc[:, :], in0=acc[:, :], in1=s2[si][:, :], op=ALU.add), reads=[s2b[si], accb], writes=[accb])
        if k2 == 2 * j + 1:
            q0 = j * 512
            S.op("pe", lambda e: e.matmul(B.ps[PS_SUM][0:1, :], lhsT=onescol[:, 0:1], rhs=acc[:, :], start=True, stop=True), reads=[onescolb, accb], writes=[B.psb[PS_SUM]])
            S.op("dve", lambda e: e.reciprocal(out=rs[:, :], in_=B.ps[PS_SUM][0:1, :]), reads=[B.psb[PS_SUM]], writes=[rsb])
            S.op("pe", lambda e: e.matmul(B.ps[PS_SUM][:, :], lhsT=onesf[:, :], rhs=rs[:, :], start=True, stop=True), reads=[onesfb, rsb], writes=[B.psb[PS_SUM]])
            S.op("act", lambda e: e.copy(out=acc[:, :], in_=B.ps[PS_SUM][:, :]), reads=[B.psb[PS_SUM]], writes=[accb])
            S.op("dve", lambda e: e.tensor_tensor(out=yT[i][:, :], in0=B.ps[PS_O][:, :], in1=acc[:, :], op=ALU.mult), reads=[B.psb[PS_O], accb], writes=[yTb[i]])
            S.dma("sp", P["ybT"][:, q0:q0 + 512], yT[i][:, :], reads=[yTb[i]])

    emit_st(0)
    for t in range(len(steps)):
        if t + 1 < len(steps):
            emit_st(t + 1)
        emit_pv(t)
        yield
    yield


def pad_gen(g, n):
    for _ in g:
        yield
        for _ in range(n):
            yield


def drive(gens, weights):
    alive = list(gens)
    w = list(weights)
    while alive:
        for g, k in list(zip(alive, w)):
            for _ in range(k):
                try:
                    next(g)
                except StopIteration:
                    idx = alive.index(g)
                    alive.pop(idx); w.pop(idx)
                    break

from concourse.bass_utils import run_bass_kernel_spmd
import ml_dtypes

SEQ = 16384
NCORE = 8
TC = SEQ // NCORE
TT_TILE = 1024

A_OUTS = dict(x1T=([D, None], F32), qaT=([512, None], BF16), kaT=([512, None], BF16), ka=([None, 512], BF16), va=([None, 1024], BF16),
              gif=([None, 8], F32), qbnT=([1024, None], BF16), qbr1T=([256, None], BF16), qbr2T=([256, None], BF16), kbnT=([1024, None], BF16),
              kbr1T=([32, None], BF16), kbr2T=([32, None], BF16), vb=([None, 1024], BF16), cxT=([1024, None], F32))


def _declare_proj(nc, P, Tc, sfx=""):
    def inp(name, shape, dt=F32):
        P[name] = nc.dram_tensor(name, shape, dt, kind="ExternalInput").ap()
    inp("gm" + sfx, [D]); inp("w_in" + sfx, [D, N_IN]); inp("gq" + sfx, [384]); inp("gkv" + sfx, [256])
    inp("w_uq" + sfx, [384, 1536]); inp("w_ukv" + sfx, [256, 2048])
    if "cs" not in P:
        inp("cs", [32, 2, Tc])
    for k, (shp, dt) in A_OUTS.items():
        P[k + sfx] = nc.dram_tensor(k + sfx, [Tc if s is None else s for s in shp], dt, kind="ExternalOutput").ap()


def _proj_P(P, sfx):
    Q = {k: P[k + sfx] for k in A_OUTS}
    Q.update(w_in=P["w_in" + sfx], w_uq=P["w_uq" + sfx], w_ukv=P["w_ukv" + sfx], cs=P["cs"])
    return Q


def build_prog(kind, Tc=TC, T=TT_TILE):
    nc = bass.Bass("TRN2", target_bir_lowering=False)
    P = {}

    def inp(name, shape, dt=F32):
        P[name] = nc.dram_tensor(name, shape, dt, kind="ExternalInput").ap()

    def ffn_in(sfx):
        inp("g" + sfx, [D]); inp("wg" + sfx, [D, DFF]); inp("wu" + sfx, [D, DFF]); inp("wd" + sfx, [DFF, D])

    if kind == "A":
        inp("xT", [D, Tc]); ffn_in("1"); _declare_proj(nc, P, Tc)
    else:
        inp("x1T_in", [D, Tc]); inp("haT", [1024, Tc]); inp("ybT", [1024, Tc], BF16); inp("ycT", [1024, Tc], BF16)
        inp("gm_m", [D]); inp("w_in_m", [D, N_IN]); inp("gon", [1024]); inp("w_branch", [3, 1024, D]); inp("w_out", [D, D])
        ffn_in("2")
        if kind == "C":
            ffn_in("1"); _declare_proj(nc, P, Tc)
        else:
            inp("gfin", [D])
            P["outT"] = nc.dram_tensor("outT", [D, Tc], F32, kind="ExternalOutput").ap()
    with contextlib.ExitStack() as stack:
        S = Sched(nc, stack)
        R = Res(S, T)
        if kind in ("A", "C"):
            A = ResA(S, R)
            g1_t, g1_b = load_vec_pp(S, "g1", P["g1"], KC)
            gm_t, gm_b = load_vec_pp(S, "gm", P["gm"], KC)
            A.gq, A.gqb = load_vec_pp(S, "gq", P["gq"], 3, scale=192 ** -0.5)
            A.gkv, A.gkvb = load_vec_pp(S, "gkv", P["gkv"], 2)
        if kind in ("C", "D"):
            g2_t, g2_b = load_vec_pp(S, "g2", P["g2"], KC)
            gmm_t, gmm_b = load_vec_pp(S, "gm_m", P["gm_m"], KC)
            gon_t, gon_b = load_vec_pp(S, "gon", P["gon"], 8)
        if kind == "D":
            gf_t, gf_b = load_vec_pp(S, "gfin", P["gfin"], KC)
        for t0 in range(0, Tc, T):
            src = P["xT"] if kind == "A" else P["x1T_in"]
            for d in range(KC):
                S.dma("sp", R.x[:, d, :], src[d * 128:(d + 1) * 128, t0:t0 + T], writes=[R.xb[d]])
            if kind in ("C", "D"):
                PM = dict(w_in=P["w_in_m"], w_branch=P["w_branch"], w_out=P["w_out"], x1T=P["x1T_in"], haT=P["haT"], ybT=P["ybT"], ycT=P["ycT"])
                emit_merge(S, R, PM, t0, gmm_t, gmm_b, gon_t, gon_b)
                emit_rmsnorm_fm(S, R, g2_t, g2_b)
                emit_ffn(S, R, P["wg2"], P["wu2"], P["wd2"])
            if kind in ("A", "C"):
                emit_rmsnorm_fm(S, R, g1_t, g1_b)
                emit_ffn(S, R, P["wg1"], P["wu1"], P["wd1"])
                for d in range(KC):
                    S.dma("sp", P["x1T"][d * 128:(d + 1) * 128, t0:t0 + T], R.x[:, d, :], reads=[R.xb[d]])
                emit_rmsnorm_fm(S, R, gm_t, gm_b)
                emit_phase_a_proj(S, R, A, _proj_P(P, ""), t0)
            else:
                xs = [R.x[:, d, :] for d in range(KC)]
                emit_rmsnorm_fm(S, R, gf_t, gf_b, src=xs, srcb=R.xb, dst=xs, dstb=R.xb)
                for d in range(KC):
                    S.dma("sp", P["outT"][d * 128:(d + 1) * 128, t0:t0 + T], R.x[:, d, :], reads=[R.xb[d]])
        info = S.emit()
    return nc, info


def build_prog_b(seq=SEQ):
    nc = bass.Bass("TRN2", target_bir_lowering=False)
    P = {}

    def inp(name, shape, dt=F32):
        P[name] = nc.dram_tensor(name, shape, dt, kind="ExternalInput").ap()

    def out(name, shape, dt=F32):
        P[name] = nc.dram_tensor(name, shape, dt, kind="ExternalOutput").ap()
    inp("cxT", [128, seq]); inp("lru_prm", [128, 8]); inp("lru_wa", [128, 128]); inp("lru_wx", [128, 128]); out("ycT", [128, seq], BF16)
    inp("qaT", [128, seq], BF16); inp("kaT", [128, seq], BF16); inp("ka", [seq, 128], BF16); inp("va", [seq, 128], BF16); inp("gif", [seq, 2])
    inp("gbias", [128, 2]); inp("tri", [128, 128]); out("ha", [seq, 128])
    inp("qbT", [128, seq], BF16); inp("qrT", [64, seq], BF16); inp("kbT", [128, seq], BF16); inp("krT", [64, seq], BF16); inp("vb", [seq, 128], BF16)
    inp("masks", [4, 128, 512], BF16); out("ybT", [128, seq], BF16)
    with contextlib.ExitStack() as stack:
        S = Sched(nc, stack)
        B = ResB(S, seq)
        drive([pad_gen(gen_lru(S, B, P), 1), gen_mlstm(S, B, P), gen_mla(S, B, P)], [1, 1, 3])
        info = S.emit()
    return nc, info


def _rope_tables(pos):
    inv = np.power(np.float32(10000.0), -np.arange(0, 64, 2, dtype=np.float32) / np.float32(64)).astype(np.float32)
    ang = (pos.astype(np.float32)[:, None] * inv[None, :]).astype(np.float32)
    return np.ascontiguousarray(np.stack([np.cos(ang).astype(np.float32).T, np.sin(ang).astype(np.float32).T], axis=1))


def _consts():
    tri = (np.arange(128)[:, None] <= np.arange(128)[None, :]).astype(np.float32)
    k = np.arange(128)[:, None]; q = np.arange(512)[None, :]
    masks = np.stack([(q >= k + 128 * o) for o in range(4)]).astype(np.float32).astype(ml_dtypes.bfloat16)
    return tri, masks


_PROGS = {}


def _prog(kind):
    if kind not in _PROGS:
        _PROGS[kind] = build_prog_b()[0] if kind == "B" else build_prog(kind)[0]
    return _PROGS[kind]


def _run(kind, in_maps):
    res = run_bass_kernel_spmd(_prog(kind), in_maps, core_ids=list(range(NCORE)))
    return res.results


def _ffn_w(inp, l, which, sfx):
    if which == 1:
        g, wg, wu, wd = inp["ffn1_norm"], inp["ffn1_w_gate"], inp["ffn1_w_up"], inp["ffn1_w_down"]
    else:
        g, wg, wu, wd = inp["ffn2_norm"], inp["ffn2_w_gate"], inp["ffn2_w_up"], inp["ffn2_w_down"]
    return {"g" + sfx: g[l], "wg" + sfx: wg[l], "wu" + sfx: wu[l], "wd" + sfx: wd[l]}


def _proj_w(inp, l):
    return {"gm": inp["mix_norm"][l], "w_in": inp["w_in"][l], "gq": inp["mla_q_norm"][l], "gkv": inp["mla_kv_norm"][l],
            "w_uq": inp["mla_w_uq"][l], "w_ukv": inp["mla_w_ukv"][l]}


def _mixer_maps(inp, l, ra):
    tri, masks = _consts()
    cat1 = lambda k: np.concatenate([r[k] for r in ra], axis=1)
    cat0 = lambda k: np.concatenate([r[k] for r in ra], axis=0)
    qaT, kaT, ka, va, gif = cat1("qaT"), cat1("kaT"), cat0("ka"), cat0("va"), cat0("gif")
    qbnT, qbr1T, qbr2T, kbnT, kbr1T, kbr2T, vb, cxT = cat1("qbnT"), cat1("qbr1T"), cat1("qbr2T"), cat1("kbnT"), cat1("kbr1T"), cat1("kbr2T"), cat0("vb"), cat1("cxT")
    krT = np.ascontiguousarray(np.concatenate([kbr1T, kbr2T], axis=0))
    gb = inp["mlstm_gate_bias"][l]
    maps = []
    for c in range(NCORE):
        h, half = c // 2, c % 2
        ch = slice(c * 128, (c + 1) * 128)
        prm = np.concatenate([inp["lru_conv_w"][l][:, ch].T, inp["lru_conv_b"][l][ch, None], inp["lru_b_a"][l][ch, None],
                              inp["lru_b_x"][l][ch, None], inp["lru_lambda"][l][ch, None]], axis=1)
        m = dict(
            cxT=np.ascontiguousarray(cxT[ch]), lru_prm=np.ascontiguousarray(prm.astype(np.float32)),
            lru_wa=np.ascontiguousarray(inp["lru_w_a"][l][c]), lru_wx=np.ascontiguousarray(inp["lru_w_x"][l][c]),
            qaT=np.ascontiguousarray(qaT[h * 128:(h + 1) * 128]), kaT=np.ascontiguousarray(kaT[h * 128:(h + 1) * 128]),
            ka=np.ascontiguousarray(ka[:, h * 128:(h + 1) * 128]), va=np.ascontiguousarray(va[:, h * 256 + half * 128:h * 256 + (half + 1) * 128]),
            gif=np.ascontiguousarray(gif[:, [h, 4 + h]]), gbias=np.ascontiguousarray(np.tile(gb[[h, 4 + h]][None, :], (128, 1)).astype(np.float32)),
            tri=tri, masks=masks,
            qbT=np.ascontiguousarray(qbnT[ch]), qrT=np.ascontiguousarray(np.concatenate([qbr1T[c * 32:(c + 1) * 32], qbr2T[c * 32:(c + 1) * 32]], axis=0)),
            kbT=np.ascontiguousarray(kbnT[ch]), krT=krT, vb=np.ascontiguousarray(vb[:, ch]))
        maps.append(m)
    return maps


def _merge_maps(inp, l, ra, rb):
    haT = np.empty((1024, SEQ), np.float32)
    ybT = np.empty((1024, SEQ), ml_dtypes.bfloat16)
    ycT = np.empty((1024, SEQ), ml_dtypes.bfloat16)
    for c in range(NCORE):
        h, half = c // 2, c % 2
        haT[h * 256 + half * 128:h * 256 + (half + 1) * 128] = np.asarray(rb[c]["ha"]).T
        ybT[c * 128:(c + 1) * 128] = np.asarray(rb[c]["ybT"])
        ycT[c * 128:(c + 1) * 128] = np.asarray(rb[c]["ycT"])
    maps = []
    for c in range(NCORE):
        tk = slice(c * TC, (c + 1) * TC)
        m = dict(x1T_in=np.asarray(ra[c]["x1T"]), haT=np.ascontiguousarray(haT[:, tk]), ybT=np.ascontiguousarray(ybT[:, tk]), ycT=np.ascontiguousarray(ycT[:, tk]),
                 gm_m=inp["mix_norm"][l], w_in_m=inp["w_in"][l], gon=inp["mlstm_out_norm"][l], w_branch=inp["w_branch"][l], w_out=inp["w_out"][l])
        m.update(_ffn_w(inp, l, 2, "2"))
        maps.append(m)
    return maps


def kernel(**inputs):
    inp = {k: np.asarray(v) for k, v in inputs.items()}
    x = inp["x"][0]
    cs = [_rope_tables(np.arange(c * TC, (c + 1) * TC)) for c in range(NCORE)]
    maps = []
    for c in range(NCORE):
        m = dict(xT=np.ascontiguousarray(x[c * TC:(c + 1) * TC].T), cs=cs[c])
        m.update(_ffn_w(inp, 0, 1, "1")); m.update(_proj_w(inp, 0))
        maps.append(m)
    ra = _run("A", maps)
    rb = _run("B", _mixer_maps(inp, 0, ra))
    maps = _merge_maps(inp, 0, ra, rb)
    for c in range(NCORE):
        maps[c].update(_ffn_w(inp, 1, 1, "1")); maps[c].update(_proj_w(inp, 1)); maps[c]["cs"] = cs[c]
    ra = _run("C", maps)
    rb = _run("B", _mixer_maps(inp, 1, ra))
    maps = _merge_maps(inp, 1, ra, rb)
    for c in range(NCORE):
        maps[c]["gfin"] = inp["final_norm"]
    rd = _run("D", maps)
    out = np.concatenate([np.asarray(rd[c]["outT"]).T for c in range(NCORE)], axis=0)
    return np.ascontiguousarray(out[None].astype(np.float32))
```
